# Optimizing a Trainium2 kernel written in Bass

```python
import jax, jax.numpy as jnp
from jax import lax
import numpy as np

D_MODEL = 2048
BATCH = 4
SEQ = 2048
DEPTH = 1
DEC_BATCH = 128
DEC_SEQ = 8
PAST_LEN = 16384
PAGE_SIZE = 128

N_HEADS = 4
D_QK = 256
D_V = 512
D_MLSTM = N_HEADS * D_V
D_CONV = D_MODEL
CONV_W = 31
D_FF = 5632
CHUNK = 128
LN_EPS = 1e-5
ALPHA = (2.0 * DEPTH) ** 0.25
BETA = (8.0 * DEPTH) ** -0.25

_SIZES = (N_HEADS * D_QK, N_HEADS * D_QK, D_MLSTM, D_MLSTM, N_HEADS, N_HEADS,
          D_CONV, D_CONV, D_MODEL, D_MODEL)
D_IN = int(sum(_SIZES))
SPLIT_IDX = [int(s) for s in np.cumsum(_SIZES)[:-1]]

kernel_name = "hybrid_mlstm_conformer_decoder_step"


def layer_norm(x, g, b):
    xf = x.astype(jnp.float32)
    mu = jnp.mean(xf, axis=-1, keepdims=True)
    var = jnp.mean(jnp.square(xf - mu), axis=-1, keepdims=True)
    return ((xf - mu) * lax.rsqrt(var + LN_EPS) * g.astype(jnp.float32) + b.astype(jnp.float32)).astype(x.dtype)


def swiglu_ffn(x, w1, w3, w2):
    return (jax.nn.silu(x @ w1) * (x @ w3)) @ w2


def mlstm_chunk(carry, inp):
    C0, n0, m0 = carry
    q, k, v, ig, lf = inp
    L = q.shape[2]
    b = jnp.cumsum(lf, axis=-1)
    causal = jnp.tril(jnp.ones((L, L), dtype=bool))
    log_d = b[..., :, None] - b[..., None, :] + ig[..., None, :]
    log_d = jnp.where(causal, log_d, -jnp.inf)
    inter = b + m0[..., None]
    m_t = jnp.maximum(inter, jnp.max(log_d, axis=-1))
    d = jnp.exp(log_d - m_t[..., None])
    w_inter = jnp.exp(inter - m_t)
    s = jnp.einsum('bhtd,bhsd->bhts', q, k) * d
    num = jnp.einsum('bhts,bhsv->bhtv', s, v) + w_inter[..., None] * jnp.einsum('bhtd,bhdv->bhtv', q, C0)
    qn = jnp.sum(s, axis=-1) + w_inter * jnp.einsum('bhtd,bhd->bht', q, n0)
    den = jnp.maximum(jnp.abs(qn), jnp.exp(-m_t))
    h = num / den[..., None]
    m_new = m_t[..., -1]
    w_k = jnp.exp(b[..., -1:] - b + ig - m_new[..., None])
    w_c = jnp.exp(inter[..., -1] - m_new)
    C_new = w_c[..., None, None] * C0 + jnp.einsum('bhs,bhsd,bhsv->bhdv', w_k, k, v)
    n_new = w_c[..., None] * n0 + jnp.einsum('bhs,bhsd->bhd', w_k, k)
    return (C_new, n_new, m_new), h


def mlstm_branch(q, k, v, o, ig_pre, fg_pre, b_igate, b_fgate, mh_g, C0, n0, m0, chunk):
    B, T, _ = q.shape
    f32 = jnp.float32

    def heads(t, d):
        return t.reshape(B, T, N_HEADS, d).transpose(0, 2, 1, 3).astype(f32)

    qh = heads(q, D_QK)
    kh = heads(k, D_QK) * (D_QK ** -0.5)
    vh = heads(v, D_V)
    ig = (ig_pre.astype(f32) + b_igate.astype(f32)).transpose(0, 2, 1)
    lf = jax.nn.log_sigmoid(fg_pre.astype(f32) + b_fgate.astype(f32)).transpose(0, 2, 1)
    nc = T // chunk

    def to_chunks(t):
        return jnp.moveaxis(t.reshape((B, N_HEADS, nc, chunk) + t.shape[3:]), 2, 0)

    carry, h = lax.scan(mlstm_chunk, (C0.astype(f32), n0.astype(f32), m0.astype(f32)),
                        (to_chunks(qh), to_chunks(kh), to_chunks(vh), to_chunks(ig), to_chunks(lf)))
    h = jnp.moveaxis(h, 0, 2).reshape(B, N_HEADS, T, D_V)
    mu = jnp.mean(h, axis=-1, keepdims=True)
    var = jnp.mean(jnp.square(h - mu), axis=-1, keepdims=True)
    h = ((h - mu) * lax.rsqrt(var + LN_EPS)).transpose(0, 2, 1, 3).reshape(B, T, D_MLSTM) * mh_g.astype(f32)
    return jax.nn.sigmoid(o) * h.astype(o.dtype), carry


def conv_branch(a, g, buf, conv_w, conv_b, ln_g, ln_b):
    u = a * jax.nn.sigmoid(g)
    upad = jnp.concatenate([buf.astype(u.dtype), u], axis=1)
    y = lax.conv_general_dilated(upad, conv_w[:, None, :].astype(u.dtype), (1,), 'VALID',
                                 dimension_numbers=('NWC', 'WIO', 'NWC'),
                                 feature_group_count=D_CONV) + conv_b
    y = jax.nn.silu(layer_norm(y, ln_g, ln_b))
    return y, upad[:, -(CONV_W - 1):]


def decoder_layer(x, C0, n0, m0, conv_buf, p, chunk):
    (ffn1_w1, ffn1_w3, ffn1_w2, ln1_g, ln1_b, w_in, b_igate, b_fgate, mh_norm_g,
     conv_w, conv_b, conv_ln_g, conv_ln_b, w_out, ln2_g, ln2_b,
     ffn2_w1, ffn2_w3, ffn2_w2, ln3_g, ln3_b) = p
    x1 = layer_norm(ALPHA * x + 0.5 * swiglu_ffn(x, ffn1_w1, ffn1_w3, ffn1_w2), ln1_g, ln1_b)
    proj = x1 @ w_in
    q, k, v, o, ig, fg, glu_a, glu_b, gate_a, gate_b = jnp.split(proj, SPLIT_IDX, axis=-1)
    h_a, (C_new, n_new, m_new) = mlstm_branch(q, k, v, o, ig, fg, b_igate, b_fgate, mh_norm_g,
                                              C0, n0, m0, chunk)
    h_b, conv_new = conv_branch(glu_a, glu_b, conv_buf, conv_w, conv_b, conv_ln_g, conv_ln_b)
    mix = (jax.nn.sigmoid(gate_a) * h_a + jax.nn.sigmoid(gate_b) * h_b) @ w_out
    x2 = layer_norm(ALPHA * x1 + mix, ln2_g, ln2_b)
    y = layer_norm(ALPHA * x2 + 0.5 * swiglu_ffn(x2, ffn2_w1, ffn2_w3, ffn2_w2), ln3_g, ln3_b)
    return y, C_new, n_new, m_new, conv_new


def setup_inputs(seed: int = 0) -> dict:
    key = jax.random.key(seed)
    ks = iter(jax.random.split(key, 40))
    f32 = jnp.float32

    def nrm(shape, scale):
        return jax.random.normal(next(ks), shape, f32) * scale

    def gain(shape):
        return 1.0 + nrm(shape, 0.02)

    L = DEPTH
    return {
        "x_prompt": nrm((BATCH, SEQ, D_MODEL), 1.0),
        "x_sample": nrm((DEC_BATCH, DEC_SEQ, D_MODEL), 1.0),
        "state_C": nrm((L, DEC_BATCH, N_HEADS, D_QK, D_V), 0.1),
        "state_n": nrm((L, DEC_BATCH, N_HEADS, D_QK), 0.1),
        "state_m": nrm((L, DEC_BATCH, N_HEADS), 1.0),
        "cache_conv": nrm((L, DEC_BATCH, CONV_W - 1, D_CONV), 0.5),
        "ffn1_w1": nrm((L, D_MODEL, D_FF), D_MODEL ** -0.5),
        "ffn1_w3": nrm((L, D_MODEL, D_FF), D_MODEL ** -0.5),
        "ffn1_w2": nrm((L, D_FF, D_MODEL), BETA * D_FF ** -0.5),
        "ln1_g": gain((L, D_MODEL)),
        "ln1_b": nrm((L, D_MODEL), 0.02),
        "w_in": nrm((L, D_MODEL, D_IN), D_MODEL ** -0.5),
        "b_igate": nrm((L, N_HEADS), 0.1),
        "b_fgate": 3.0 + nrm((L, N_HEADS), 0.5),
        "mh_norm_g": gain((L, D_MLSTM)),
        "conv_w": nrm((L, CONV_W, D_CONV), CONV_W ** -0.5),
        "conv_b": nrm((L, D_CONV), 0.02),
        "conv_ln_g": gain((L, D_CONV)),
        "conv_ln_b": nrm((L, D_CONV), 0.02),
        "w_out": nrm((L, D_MODEL, D_MODEL), BETA * D_MODEL ** -0.5),
        "ln2_g": gain((L, D_MODEL)),
        "ln2_b": nrm((L, D_MODEL), 0.02),
        "ffn2_w1": nrm((L, D_MODEL, D_FF), D_MODEL ** -0.5),
        "ffn2_w3": nrm((L, D_MODEL, D_FF), D_MODEL ** -0.5),
        "ffn2_w2": nrm((L, D_FF, D_MODEL), BETA * D_FF ** -0.5),
        "ln3_g": gain((L, D_MODEL)),
        "ln3_b": nrm((L, D_MODEL), 0.02),
    }


def reference(x_prompt, x_sample, state_C, state_n, state_m, cache_conv,
              ffn1_w1, ffn1_w3, ffn1_w2, ln1_g, ln1_b, w_in, b_igate, b_fgate, mh_norm_g,
              conv_w, conv_b, conv_ln_g, conv_ln_b, w_out, ln2_g, ln2_b,
              ffn2_w1, ffn2_w3, ffn2_w2, ln3_g, ln3_b):
    yp, ys = x_prompt, x_sample
    Bp = x_prompt.shape[0]
    Cp_l, np_l, mp_l, cp_l = [], [], [], []
    Cs_l, ns_l, ms_l, cs_l = [], [], [], []
    for l in range(DEPTH):
        p = (ffn1_w1[l], ffn1_w3[l], ffn1_w2[l], ln1_g[l], ln1_b[l], w_in[l], b_igate[l], b_fgate[l],
             mh_norm_g[l], conv_w[l], conv_b[l], conv_ln_g[l], conv_ln_b[l], w_out[l], ln2_g[l], ln2_b[l],
             ffn2_w1[l], ffn2_w3[l], ffn2_w2[l], ln3_g[l], ln3_b[l])
        C0 = jnp.zeros((Bp, N_HEADS, D_QK, D_V), jnp.float32)
        n0 = jnp.zeros((Bp, N_HEADS, D_QK), jnp.float32)
        m0 = jnp.zeros((Bp, N_HEADS), jnp.float32)
        buf0 = jnp.zeros((Bp, CONV_W - 1, D_CONV), x_prompt.dtype)
        yp, Cp, np_, mp, cp = decoder_layer(yp, C0, n0, m0, buf0, p, CHUNK)
        ys, Cs, ns, ms, cs = decoder_layer(ys, state_C[l], state_n[l], state_m[l], cache_conv[l], p,
                                           x_sample.shape[1])
        Cp_l.append(Cp); np_l.append(np_); mp_l.append(mp); cp_l.append(cp)
        Cs_l.append(Cs); ns_l.append(ns); ms_l.append(ms); cs_l.append(cs)
    return (yp, ys,
            jnp.stack(Cp_l), jnp.stack(np_l), jnp.stack(mp_l), jnp.stack(cp_l),
            jnp.stack(Cs_l), jnp.stack(ns_l), jnp.stack(ms_l), jnp.stack(cs_l))
```

```python
import numpy as np
from contextlib import ExitStack
import concourse.bass as bass
import concourse.mybir as mybir
from concourse.bass_utils import run_bass_kernel_spmd

F32 = mybir.dt.float32
BF16 = mybir.dt.bfloat16
ALU = mybir.AluOpType
AF = mybir.ActivationFunctionType
AX = mybir.AxisListType

D = 2048
DFF = 5632
DIN = 14344
NH = 4
DQK = 256
DV = 512
CW = 31
HIST = CW - 1
EPS = 1e-5
ALPHA = 2.0 ** 0.25
NEG = -30000.0
O_Q, O_K, O_V, O_O, O_G, O_GA, O_GB, O_TA, O_TB = 0, 1024, 2048, 4096, 6144, 6152, 8200, 10248, 12296
C_ID, C_ONE, C_TRI, C_NM4, C_UB, C_TB, C_NM4S, C_NF4S, C_SM, C_FM, NCON = 0, 128, 256, 384, 512, 640, 768, 896, 1024, 1040, 1056
P_L1G, P_L1B, P_L2G, P_L2B, P_L3G, P_L3B, P_CLG, P_CLB, P_CB, P_CWT, NPF = 0, 16, 32, 48, 64, 80, 96, 112, 128, 144, 640


import os


class _Stop(Exception):
    pass


def ckpt(n):
    lim = float(os.environ.get("KSTOP", "0"))
    if lim and n >= lim:
        raise _Stop()


class Sched:
    def __init__(self, nc, stack):
        self.nc = nc
        self.eng = {"pe": nc.tensor, "act": nc.scalar, "dve": nc.vector, "pool": nc.gpsimd, "sp": nc.sync}
        self.sems = {}
        self.stack = stack
        self.cnt = {}
        self.seen = {e: {} for e in self.eng}
        self.ops = {e: [] for e in self.eng}
        self.buf = {}

    def sem(self, key):
        if key not in self.sems:
            self.sems[key] = self.stack.enter_context(self.nc.semaphore("s_" + key))
            self.cnt[key] = 0
        return self.sems[key]

    def _deps(self, reads, writes):
        toks = []
        for k in reads:
            st = self.buf.get(k)
            if st and st[0] is not None:
                toks.append(st[0])
        for k in writes:
            st = self.buf.get(k)
            if st:
                if st[0] is not None:
                    toks.append(st[0])
                toks.extend(st[1])
        return toks

    def _record(self, tok, reads, writes):
        for k in reads:
            st = self.buf.setdefault(k, [None, []])
            st[1].append(tok)
            if len(st[1]) > 48:
                best = {}
                for s, v in st[1]:
                    if best.get(s, -1) < v:
                        best[s] = v
                st[1] = list(best.items())
        for k in writes:
            self.buf[k] = [tok, []]

    def _waits(self, e, toks):
        need = {}
        for s, v in toks:
            if s not in self.eng:
                v = self.cnt[s]
            if self.seen[e].get(s, 0) >= v:
                continue
            if s == e and e == "pe":
                continue
            if need.get(s, 0) < v:
                need[s] = v
        out = []
        for s, v in need.items():
            self.seen[e][s] = v
            out.append((s, v))
        return out

    def op(self, e, fns, reads=(), writes=()):
        if not isinstance(fns, (list, tuple)):
            fns = [fns]
        waits = self._waits(e, self._deps(reads, writes))
        self.sem(e)
        self.cnt[e] += 1
        tok = (e, self.cnt[e])
        self.ops[e].append((waits, list(fns), (e, 1)))
        self._record(tok, reads, writes)
        return tok

    def dma(self, q, dsem, fn, reads=(), writes=()):
        waits = self._waits(q, self._deps(reads, writes))
        self.sem(dsem)
        self.cnt[dsem] += 16
        tok = (dsem, self.cnt[dsem])
        self.ops[q].append((waits, [fn], (dsem, 16)))
        self._record(tok, reads, writes)
        return tok

    def barrier(self):
        toks = [(s, v) for s, v in self.cnt.items() if v > 0]
        for e in self.eng:
            w = self._waits(e, toks)
            if w:
                self.ops[e].append((w, [], None))
        self.buf = {}

    def soft_barrier(self):
        toks = [(s, self.cnt[s]) for s in ("pe", "act", "dve", "pool") if self.cnt.get(s, 0) > 0]
        for e in ("pe", "act", "dve"):
            w = self._waits(e, toks)
            if w:
                self.ops[e].append((w, [], None))

    def emit(self, block):
        sems = self.sems
        for e, name in (("pe", "tensor"), ("act", "scalar"), ("dve", "vector"), ("pool", "gpsimd"), ("sp", "sync")):
            def body(engine, ops=self.ops[e]):
                for waits, fns, inc in ops:
                    for s, v in waits:
                        engine.wait_ge(sems[s], v)
                    for i, fn in enumerate(fns):
                        ins = fn(engine)
                        if i == len(fns) - 1 and inc is not None:
                            ins.then_inc(sems[inc[0]], inc[1])
            getattr(block, name)(body)


def build_program():
    nc = bass.Bass("TRN2", target_bir_lowering=False)

    def din(name, shape):
        return nc.dram_tensor(name, list(shape), F32, kind="ExternalInput").ap()

    def dout(name, shape):
        return nc.dram_tensor(name, list(shape), F32, kind="ExternalOutput").ap()

    xp = din("xp", [1024, D]); xpre = din("xpre", [1024, D]); msk_d = din("msk", [128, 1]); xs = din("xs", [128, D])
    sC = din("sC", [16, NH, DQK, DV]); sn = din("sn", [128, 128]); smt = din("smt", [128, NH]); cc = din("cc", [16, HIST, D])
    W = {n: din(n, s) for n, s in [("f1w1", [D, DFF]), ("f1w3", [D, DFF]), ("f1w2", [DFF, D]), ("win", [D, DIN]), ("wout", [D, D]),
                                   ("f2w1", [D, DFF]), ("f2w3", [D, DFF]), ("f2w2", [DFF, D])]}
    pf_d = din("pf", [128, NPF]); con_d = din("consts", [128, NCON]); mhg_d = din("mhg", [128, D]); gb_d = din("gb", [128, 8])
    yp = dout("yp", [1024, D]); ys = dout("ys", [128, D])
    Cp = dout("Cp", [NH, DQK, DV]); npo = dout("np", [8, 128]); mpo = dout("mp", [1, NH]); cpo = dout("cp", [HIST, D])
    Cs = dout("Cs", [16, NH, DQK, DV]); nso = dout("ns", [128, 128]); mso = dout("ms", [16, NH]); cso = dout("cs", [16, HIST, D])

    yscr = nc.dram_tensor("yscr", [16, 128, 512], F32).ap()
    wcache = nc.dram_tensor("wcache", [128, 803072], BF16).ap()
    with ExitStack() as st:
        S = Sched(nc, st)
        uid = [0]

        def sb(shape, dt=F32):
            uid[0] += 1
            return st.enter_context(nc.sbuf_tensor("t%d" % uid[0], list(shape), dt))

        def pst(shape, dt=F32):
            uid[0] += 1
            return st.enter_context(nc.psum_tensor("p%d" % uid[0], list(shape), dt))

        xT = sb([128, 16, 512], BF16)
        resT = sb([128, 16, 512])
        ybuf = sb([128, 512])
        big = sb([128, 44 * 512], BF16)
        hid = big[:].rearrange("p (f t) -> p f t", f=44)
        termA = big[:, 0:8192].rearrange("p (i c) -> p i c", i=4)
        mixT = big[:, 8192:16384].rearrange("p (k t) -> p k t", k=16)
        qkT = big[:, 16384:18432].rearrange("p (o t) -> p o t", o=4)
        ktok = big[:, 18432:19456].rearrange("p (i c) -> p i c", i=4)
        vtok = big[:, 19456:21504].rearrange("p (i c) -> p i c", i=4)
        wb = [sb([128, 8192], BF16) for _ in range(2)]
        Cst = sb([128, NH, 2, DV]); nst = sb([128, 8]); mst = sb([128, NH])
        Cb = sb([128, 2, DV], BF16); nb = sb([128, 16], BF16)
        wb.append(Cst[:].rearrange("p h a v -> p (h a v)").bitcast(BF16))
        con = sb([128, NCON]); pf = sb([128, NPF]); gb = sb([128, 8]); msk = sb([128, 1]); pfa = sb([128, 96])
        idb = sb([128, 128], BF16); oneb = sb([128, 16], BF16); epsc = sb([128, 2])
        xtok = [sb([128, D])]
        tmp = [sb([128, 512]) for _ in range(4)]
        ssum = sb([128, 512]); ssq = sb([128, 512]); mean = sb([128, 512]); rstd = sb([128, 512])
        gts = sb([128, 4, 8]); cdiag = sb([128, CW * 128], BF16)
        dg = cdiag[:, 0:1024].bitcast(F32).rearrange("p (h s) -> p h s", h=4)
        cdv = cdiag[:].rearrange("p (k q) -> p k q", k=CW)
        sv = sb([128, 4, 64])
        negmx = sb([128, 4, 4]); wint = sb([128, 4, 4]); wk_ = sb([128, 4, 4]); wc_ = sb([128, 4, 4]); emt = sb([128, 4, 4])
        dt_ = sb([128, 128]); smf = sb([128, 128]); smb = sb([128, 128], BF16); smT = sb([128, 128], BF16)
        kw = sb([128, 256], BF16); sc1 = sb([128, 16]); bnst = sb([128, 6]); wks = sb([128, 2])
        upb = [sb([128, HIST + 512], BF16) for _ in range(2)]
        hist = sb([128, 16, HIST], BF16); utail = sb([128, 32])
        stage = xtok[0]
        c0f = [sb([128, 2, DV]) for _ in range(2)]
        n0T = sb([128, 128]); nnew = sb([128, 128]); wm = sb([128, 16]); wcb = sb([128, 64]); wsel = sb([128, 64])
        ctk = sb([128, 4, 128]); upS = sb([128, 16, HIST + 8], BF16); cnew = [sb([128, DV]), ybuf]
        accb = [pst([128, 512]) for _ in range(5)]
        auxb = [pst([128, 512]) for _ in range(2)]
        pTb = pst([128, 1024], BF16)
        ring = {"acc": 0, "aux": 0, "wb": 0, "tmp": 0, "c0f": 0, "cnew": 0, "ystg": 0, "xtok": 0}

        def nxt(kind, n):
            i = ring[kind]
            ring[kind] = (i + 1) % n
            return i

        def acc():
            i = nxt("acc", 5)
            return accb[i], ("acc", i)

        def aux():
            i = nxt("aux", 2)
            return auxb[i], ("aux", i)

        def gtmp():
            i = nxt("tmp", 4)
            return tmp[i], ("tmp", i)

        block = st.enter_context(nc.Block())
        ident = con[:, C_ID:C_ID + 128]
        ones = con[:, C_ONE:C_ONE + 128]

        S.dma("sp", "c_con", lambda e: e.dma_start(out=con[:], in_=con_d), writes=["con"])
        S.dma("sp", "c_pf", lambda e: e.dma_start(out=pf[:], in_=pf_d), writes=["pf"])
        S.dma("sp", "c_gb", lambda e: e.dma_start(out=gb[:], in_=gb_d), writes=["gb"])
        S.dma("sp", "c_msk", lambda e: e.dma_start(out=msk[:], in_=msk_d), writes=["msk"])
        S.op("dve", lambda e: e.tensor_copy(out=idb[:], in_=ident), reads=["con"], writes=["idb"])
        S.op("dve", lambda e: e.memset(oneb[:], 1.0), writes=["oneb"])
        S.op("dve", lambda e: e.memset(epsc[:], EPS), writes=["epsc"])
        S.op("dve", lambda e: e.tensor_scalar(out=pfa[:], in0=pf[:, 0:96], scalar1=ALPHA, scalar2=None, op0=ALU.mult), reads=["pf"], writes=["pfa"])

        wstate = {"idx": 0, "pass": 0, "off": 0, "coff": {}, "bgq": [], "bg_on": False}

        def wload(src, r0, nr, c0, ncol, slot, off, key, part=None):
            kch = nr // 128
            n = kch * ncol
            ck = (src.tensor.name, r0, nr, c0, ncol)
            wkeys = [(key, "a"), (key, "b")] if part is None else [(key, part)]
            if slot == 2:
                wkeys = wkeys + ["Cst"]
            fresh = ck not in wstate["coff"]
            if fresh:
                wstate["coff"][ck] = wstate["off"]
                wstate["off"] += n
            co = wstate["coff"][ck]
            idx = ck
            if (not fresh) and wstate.get("bg_on") and wstate["bgq"]:
                wstate["bgtick"] = wstate.get("bgtick", 0) + 1
                if wstate["bgtick"] % 2 == 0:
                    bg_cast(*wstate["bgq"].pop(0))
            if fresh:
                dst = wb[slot][:, off:off + n].rearrange("p (k c) -> p k c", k=kch)
                S.dma("pool", "d_w%d%s" % (slot, part or "a"),
                      lambda e: e.dma_start(out=dst, in_=src[r0:r0 + nr, c0:c0 + ncol].rearrange("(k p) c -> p k c", p=128)),
                      writes=wkeys)
                S.dma("sp", "d_wbk%d%s" % (slot, part or "a"), lambda e: e.dma_start(out=wcache[:, co:co + n], in_=wb[slot][:, off:off + n]),
                      reads=wkeys, writes=[("wc", idx)])
            else:
                S.dma("pool", "d_w%d%s" % (slot, part or "a"), lambda e: e.dma_start(out=wb[slot][:, off:off + n], in_=wcache[:, co:co + n]),
                      reads=[("wc", idx)], writes=wkeys)

        def bg_cast(src, r0, nr, c0, ncol):
            kch = nr // 128
            n = kch * ncol
            ck = (src.tensor.name, r0, nr, c0, ncol)
            if ck in wstate["coff"]:
                return
            wstate["coff"][ck] = wstate["off"]
            co = wstate["off"]
            wstate["off"] += n
            S.dma("pool", "d_bg", lambda e: e.dma_start(out=wcache[:, co:co + n].rearrange("p (k c) -> p k c", k=kch),
                                                       in_=src[r0:r0 + nr, c0:c0 + ncol].rearrange("(k p) c -> p k c", p=128)),
                  writes=[("wc", ck)])

        def wslot():
            i = nxt("wb", wstate.get("nslots", 2))
            return i, ("wb", i)

        def mm_group(psap, pairs, reads, pkey):
            fns = []
            n = len(pairs)
            for i, (l, r) in enumerate(pairs):
                fns.append(lambda e, l=l, r=r, i=i: e.matmul(psap, lhsT=l, rhs=r, start=(i == 0), stop=(i == n - 1)))
            rr = []
            for r in reads:
                if isinstance(r, tuple) and r[0] == "wb":
                    rr += [(r, "a"), (r, "b")]
                else:
                    rr.append(r)
            S.op("pe", fns, reads=rr, writes=[pkey])

        def run_pass(NT, sample, x_src, y_dst, first, last, prefix=False, lastprefix=False):
            ntile = NT // 128
            ckpt(0.5)

            for i in range(ntile):
                xi = 0
                S.dma("sp", "d_xtok", lambda e, xi=xi, i=i: e.dma_start(out=xtok[xi][:], in_=x_src[i * 128:(i + 1) * 128, :]), writes=[("xtok", xi)])
                for g in range(4):
                    ps, pk = acc()
                    S.op("pe", [lambda e, q=q, g=g, xi=xi, ps=ps: e.transpose(ps[:, q * 128:(q + 1) * 128], xtok[xi][:, (4 * g + q) * 128:(4 * g + q + 1) * 128], ident) for q in range(4)],
                         reads=[("xtok", xi), "con"], writes=[pk])
                    pv = ps[:].rearrange("p (q t) -> p q t", q=4)
                    tx, txk = gtmp()
                    tv = tx[:].rearrange("p (q t) -> p q t", q=4)
                    S.op("dve", lambda e, tx=tx, ps=ps: e.tensor_copy(out=tx[:], in_=ps[:]), reads=[pk], writes=[txk])
                    S.op("act", lambda e, g=g, i=i, tv=tv: e.copy(out=xT[:, 4 * g:4 * g + 4, i * 128:(i + 1) * 128], in_=tv), reads=[txk], writes=["xT"])
                    S.op("dve", lambda e, g=g, i=i, tv=tv: e.tensor_scalar(out=resT[:, 4 * g:4 * g + 4, i * 128:(i + 1) * 128], in0=tv, scalar1=ALPHA, scalar2=None, op0=ALU.mult), reads=[txk], writes=[("resT", 4 * g + q_) for q_ in range(4)])

            def stats_acc(src_ap, j, srckey):
                t, tk = gtmp()
                S.op("act", lambda e: e.activation(out=t[:, 0:NT], in_=src_ap, func=AF.Square), reads=[srckey], writes=[tk])
                if j == 0:
                    S.op("dve", lambda e: e.tensor_copy(out=ssum[:, 0:NT], in_=src_ap), reads=[srckey], writes=["ssum"])
                    S.op("dve", lambda e: e.tensor_copy(out=ssq[:, 0:NT], in_=t[:, 0:NT]), reads=[tk], writes=["ssq"])
                else:
                    S.op("dve", lambda e: e.tensor_tensor(out=ssum[:, 0:NT], in0=ssum[:, 0:NT], in1=src_ap, op=ALU.add), reads=[srckey, "ssum"], writes=["ssum"])
                    S.op("dve", lambda e: e.tensor_tensor(out=ssq[:, 0:NT], in0=ssq[:, 0:NT], in1=t[:, 0:NT], op=ALU.add), reads=[tk, "ssq"], writes=["ssq"])

            def stats_fin():
                ps, pk = aux()
                S.op("pe", lambda e: e.matmul(ps[:, 0:NT], lhsT=ones, rhs=ssum[:, 0:NT], start=True, stop=True), reads=["con", "ssum"], writes=[pk])
                S.op("act", lambda e: e.mul(out=mean[:, 0:NT], in_=ps[:, 0:NT], mul=1.0 / D), reads=[pk], writes=["mean"])
                ps2, pk2 = aux()
                S.op("pe", lambda e: e.matmul(ps2[:, 0:NT], lhsT=ones, rhs=ssq[:, 0:NT], start=True, stop=True), reads=["con", "ssq"], writes=[pk2])
                S.op("act", lambda e: e.mul(out=rstd[:, 0:NT], in_=ps2[:, 0:NT], mul=1.0 / D), reads=[pk2], writes=["rstd"])
                t, tk = gtmp()
                S.op("dve", lambda e: e.tensor_tensor(out=t[:, 0:NT], in0=mean[:, 0:NT], in1=mean[:, 0:NT], op=ALU.mult), reads=["mean"], writes=[tk])
                S.op("dve", lambda e: e.tensor_tensor(out=rstd[:, 0:NT], in0=rstd[:, 0:NT], in1=t[:, 0:NT], op=ALU.subtract), reads=[tk, "rstd"], writes=["rstd"])
                S.op("act", lambda e: e.activation(out=rstd[:, 0:NT], in_=rstd[:, 0:NT], func=AF.Ln, bias=epsc[:, 0:1]), reads=["rstd", "epsc"], writes=["rstd"])
                S.op("act", lambda e: e.activation(out=rstd[:, 0:NT], in_=rstd[:, 0:NT], func=AF.Exp, scale=-0.5), reads=["rstd"], writes=["rstd"])

            def ln_apply(pg, pb_, final, need_res=True):
                stats_fin()
                for j in range(16):
                    t, tk = gtmp()
                    S.op("dve", lambda e, j=j, t=t: e.tensor_tensor(out=t[:, 0:NT], in0=resT[:, j, 0:NT], in1=mean[:, 0:NT], op=ALU.subtract), reads=[("resT", j), "mean"], writes=[tk])
                    S.op("dve", lambda e, t=t: e.tensor_tensor(out=t[:, 0:NT], in0=t[:, 0:NT], in1=rstd[:, 0:NT], op=ALU.mult), reads=[tk, "rstd"], writes=[tk])
                    if not final:
                        S.op("act", lambda e, j=j, t=t: e.activation(out=xT[:, j, 0:NT], in_=t[:, 0:NT], func=AF.Identity, bias=pf[:, pb_ + j:pb_ + j + 1], scale=pf[:, pg + j:pg + j + 1]), reads=[tk, "pf"], writes=["xT"])
                        if need_res:
                            S.op("act", lambda e, j=j, t=t: e.activation(out=resT[:, j, 0:NT], in_=t[:, 0:NT], func=AF.Identity, bias=pfa[:, pb_ + j:pb_ + j + 1], scale=pfa[:, pg + j:pg + j + 1]), reads=[tk, "pfa"], writes=[("resT", j)])
                    else:
                        t2, tk2 = gtmp()
                        S.op("act", lambda e, j=j, t=t, t2=t2: e.activation(out=t2[:, 0:NT], in_=t[:, 0:NT], func=AF.Identity, bias=pf[:, pb_ + j:pb_ + j + 1], scale=pf[:, pg + j:pg + j + 1]), reads=[tk, "pf"], writes=[tk2])
                        ps, pk = acc()
                        S.op("pe", [lambda e, i=i, ps=ps, t2=t2: e.transpose(ps[:, i * 128:(i + 1) * 128], t2[:, i * 128:(i + 1) * 128], ident) for i in range(ntile)], reads=[tk2, "con"], writes=[pk])
                        ty, tyk = gtmp()
                        tyv = ty[:, 0:NT].rearrange("p (i c) -> p i c", i=ntile)
                        S.op("act", lambda e, tyv=tyv, ps=ps: e.copy(out=tyv, in_=ps[:, 0:NT].rearrange("p (i c) -> p i c", i=ntile)), reads=[pk], writes=[tyk])
                        S.dma("sp", "d_tmp%d" % tyk[1], lambda e, tyv=tyv, j=j: e.dma_start(out=y_dst[:, j * 128:(j + 1) * 128].rearrange("(i t) c -> t i c", t=128), in_=tyv), reads=[tyk], writes=["yout"])

            def ffn(w1, w3, w2):
                for s in range(22):
                    sl, wk = wslot()
                    wload(w1, 0, D, s * 256, 256, sl, 0, wk, "a")
                    wload(w3, 0, D, s * 256, 256, sl, 4096, wk, "b")
                    wv = wb[sl][:].rearrange("p (k c) -> p k c", k=32)
                    for fc in range(2):
                        f = 2 * s + fc
                        pa, pka = acc()
                        mm_group(pa[:, 0:NT], [(wv[:, kc, fc * 128:(fc + 1) * 128], xT[:, kc, 0:NT]) for kc in range(16)], [wk, "xT"], pka)
                        pb2, pkb = acc()
                        mm_group(pb2[:, 0:NT], [(wv[:, 16 + kc, fc * 128:(fc + 1) * 128], xT[:, kc, 0:NT]) for kc in range(16)], [wk, "xT"], pkb)
                        t, tk = gtmp()
                        S.op("act", lambda e, t=t, pa=pa: e.activation(out=t[:, 0:NT], in_=pa[:, 0:NT], func=AF.Silu), reads=[pka], writes=[tk])
                        S.op("dve", lambda e, t=t, pb2=pb2, f=f: e.tensor_tensor(out=hid[:, f, 0:NT], in0=t[:, 0:NT], in1=pb2[:, 0:NT], op=ALU.mult), reads=[tk, pkb], writes=["hid"])
                for j in range(16):
                    sl, wk = wslot()
                    wload(w2, 0, DFF, j * 128, 128, sl, 0, wk)
                    wv = wb[sl][:, 0:44 * 128].rearrange("p (k c) -> p k c", k=44)
                    pz, pkz = acc()
                    mm_group(pz[:, 0:NT], [(wv[:, f, :], hid[:, f, 0:NT]) for f in range(44)], [wk, "hid"], pkz)
                    S.op("dve", lambda e, j=j, pz=pz: e.scalar_tensor_tensor(out=resT[:, j, 0:NT], in0=pz[:, 0:NT], scalar=0.5, in1=resT[:, j, 0:NT], op0=ALU.mult, op1=ALU.add), reads=[pkz, ("resT", j)], writes=[("resT", j)])
                    stats_acc(resT[:, j, 0:NT], j, ("resT", j))

            base = 100 if sample else 0
            ckpt(base + 1)
            ffn(W["f1w1"], W["f1w3"], W["f1w2"])
            ckpt(base + 2)
            ln_apply(P_L1G, P_L1B, False, need_res=not prefix)
            ckpt(base + 3)
            S.soft_barrier()
            mixer(NT, sample, first, last, prefix, lastprefix)
            S.soft_barrier()
            if prefix:
                return
            ckpt(base + 20)
            ln_apply(P_L2G, P_L2B, False)
            ckpt(base + 21)
            ffn(W["f2w1"], W["f2w3"], W["f2w2"])
            ckpt(base + 22)
            ln_apply(P_L3G, P_L3B, True)
            ckpt(base + 23)

        def mixer(NT, sample, first, last, prefix=False, lastprefix=False):
            ntile = NT // 128
            win = W["win"]
            cU = con[:, C_UB:C_UB + 128] if sample else con[:, C_TRI:C_TRI + 128]
            cT = con[:, C_TB:C_TB + 128] if sample else ones
            cNM = con[:, C_NM4S:C_NM4S + 128] if sample else con[:, C_NM4:C_NM4 + 128]
            if first:
                S.op("dve", lambda e: e.memset(Cst[:], 0.0), writes=["Cst"])
                S.op("dve", lambda e: e.memset(nst[:], 0.0), writes=["nst"])
                S.op("dve", lambda e: e.memset(mst[:], 0.0), writes=["mst"])
                S.op("dve", lambda e: e.memset(hist[:], 0.0), writes=["hist"])
            if sample:
                S.dma("sp", "d_mst", lambda e: e.dma_start(out=mst[:], in_=smt), writes=["mst"])
                S.dma("sp", "d_nnew", lambda e: e.dma_start(out=nnew[:], in_=sn), writes=["nnew"])
                ps, pk = aux()
                S.op("pe", lambda e, ps=ps: e.transpose(ps[:, 0:128], nnew[:], ident), reads=["nnew", "con"], writes=[pk])
                S.op("dve", lambda e, ps=ps: e.tensor_copy(out=n0T[:], in_=ps[:, 0:128]), reads=[pk], writes=["n0T"])
                S.dma("sp", "d_dd", lambda e: e.dma_start(out=cso[:, 0:HIST - 8, :], in_=cc[:, 8:HIST, :]), writes=["cso_a"])

            sl, wk = wslot()
            wload(win, 0, D, O_G, 8, sl, 0, wk)
            wv = wb[sl][:, 0:128].rearrange("p (k c) -> p k c", k=16)
            for i in range(ntile):
                ps, pk = aux()
                mm_group(ps[:, 0:8], [(xT[:, kc, i * 128:(i + 1) * 128], wv[:, kc, :]) for kc in range(16)], [wk, "xT"], pk)
                S.op("dve", lambda e, i=i, ps=ps: e.tensor_tensor(out=gts[:, i, :], in0=ps[:, 0:8], in1=gb[:], op=ALU.add), reads=[pk, "gb"], writes=["gts"])
            for i in range(ntile):
                v = sv[:, i, :]
                S.op("act", lambda e, i=i, v=v: e.activation(out=v[:, 4:8], in_=gts[:, i, 4:8], func=AF.Exp, scale=-1.0), reads=["gts"], writes=["sv"])
                S.op("act", lambda e, v=v: e.activation(out=v[:, 0:4], in_=v[:, 4:8], func=AF.Ln, bias=1.0), reads=["sv"], writes=["sv"])
                ps, pk = aux()
                S.op("pe", [lambda e, v=v, ps=ps: e.matmul(ps[:, 0:4], lhsT=cU, rhs=v[:, 0:4], start=True, stop=True),
                            lambda e, v=v, ps=ps: e.matmul(ps[:, 4:8], lhsT=cT, rhs=v[:, 0:4], start=True, stop=True)], reads=["sv", "con"], writes=[pk])
                S.op("act", lambda e, v=v, ps=ps: e.mul(out=v[:, 8:16], in_=ps[:, 0:8], mul=-1.0), reads=[pk], writes=["sv"])
                S.op("dve", lambda e, i=i, v=v: e.tensor_tensor(out=v[:, 16:20], in0=gts[:, i, 0:4], in1=v[:, 8:12], op=ALU.subtract), reads=["sv", "gts"], writes=["sv"])
                for h in range(NH):
                    S.op("dve", lambda e, h=h, v=v: e.tensor_scalar(out=dg[:, h, :], in0=ident, scalar1=v[:, 16 + h:17 + h], scalar2=None, op0=ALU.mult), reads=["sv", "con"], writes=["dg"])
                pA, pkA = aux()
                S.op("pe", [lambda e, h=h, pA=pA: e.matmul(pA[:, h * 128:(h + 1) * 128], lhsT=ones, rhs=dg[:, h, :], start=True, stop=True) for h in range(NH)], reads=["dg", "con"], writes=[pkA])
                ta, tka = gtmp()
                for h in range(NH):
                    S.op("dve", lambda e, h=h, ta=ta, pA=pA: e.tensor_tensor(out=ta[:, h * 128:(h + 1) * 128], in0=pA[:, h * 128:(h + 1) * 128], in1=cNM, op=ALU.add), reads=[pkA, "con", tka], writes=[tka])
                S.op("dve", lambda e, v=v, ta=ta: e.tensor_reduce(out=v[:, 20:24], in_=ta[:].rearrange("p (h s) -> p h s", h=4), axis=AX.X, op=ALU.max), reads=[tka], writes=["sv"])
                if sample:
                    t, tk = gtmp()
                    for h in range(NH):
                        S.op("dve", lambda e, h=h, t=t, pA=pA: e.tensor_tensor(out=t[:, h * 128:(h + 1) * 128], in0=pA[:, h * 128:(h + 1) * 128], in1=con[:, C_NF4S:C_NF4S + 128], op=ALU.add), reads=[pkA, "con", tk], writes=[tk])
                    S.op("dve", lambda e, t=t, v=v: e.tensor_reduce(out=v[:, 24:28], in_=t[:].rearrange("p (h s) -> p h s", h=4), axis=AX.X, op=ALU.max), reads=[tk], writes=["sv"])
                else:
                    S.op("dve", lambda e, v=v, pA=pA: e.tensor_reduce(out=v[:, 24:28], in_=pA[:].rearrange("p (h s) -> p h s", h=4), axis=AX.X, op=ALU.max), reads=[pkA], writes=["sv"])
                S.op("dve", lambda e, v=v: e.tensor_tensor(out=v[:, 28:32], in0=v[:, 20:24], in1=mst[:], op=ALU.max), reads=["sv", "mst"], writes=["sv"])
                S.op("dve", lambda e, v=v: e.tensor_tensor(out=v[:, 32:36], in0=v[:, 24:28], in1=mst[:], op=ALU.max), reads=["sv", "mst"], writes=["sv"])
                S.op("dve", lambda e, v=v: e.tensor_tensor(out=v[:, 32:36], in0=v[:, 32:36], in1=v[:, 12:16], op=ALU.add), reads=["sv"], writes=["sv"])
                S.op("dve", lambda e, i=i, v=v: e.tensor_scalar(out=negmx[:, i, :], in0=v[:, 28:32], scalar1=-1.0, scalar2=None, op0=ALU.mult), reads=["sv"], writes=["gv"])
                S.op("dve", lambda e, v=v: e.tensor_tensor(out=v[:, 36:40], in0=mst[:], in1=v[:, 28:32], op=ALU.subtract), reads=["sv", "mst"], writes=["sv"])
                S.op("act", lambda e, i=i, v=v: e.activation(out=wint[:, i, :], in_=v[:, 36:40], func=AF.Exp), reads=["sv"], writes=["gv"])
                S.op("dve", lambda e, v=v: e.tensor_tensor(out=v[:, 40:44], in0=v[:, 12:16], in1=v[:, 32:36], op=ALU.subtract), reads=["sv"], writes=["sv"])
                S.op("dve", lambda e, v=v: e.tensor_tensor(out=v[:, 44:48], in0=v[:, 40:44], in1=v[:, 16:20], op=ALU.add), reads=["sv"], writes=["sv"])
                S.op("act", lambda e, i=i, v=v: e.activation(out=wk_[:, i, :], in_=v[:, 44:48], func=AF.Exp), reads=["sv"], writes=["gv"])
                S.op("dve", lambda e, v=v: e.tensor_tensor(out=v[:, 48:52], in0=v[:, 40:44], in1=mst[:], op=ALU.add), reads=["sv", "mst"], writes=["sv"])
                S.op("act", lambda e, i=i, v=v: e.activation(out=wc_[:, i, :], in_=v[:, 48:52], func=AF.Exp), reads=["sv"], writes=["gv"])
                S.op("dve", lambda e, v=v: e.tensor_tensor(out=v[:, 52:56], in0=v[:, 8:12], in1=v[:, 28:32], op=ALU.add), reads=["sv"], writes=["sv"])
                S.op("act", lambda e, i=i, v=v: e.activation(out=emt[:, i, :], in_=v[:, 52:56], func=AF.Exp, scale=-1.0), reads=["sv"], writes=["gv"])
                S.op("dve", lambda e, v=v: e.tensor_copy(out=mst[:], in_=v[:, 32:36]), reads=["sv"], writes=["mst"])

            ckpt(4)
            if not prefix:
                for blk in range(4):
                    sl, wk = wslot()
                    wload(win, 0, D, O_O + blk * 512, 512, sl, 0, wk)
                    wv = wb[sl][:].rearrange("p (k c) -> p k c", k=16)
                    mi = 0
                    S.dma("sp", "d_xtok", lambda e, mi=mi, blk=blk: e.dma_start(out=xtok[mi][:, 0:512], in_=mhg_d[:, blk * 512:(blk + 1) * 512]), writes=[("xtok", mi)])
                    for i in range(ntile):
                        ps, pk = acc()
                        mm_group(ps[:], [(xT[:, kc, i * 128:(i + 1) * 128], wv[:, kc, :]) for kc in range(16)], [wk, "xT"], pk)
                        t, tk = gtmp()
                        S.op("act", lambda e, t=t, ps=ps: e.activation(out=t[:], in_=ps[:], func=AF.Sigmoid), reads=[pk], writes=[tk])
                        S.op("dve", lambda e, t=t, i=i, blk=blk, mi=mi: e.tensor_tensor(out=termA[:, i, blk * 512:(blk + 1) * 512], in0=t[:], in1=xtok[mi][:, 0:512], op=ALU.mult), reads=[tk, ("xtok", mi)], writes=["termA"])
                for blk in range(4):
                    sl, wk = wslot()
                    wload(win, 0, D, O_TA + blk * 512, 512, sl, 0, wk)
                    wv = wb[sl][:].rearrange("p (k c) -> p k c", k=16)
                    for i in range(ntile):
                        ps, pk = acc()
                        mm_group(ps[:], [(xT[:, kc, i * 128:(i + 1) * 128], wv[:, kc, :]) for kc in range(16)], [wk, "xT"], pk)
                        t, tk = gtmp()
                        S.op("act", lambda e, t=t, ps=ps: e.activation(out=t[:], in_=ps[:], func=AF.Sigmoid), reads=[pk], writes=[tk])
                        S.op("dve", lambda e, t=t, i=i, blk=blk: e.tensor_tensor(out=termA[:, i, blk * 512:(blk + 1) * 512], in0=t[:], in1=termA[:, i, blk * 512:(blk + 1) * 512], op=ALU.mult), reads=[tk, "termA"], writes=["termA"])

            ckpt(5)
            for h in range(NH):
                sl, wk = wslot()
                if not prefix:
                    wload(win, 0, D, O_Q + h * 256, 256, sl, 0, wk, "a")
                wload(win, 0, D, O_K + h * 256, 256, sl, 4096, wk, "b")
                wv = wb[sl][:].rearrange("p (k c) -> p k c", k=32)
                for oc in (range(4) if not prefix else []):
                    ps, pk = acc()
                    base = 0 if oc < 2 else 16
                    mm_group(ps[:, 0:NT], [(wv[:, base + kc, (oc % 2) * 128:(oc % 2 + 1) * 128], xT[:, kc, 0:NT]) for kc in range(16)], [wk, "xT"], pk)
                    S.op("act", lambda e, oc=oc, ps=ps: e.mul(out=qkT[:, oc, 0:NT], in_=ps[:, 0:NT], mul=(1.0 if oc < 2 else 0.0625)), reads=[pk], writes=["qkT"])
                for i in range(ntile):
                    ps, pk = acc()
                    mm_group(ps[:, 0:256], [(xT[:, kc, i * 128:(i + 1) * 128], wv[:, 16 + kc, :]) for kc in range(16)], [wk, "xT"], pk)
                    S.op("act", lambda e, i=i, ps=ps: e.mul(out=ktok[:, i, :], in_=ps[:, 0:256], mul=0.0625), reads=[pk], writes=["ktok"])
                sl, wk = wslot()
                wload(win, 0, D, O_V + h * 512, 512, sl, 0, wk)
                wv = wb[sl][:].rearrange("p (k c) -> p k c", k=16)
                for i in range(ntile):
                    ps, pk = acc()
                    mm_group(ps[:], [(xT[:, kc, i * 128:(i + 1) * 128], wv[:, kc, :]) for kc in range(16)], [wk, "xT"], pk)
                    S.op("dve", lambda e, i=i, ps=ps: e.tensor_copy(out=vtok[:, i, :], in_=ps[:]), reads=[pk], writes=["vtok"])
                for i in range(ntile):
                    chunk(NT, sample, h, i, prefix)

            ckpt(6)
            if prefix:
                if lastprefix:
                    conv_branch(NT, sample, last, True)
                return
            conv_branch(NT, sample, last)
            ckpt(8)

            for s in range(4):
                sl, wk = wslot()
                wload(W["wout"], 0, D, s * 512, 512, sl, 0, wk)
                wv = wb[sl][:].rearrange("p (k c) -> p k c", k=16)
                for cc_ in range(4):
                    j = 4 * s + cc_
                    pz, pkz = acc()
                    mm_group(pz[:, 0:NT], [(wv[:, kc, cc_ * 128:(cc_ + 1) * 128], mixT[:, kc, 0:NT]) for kc in range(16)], [wk, "mixT"], pkz)
                    S.op("dve", lambda e, j=j, pz=pz: e.tensor_tensor(out=resT[:, j, 0:NT], in0=pz[:, 0:NT], in1=resT[:, j, 0:NT], op=ALU.add), reads=[pkz, ("resT", j)], writes=[("resT", j)])
                    stats_acc_g(NT, resT[:, j, 0:NT], j, ("resT", j))

            ckpt(9)
            if last and not sample:
                for h in range(NH):
                    S.dma("sp", "d_Cst", lambda e, h=h: e.dma_start(out=Cp[h].rearrange("(a p) v -> p a v", p=128), in_=Cst[:, h, :, :]), reads=["Cst"], writes=[("Cp", h)])
                ps, pk = aux()
                S.op("pe", lambda e, ps=ps: e.transpose(ps[0:8, 0:128], nst[:], ident), reads=["nst", "con"], writes=[pk])
                S.op("dve", lambda e, ps=ps: e.tensor_copy(out=stage[0:8, 0:128], in_=ps[0:8, 0:128]), reads=[pk], writes=[("xtok", 0)])
                S.dma("sp", "d_xtok", lambda e: e.dma_start(out=npo, in_=stage[0:8, 0:128]), reads=[("xtok", 0)], writes=["npo"])
                S.dma("sp", "d_mst", lambda e: e.dma_start(out=mpo, in_=mst[0:1, :]), reads=["mst"], writes=["mpo"])

        def stats_acc_g(NT, src_ap, j, srckey):
            t, tk = gtmp()
            S.op("act", lambda e: e.activation(out=t[:, 0:NT], in_=src_ap, func=AF.Square), reads=[srckey], writes=[tk])
            if j == 0:
                S.op("dve", lambda e: e.tensor_copy(out=ssum[:, 0:NT], in_=src_ap), reads=[srckey], writes=["ssum"])
                S.op("dve", lambda e: e.tensor_copy(out=ssq[:, 0:NT], in_=t[:, 0:NT]), reads=[tk], writes=["ssq"])
            else:
                S.op("dve", lambda e: e.tensor_tensor(out=ssum[:, 0:NT], in0=ssum[:, 0:NT], in1=src_ap, op=ALU.add), reads=[srckey, "ssum"], writes=["ssum"])
                S.op("dve", lambda e: e.tensor_tensor(out=ssq[:, 0:NT], in0=ssq[:, 0:NT], in1=t[:, 0:NT], op=ALU.add), reads=[tk, "ssq"], writes=["ssq"])

        def chunk(NT, sample, h, i, prefix=False):
            tsl = slice(i * 128, (i + 1) * 128)
            if prefix:
                state_update(h, i)
                return
            cNMc = con[:, C_NM4S:C_NM4S + 128] if sample else con[:, C_NM4:C_NM4 + 128]
            S.op("dve", lambda e: e.tensor_scalar(out=dg[:, 0, :], in0=ident, scalar1=sv[:, i, 16 + h:17 + h], scalar2=None, op0=ALU.mult), reads=["sv", "con"], writes=["dg"])
            pA, pkA = aux()
            S.op("pe", lambda e: e.matmul(pA[:, 0:128], lhsT=ones, rhs=dg[:, 0, :], start=True, stop=True), reads=["dg", "con"], writes=[pkA])
            S.op("dve", lambda e: e.tensor_tensor(out=dt_[:], in0=pA[:, 0:128], in1=cNMc, op=ALU.add), reads=[pkA, "con"], writes=["dt"])
            S.op("act", lambda e: e.activation(out=dt_[:], in_=dt_[:], func=AF.Exp, bias=negmx[:, i, h:h + 1]), reads=["dt", "gv"], writes=["dt"])
            pS, pkS = aux()
            mm_group(pS[:, 0:128], [(qkT[:, half, tsl], qkT[:, 2 + half, tsl]) for half in range(2)], ["qkT"], pkS)
            S.op("dve", lambda e: e.tensor_tensor(out=smf[:], in0=pS[:, 0:128], in1=dt_[:], op=ALU.mult), reads=[pkS, "dt"], writes=["smf"])
            S.op("dve", lambda e: e.tensor_reduce(out=sc1[:, 0:1], in_=smf[:], axis=AX.X, op=ALU.add), reads=["smf"], writes=["sc1"])
            S.op("act", lambda e: e.copy(out=smb[:], in_=smf[:]), reads=["smf"], writes=["smb"])
            S.op("pe", lambda e: e.transpose(pTb[:, 0:128], smb[:], idb[:]), reads=["smb", "idb"], writes=["pT"])
            S.op("act", lambda e: e.copy(out=smT[:], in_=pTb[:, 0:128]), reads=["pT"], writes=["smT"])
            gs, gk = gtmp()
            if not sample:
                S.op("dve", lambda e: e.tensor_copy(out=Cb[:], in_=Cst[:, h, :, :]), reads=["Cst"], writes=["Cb"])
                S.op("dve", lambda e: e.tensor_copy(out=nb[:, 0:2], in_=nst[:, 2 * h:2 * h + 2]), reads=["nst"], writes=["nb"])
                pG, pkG = acc()
                mm_group(pG[:], [(qkT[:, half, tsl], Cb[:, half, :]) for half in range(2)], ["qkT", "Cb"], pkG)
                pq, pkq = aux()
                mm_group(pq[:, 0:1], [(qkT[:, half, tsl], nb[:, half:half + 1]) for half in range(2)], ["qkT", "nb"], pkq)
                S.op("act", lambda e: e.activation(out=gs[:], in_=pG[:], func=AF.Identity, scale=wint[:, i, h:h + 1]), reads=[pkG, "gv"], writes=[gk])
                S.op("dve", lambda e: e.scalar_tensor_tensor(out=sc1[:, 1:2], in0=pq[:, 0:1], scalar=wint[:, i, h:h + 1], in1=sc1[:, 0:1], op0=ALU.mult, op1=ALU.add), reads=[pkq, "gv", "sc1"], writes=["sc1"])
            else:
                S.op("dve", lambda e: e.tensor_scalar(out=wm[:], in0=con[:, C_SM:C_SM + 16], scalar1=wint[:, i, h:h + 1], scalar2=None, op0=ALU.mult), reads=["con", "gv"], writes=["wm"])
                S.op("dve", lambda e: e.tensor_scalar(out=wsel[:, 0:16], in0=con[:, C_FM:C_FM + 16], scalar1=wc_[:, i, h:h + 1], scalar2=None, op0=ALU.mult), reads=["con", "gv"], writes=["wsel"])
                pW, pkW = aux()
                S.op("pe", lambda e: e.matmul(pW[:, 0:16], lhsT=ones, rhs=wsel[:, 0:16], start=True, stop=True), reads=["wsel", "con"], writes=[pkW])
                S.op("dve", lambda e: e.tensor_copy(out=wcb[:, 0:16], in_=pW[:, 0:16]), reads=[pkW], writes=["wcb"])
                S.op("dve", lambda e: e.tensor_copy(out=nb[:, 0:16], in_=n0T[:].rearrange("p (s r) -> p s r", r=8)[:, :, 2 * h]), reads=["n0T"], writes=["nb"])
                S.op("act", lambda e: e.copy(out=kw[:, 0:16], in_=n0T[:].rearrange("p (s r) -> p s r", r=8)[:, :, 2 * h + 1]), reads=["n0T"], writes=["kw"])
                pq, pkq = aux()
                mm_group(pq[:, 0:16], [(qkT[:, 0, tsl], nb[:, 0:16]), (qkT[:, 1, tsl], kw[:, 0:16])], ["qkT", "nb", "kw"], pkq)
                S.op("dve", lambda e: e.tensor_tensor(out=wsel[:, 16:32], in0=pq[:, 0:16], in1=wm[:], op=ALU.mult), reads=[pkq, "wm"], writes=["wsel2"])
                S.op("dve", lambda e: e.tensor_reduce(out=sc1[:, 2:3], in_=wsel[:, 16:32], axis=AX.X, op=ALU.add), reads=["wsel2"], writes=["sc1b"])
                S.op("dve", lambda e: e.tensor_tensor(out=sc1[:, 1:2], in0=sc1[:, 2:3], in1=sc1[:, 0:1], op=ALU.add), reads=["sc1b", "sc1"], writes=["sc1"])
                for s in range(16):
                    ci = nxt("c0f", 2)
                    S.dma("pool", "d_c0f%d" % ci, lambda e, s=s, ci=ci: e.dma_start(out=c0f[ci][:], in_=sC[s, h].rearrange("(a p) v -> p a v", p=128)), writes=[("c0f", ci)])
                    S.op("act", lambda e, ci=ci: e.copy(out=Cb[:], in_=c0f[ci][:]), reads=[("c0f", ci)], writes=["Cb"])
                    pG, pkG = acc()
                    mm_group(pG[:], [(qkT[:, half, tsl], Cb[:, half, :]) for half in range(2)], ["qkT", "Cb"], pkG)
                    if s == 0:
                        S.op("dve", lambda e, pG=pG, s=s: e.tensor_scalar(out=gs[:], in0=pG[:], scalar1=wm[:, s:s + 1], scalar2=None, op0=ALU.mult), reads=[pkG, "wm"], writes=[gk])
                    else:
                        S.op("dve", lambda e, pG=pG, s=s: e.scalar_tensor_tensor(out=gs[:], in0=pG[:], scalar=wm[:, s:s + 1], in1=gs[:], op0=ALU.mult, op1=ALU.add), reads=[pkG, "wm", gk], writes=[gk])
                    S.op("dve", lambda e, s=s: e.tensor_scalar(out=wks[:, 0:1], in0=wk_[:, i, h:h + 1], scalar1=con[:, C_SM + s:C_SM + s + 1], scalar2=None, op0=ALU.mult), reads=["gv", "con"], writes=["smf1"])
                    S.op("dve", lambda e: e.tensor_scalar(out=kw[:], in0=ktok[:, i, :], scalar1=wks[:, 0:1], scalar2=None, op0=ALU.mult), reads=["ktok", "smf1"], writes=["kw"])
                    for half in range(2):
                        pC, pkC = acc()
                        mm_group(pC[:], [(kw[:, half * 128:(half + 1) * 128], vtok[:, i, :])], ["kw", "vtok"], pkC)
                        cn = nxt("cnew", 2)
                        cnk = ("cnew", 0) if cn == 0 else "yT"
                        S.op("dve", lambda e, half=half, pC=pC, cn=cn, ci=ci, s=s: e.scalar_tensor_tensor(out=cnew[cn][:], in0=c0f[ci][:, half, :], scalar=wcb[:, s:s + 1], in1=pC[:], op0=ALU.mult, op1=ALU.add), reads=[("c0f", ci), "wcb", pkC], writes=[cnk])
                        S.dma("sp", "d_cnew%d" % cn, lambda e, half=half, cn=cn, s=s: e.dma_start(out=Cs[s, h, half * 128:(half + 1) * 128, :], in_=cnew[cn][:]), reads=[cnk], writes=[("Cs", s, h, half)])
                        pn, pkn = aux()
                        mm_group(pn[:, 0:1], [(kw[:, half * 128:(half + 1) * 128], oneb[:, 0:1])], ["kw", "oneb"], pkn)
                        col = s * 8 + 2 * h + half
                        S.op("dve", lambda e, pn=pn, col=col, s=s: e.scalar_tensor_tensor(out=nnew[:, col:col + 1], in0=n0T[:, col:col + 1], scalar=wcb[:, s:s + 1], in1=pn[:, 0:1], op0=ALU.mult, op1=ALU.add), reads=["n0T", "wcb", pkn], writes=["nnew"])
            pN, pkN = acc()
            mm_group(pN[:], [(smT[:], vtok[:, i, :])], ["smT", "vtok"], pkN)
            hu, hk = gtmp()
            S.op("dve", lambda e: e.tensor_tensor(out=hu[:], in0=pN[:], in1=gs[:], op=ALU.add), reads=[pkN, gk], writes=[hk])
            S.op("dve", lambda e: e.tensor_scalar(out=sc1[:, 9:10], in0=sc1[:, 1:2], scalar1=-1.0, scalar2=None, op0=ALU.mult), reads=["sc1"], writes=["sc1n"])
            S.op("dve", lambda e: e.tensor_tensor(out=sc1[:, 9:10], in0=sc1[:, 9:10], in1=sc1[:, 1:2], op=ALU.max), reads=["sc1", "sc1n"], writes=["sc1n"])
            S.op("dve", lambda e: e.tensor_tensor(out=sc1[:, 3:4], in0=sc1[:, 9:10], in1=emt[:, i, h:h + 1], op=ALU.max), reads=["sc1n", "gv"], writes=["sc1c"])
            S.op("dve", lambda e: e.bn_stats(out=bnst[:], in_=hu[:]), reads=[hk], writes=["bnst"])
            S.op("dve", lambda e: e.bn_aggr(out=sc1[:, 4:6], in_=bnst[:]), reads=["bnst"], writes=["sc1d"])
            S.op("dve", lambda e: e.tensor_scalar(out=sc1[:, 6:7], in0=sc1[:, 3:4], scalar1=sc1[:, 3:4], scalar2=EPS, op0=ALU.mult, op1=ALU.mult), reads=["sc1c"], writes=["sc1e"])
            S.op("act", lambda e: e.activation(out=sc1[:, 7:8], in_=sc1[:, 5:6], func=AF.Ln, bias=sc1[:, 6:7]), reads=["sc1d", "sc1e"], writes=["sc1f"])
            S.op("act", lambda e: e.activation(out=sc1[:, 7:8], in_=sc1[:, 7:8], func=AF.Exp, scale=-0.5), reads=["sc1f"], writes=["sc1f"])
            S.op("dve", lambda e: e.tensor_scalar(out=hu[:], in0=hu[:], scalar1=sc1[:, 4:5], scalar2=sc1[:, 7:8], op0=ALU.subtract, op1=ALU.mult), reads=[hk, "sc1d", "sc1f"], writes=[hk])
            S.op("dve", lambda e: e.tensor_tensor(out=termA[:, i, h * 512:(h + 1) * 512], in0=hu[:], in1=termA[:, i, h * 512:(h + 1) * 512], op=ALU.mult), reads=[hk, "termA"], writes=["termA"])
            if not sample:
                state_update(h, i)

        def state_update(h, i):
            S.op("dve", lambda e: e.tensor_scalar(out=kw[:], in0=ktok[:, i, :], scalar1=wk_[:, i, h:h + 1], scalar2=None, op0=ALU.mult), reads=["ktok", "gv"], writes=["kw"])
            for half in range(2):
                pC, pkC = acc()
                mm_group(pC[:], [(kw[:, half * 128:(half + 1) * 128], vtok[:, i, :])], ["kw", "vtok"], pkC)
                S.op("dve", lambda e, half=half, pC=pC: e.scalar_tensor_tensor(out=Cst[:, h, half, :], in0=Cst[:, h, half, :], scalar=wc_[:, i, h:h + 1], in1=pC[:], op0=ALU.mult, op1=ALU.add), reads=["Cst", "gv", pkC], writes=["Cst"])
                pn, pkn = aux()
                mm_group(pn[:, 0:1], [(kw[:, half * 128:(half + 1) * 128], oneb[:, 0:1])], ["kw", "oneb"], pkn)
                S.op("dve", lambda e, half=half, pn=pn: e.scalar_tensor_tensor(out=nst[:, 2 * h + half:2 * h + half + 1], in0=nst[:, 2 * h + half:2 * h + half + 1], scalar=wc_[:, i, h:h + 1], in1=pn[:, 0:1], op0=ALU.mult, op1=ALU.add), reads=["nst", "gv", pkn], writes=["nst"])


        def conv_branch(NT, sample, last, prefix=False):
            ntile = NT // 128
            win = W["win"]
            for s in range(8):
                sl, wk = wslot()
                wload(win, 0, D, O_GA + s * 256, 256, sl, 0, wk, "a")
                wload(win, 0, D, O_GB + s * 256, 256, sl, 4096, wk, "b")
                wv = wb[sl][:].rearrange("p (k c) -> p k c", k=32)
                for cc_ in range(2):
                    j = 2 * s + cc_
                    n0 = NT - 128 if prefix else 0
                    pa, pka = acc()
                    mm_group(pa[:, n0:NT], [(wv[:, kc, cc_ * 128:(cc_ + 1) * 128], xT[:, kc, n0:NT]) for kc in range(16)], [wk, "xT"], pka)
                    pb2, pkb = acc()
                    mm_group(pb2[:, n0:NT], [(wv[:, 16 + kc, cc_ * 128:(cc_ + 1) * 128], xT[:, kc, n0:NT]) for kc in range(16)], [wk, "xT"], pkb)
                    t, tk = gtmp()
                    S.op("act", lambda e, t=t, pb2=pb2, n0=n0: e.activation(out=t[:, n0:NT], in_=pb2[:, n0:NT], func=AF.Sigmoid), reads=[pkb], writes=[tk])
                    if prefix:
                        up = upb[j % 2]; uk = ("upb", j % 2)
                        S.op("dve", lambda e, up=up, t=t, pa=pa: e.tensor_tensor(out=up[:, HIST + 384:HIST + 512], in0=pa[:, 384:512], in1=t[:, 384:512], op=ALU.mult), reads=[pka, tk, uk], writes=[uk])
                        S.op("act", lambda e, up=up, j=j: e.copy(out=hist[:, j, :], in_=up[:, 512:512 + HIST]), reads=[uk], writes=["hist"])
                        continue
                    cwc = P_CWT + j * CW
                    S.op("dve", lambda e, cwc=cwc: e.tensor_tensor(out=cdv, in0=idb[:].unsqueeze(1).broadcast_to([128, CW, 128]), in1=pf[:, cwc:cwc + CW].unsqueeze(2).broadcast_to([128, CW, 128]), op=ALU.mult), reads=["idb", "pf"], writes=["dg"])
                    py, pky = acc()
                    if not sample:
                        up = upb[j % 2]; uk = ("upb", j % 2)
                        S.op("act", lambda e, up=up, j=j: e.copy(out=up[:, 0:HIST], in_=hist[:, j, :]), reads=["hist"], writes=[uk])
                        S.op("dve", lambda e, up=up, t=t, pa=pa: e.tensor_tensor(out=up[:, HIST:HIST + 512], in0=pa[:, 0:512], in1=t[:, 0:512], op=ALU.mult), reads=[pka, tk, uk], writes=[uk])
                        S.op("act", lambda e, up=up, j=j: e.copy(out=hist[:, j, :], in_=up[:, 512:512 + HIST]), reads=[uk], writes=["hist"])
                        mm_group(py[:, 0:512], [(cdv[:, k, :], up[:, k:k + 512]) for k in range(CW)], ["dg", uk], pky)
                        if last:
                            S.op("dve", lambda e, t=t, pa=pa: e.tensor_tensor(out=utail[:, 0:HIST], in0=pa[:, 512 - HIST:512], in1=t[:, 512 - HIST:512], op=ALU.mult), reads=[pka, tk], writes=["utail"])
                            ps, pk = aux()
                            S.op("pe", lambda e, ps=ps: e.transpose(ps[0:HIST, 0:128], utail[:, 0:HIST], ident), reads=["utail", "con"], writes=[pk])
                            S.op("act", lambda e, ps=ps, j=j: e.copy(out=stage[0:HIST, j * 128:(j + 1) * 128], in_=ps[0:HIST, 0:128]), reads=[pk], writes=[("xtok", 0)])
                    else:
                        S.dma("sp", "d_ctk", lambda e, j=j: e.dma_start(out=ctk[0:120, :, :], in_=cc.rearrange("(g q) r c -> (q r) g c", g=4)[:, :, j * 128:(j + 1) * 128]), writes=["ctk"])
                        ps, pk = aux()
                        S.op("pe", [lambda e, g=g, ps=ps: e.transpose(ps[:, g * 120:(g + 1) * 120], ctk[0:120, g, :], ident[0:120, 0:120]) for g in range(4)], reads=["ctk", "con"], writes=[pk])
                        S.op("act", lambda e, ps=ps: e.copy(out=upS[:, :, 0:HIST], in_=ps[:, 0:480].rearrange("p (s r) -> p s r", r=HIST)), reads=[pk], writes=["upS"])
                        S.op("dve", lambda e, t=t, pa=pa: e.tensor_tensor(out=upS[:, :, HIST:HIST + 8], in0=pa[:, 0:128].rearrange("p (s r) -> p s r", r=8), in1=t[:, 0:128].rearrange("p (s r) -> p s r", r=8), op=ALU.mult), reads=[pka, tk, "upS"], writes=["upS"])
                        mm_group(py[:, 0:128], [(cdv[:, k, :], upS[:, :, k:k + 8]) for k in range(CW)], ["dg", "upS"], pky)
                        t2, tk2 = gtmp()
                        S.op("dve", lambda e, t2=t2, t=t, pa=pa: e.tensor_tensor(out=t2[:, 0:128], in0=pa[:, 0:128], in1=t[:, 0:128], op=ALU.mult), reads=[pka, tk], writes=[tk2])
                        ps2, pk2 = aux()
                        S.op("pe", lambda e, ps2=ps2, t2=t2: e.transpose(ps2[:, 0:128], t2[:, 0:128], ident), reads=[tk2, "con"], writes=[pk2])
                        S.op("act", lambda e, ps2=ps2, j=j: e.copy(out=stage[:, j * 128:(j + 1) * 128], in_=ps2[:, 0:128]), reads=[pk2], writes=[("xtok", 0)])
                    S.op("act", lambda e, py=py, j=j: e.activation(out=ybuf[:, 0:NT], in_=py[:, 0:NT], func=AF.Identity, bias=pf[:, P_CB + j:P_CB + j + 1]), reads=[pky, "pf"], writes=["yT"])
                    stats_acc_g(NT, ybuf[:, 0:NT], j, "yT")
                    S.dma("sp", "d_ybuf", lambda e, j=j: e.dma_start(out=yscr[j, :, 0:NT], in_=ybuf[:, 0:NT]), reads=["yT"], writes=[("yscr", j)])
            if prefix:
                return
            if sample:
                for s in range(16):
                    S.dma("sp", "d_xtok", lambda e, s=s: e.dma_start(out=cso[s, HIST - 8:HIST, :], in_=stage[s * 8:(s + 1) * 8, :]), reads=[("xtok", 0)], writes=[("cso_b", s)])
            elif last:
                S.dma("sp", "d_xtok", lambda e: e.dma_start(out=cpo, in_=stage[0:HIST, :]), reads=[("xtok", 0)], writes=["cpo"])
            ckpt(7)
            conv_ln_and_mix(NT, sample)

        def conv_ln_and_mix(NT, sample):
            ntile = NT // 128
            win = W["win"]
            ps, pk = aux()
            S.op("pe", lambda e, ps=ps: e.matmul(ps[:, 0:NT], lhsT=ones, rhs=ssum[:, 0:NT], start=True, stop=True), reads=["con", "ssum"], writes=[pk])
            S.op("act", lambda e, ps=ps: e.mul(out=mean[:, 0:NT], in_=ps[:, 0:NT], mul=1.0 / D), reads=[pk], writes=["mean"])
            ps2, pk2 = aux()
            S.op("pe", lambda e, ps2=ps2: e.matmul(ps2[:, 0:NT], lhsT=ones, rhs=ssq[:, 0:NT], start=True, stop=True), reads=["con", "ssq"], writes=[pk2])
            S.op("act", lambda e, ps2=ps2: e.mul(out=rstd[:, 0:NT], in_=ps2[:, 0:NT], mul=1.0 / D), reads=[pk2], writes=["rstd"])
            t, tk = gtmp()
            S.op("dve", lambda e, t=t: e.tensor_tensor(out=t[:, 0:NT], in0=mean[:, 0:NT], in1=mean[:, 0:NT], op=ALU.mult), reads=["mean"], writes=[tk])
            S.op("dve", lambda e, t=t: e.tensor_tensor(out=rstd[:, 0:NT], in0=rstd[:, 0:NT], in1=t[:, 0:NT], op=ALU.subtract), reads=[tk, "rstd"], writes=["rstd"])
            S.op("act", lambda e: e.activation(out=rstd[:, 0:NT], in_=rstd[:, 0:NT], func=AF.Ln, bias=epsc[:, 0:1]), reads=["rstd", "epsc"], writes=["rstd"])
            S.op("act", lambda e: e.activation(out=rstd[:, 0:NT], in_=rstd[:, 0:NT], func=AF.Exp, scale=-0.5), reads=["rstd"], writes=["rstd"])
            for s in range(4):
                sl, wk = wslot()
                wload(win, 0, D, O_TB + s * 512, 512, sl, 0, wk)
                wv = wb[sl][:].rearrange("p (k c) -> p k c", k=16)
                for cc_ in range(4):
                    j = 4 * s + cc_
                    pg, pkg = acc()
                    mm_group(pg[:, 0:NT], [(wv[:, kc, cc_ * 128:(cc_ + 1) * 128], xT[:, kc, 0:NT]) for kc in range(16)], [wk, "xT"], pkg)
                    t, tk = gtmp()
                    S.op("act", lambda e, t=t, pg=pg: e.activation(out=t[:, 0:NT], in_=pg[:, 0:NT], func=AF.Sigmoid), reads=[pkg], writes=[tk])
                    t2, tk2 = gtmp()
                    S.dma("sp", "d_tmp%d" % tk2[1], lambda e, j=j, t2=t2: e.dma_start(out=t2[:, 0:NT], in_=yscr[j, :, 0:NT]), reads=[("yscr", j)], writes=[tk2])
                    S.op("dve", lambda e, j=j, t2=t2: e.tensor_tensor(out=t2[:, 0:NT], in0=t2[:, 0:NT], in1=mean[:, 0:NT], op=ALU.subtract), reads=[tk2, "mean"], writes=[tk2])
                    S.op("dve", lambda e, t2=t2: e.tensor_tensor(out=t2[:, 0:NT], in0=t2[:, 0:NT], in1=rstd[:, 0:NT], op=ALU.mult), reads=[tk2, "rstd"], writes=[tk2])
                    S.op("dve", lambda e, j=j, t2=t2: e.tensor_scalar(out=t2[:, 0:NT], in0=t2[:, 0:NT], scalar1=pf[:, P_CLG + j:P_CLG + j + 1], scalar2=pf[:, P_CLB + j:P_CLB + j + 1], op0=ALU.mult, op1=ALU.add), reads=[tk2, "pf"], writes=[tk2])
                    s1, sk1 = gtmp()
                    S.op("act", lambda e, s1=s1, t2=t2: e.activation(out=s1[:, 0:NT], in_=t2[:, 0:NT], func=AF.Sigmoid), reads=[tk2], writes=[sk1])
                    S.op("dve", lambda e, s1=s1, t2=t2: e.tensor_tensor(out=t2[:, 0:NT], in0=t2[:, 0:NT], in1=s1[:, 0:NT], op=ALU.mult), reads=[sk1, tk2], writes=[tk2])
                    S.op("dve", lambda e, t=t, t2=t2: e.tensor_tensor(out=t2[:, 0:NT], in0=t2[:, 0:NT], in1=t[:, 0:NT], op=ALU.mult), reads=[tk, tk2], writes=[tk2])
                    S.op("pe", [lambda e, i=i, j=j: e.transpose(pTb[:, i * 128:(i + 1) * 128], termA[:, i, j * 128:(j + 1) * 128], idb[:]) for i in range(ntile)], reads=["termA", "idb"], writes=["pT"])
                    S.op("dve", lambda e, j=j, t2=t2: e.tensor_tensor(out=mixT[:, j, 0:NT], in0=pTb[:, 0:NT], in1=t2[:, 0:NT], op=ALU.add), reads=["pT", tk2], writes=["mixT"])

        def main_seq():
            run_pass(512, False, xpre[0:512, :], None, True, False, prefix=True)
            win_ = W["win"]
            bgq = []
            for blk in range(4):
                bgq.append((win_, 0, D, O_O + blk * 512, 512))
            for blk in range(4):
                bgq.append((win_, 0, D, O_TA + blk * 512, 512))
            for h in range(NH):
                bgq.append((win_, 0, D, O_Q + h * 256, 256))
            for s_ in range(4):
                bgq.append((win_, 0, D, O_TB + s_ * 512, 512))
            for s_ in range(4):
                bgq.append((W["wout"], 0, D, s_ * 512, 512))
            for s_ in range(22):
                bgq.append((W["f2w1"], 0, D, s_ * 256, 256))
                bgq.append((W["f2w3"], 0, D, s_ * 256, 256))
            for j in range(16):
                bgq.append((W["f2w2"], 0, DFF, j * 128, 128))
            wstate["bgq"] = bgq
            wstate["bg_on"] = True
            run_pass(512, False, xpre[512:1024, :], None, False, False, prefix=True, lastprefix=True)
            S.op("dve", lambda e: e.tensor_scalar(out=Cst[:].rearrange("p h a v -> p (h a v)"), in0=Cst[:].rearrange("p h a v -> p (h a v)"), scalar1=msk[:, 0:1], scalar2=None, op0=ALU.mult), reads=["Cst", "msk"], writes=["Cst"])
            S.op("dve", lambda e: e.tensor_scalar(out=nst[:], in0=nst[:], scalar1=msk[:, 0:1], scalar2=None, op0=ALU.mult), reads=["nst", "msk"], writes=["nst"])
            S.op("dve", lambda e: e.tensor_scalar(out=mst[:], in0=mst[:], scalar1=msk[:, 0:1], scalar2=None, op0=ALU.mult), reads=["mst", "msk"], writes=["mst"])
            S.op("dve", lambda e: e.tensor_scalar(out=hist[:].rearrange("p j r -> p (j r)"), in0=hist[:].rearrange("p j r -> p (j r)"), scalar1=msk[:, 0:1], scalar2=None, op0=ALU.mult), reads=["hist", "msk"], writes=["hist"])
            run_pass(512, False, xp[0:512, :], yp[0:512, :], False, False)
            run_pass(512, False, xp[512:1024, :], yp[512:1024, :], False, True)
            wstate["nslots"] = 3
            run_pass(128, True, xs, ys, False, True)
            ps, pk = aux()
            S.op("pe", lambda e: e.transpose(ps[:, 0:128], nnew[:], ident), reads=["nnew", "con"], writes=[pk])
            S.op("dve", lambda e: e.tensor_copy(out=stage[:, 0:128], in_=ps[:, 0:128]), reads=[pk], writes=[("xtok", 0)])
            S.dma("sp", "d_xtok", lambda e: e.dma_start(out=nso, in_=stage[:, 0:128]), reads=[("xtok", 0)], writes=["nso"])
            for s in range(16):
                S.dma("sp", "d_mst", lambda e, s=s: e.dma_start(out=mso[s:s + 1, :], in_=mst[8 * s:8 * s + 1, :]), reads=["mst"], writes=[("mso", s)])
        try:
            main_seq()
        except _Stop:
            pass
        S.barrier()
        S.emit(block)
    return nc


def _consts():
    c = np.zeros((128, NCON), np.float32)
    idx = np.arange(128)
    c[:, C_ID:C_ID + 128] = np.eye(128)
    c[:, C_ONE:C_ONE + 128] = 1.0
    tri = (idx[:, None] <= idx[None, :]).astype(np.float32)
    c[:, C_TRI:C_TRI + 128] = tri
    nm = np.where(idx[None, :] <= idx[:, None], 0.0, NEG).astype(np.float32)
    c[:, C_NM4:C_NM4 + 128] = nm
    seq = idx // 8
    same = (seq[:, None] == seq[None, :])
    c[:, C_UB:C_UB + 128] = tri * same
    c[:, C_TB:C_TB + 128] = same.astype(np.float32)
    nms = np.where(same & (idx[None, :] <= idx[:, None]), 0.0, NEG).astype(np.float32)
    c[:, C_NM4S:C_NM4S + 128] = nms
    nfs = np.where(same, 0.0, NEG).astype(np.float32)
    c[:, C_NF4S:C_NF4S + 128] = nfs
    c[:, C_SM:C_SM + 16] = (seq[:, None] == np.arange(16)[None, :])
    c[:, C_FM:C_FM + 16] = (idx[:, None] == 8 * np.arange(16)[None, :])
    return c


_NC = None


def kernel(x_prompt, x_sample, state_C, state_n, state_m, cache_conv,
           ffn1_w1, ffn1_w3, ffn1_w2, ln1_g, ln1_b, w_in, b_igate, b_fgate, mh_norm_g,
           conv_w, conv_b, conv_ln_g, conv_ln_b, w_out, ln2_g, ln2_b,
           ffn2_w1, ffn2_w3, ffn2_w2, ln3_g, ln3_b):
    global _NC
    f = lambda a: np.ascontiguousarray(np.asarray(a, dtype=np.float32))
    pfa = np.zeros((128, NPF), np.float32)
    for k, a in enumerate([ln1_g, ln1_b, ln2_g, ln2_b, ln3_g, ln3_b, conv_ln_g, conv_ln_b, conv_b]):
        pfa[:, k * 16:(k + 1) * 16] = f(a)[0].reshape(16, 128).T
    cw = f(conv_w)[0]
    pfa[:, P_CWT:P_CWT + 16 * CW] = cw.reshape(CW, 16, 128).transpose(2, 1, 0).reshape(128, 16 * CW)
    mhg = np.ascontiguousarray(np.broadcast_to(f(mh_norm_g)[0][None, :], (128, D)))
    gbb = np.ascontiguousarray(np.broadcast_to(np.concatenate([f(b_igate)[0], f(b_fgate)[0]])[None, :], (128, 8)))
    con = _consts()
    shared = {"f1w1": f(ffn1_w1)[0], "f1w3": f(ffn1_w3)[0], "f1w2": f(ffn1_w2)[0], "win": f(w_in)[0], "wout": f(w_out)[0],
              "f2w1": f(ffn2_w1)[0], "f2w3": f(ffn2_w3)[0], "f2w2": f(ffn2_w2)[0], "pf": pfa, "consts": con, "mhg": mhg, "gb": gbb}
    xpr = f(x_prompt); xsa = f(x_sample); sCf = f(state_C)[0]; snf = f(state_n)[0]; smf_ = f(state_m)[0]; ccf = f(cache_conv)[0]
    in_maps = []
    for c in range(8):
        sl = slice(16 * c, 16 * (c + 1))
        m = dict(shared)
        half = c // 4
        m["xp"] = np.ascontiguousarray(xpr[c % 4, half * 1024:(half + 1) * 1024])
        m["xpre"] = np.ascontiguousarray(xpr[c % 4, 0:1024])
        m["msk"] = np.full((128, 1), float(half), np.float32)
        m["xs"] = xsa[sl].reshape(128, D)
        m["sC"] = sCf[sl]
        m["sn"] = snf[sl].reshape(128, 128)
        m["smt"] = np.ascontiguousarray(np.repeat(smf_[sl], 8, axis=0))
        m["cc"] = ccf[sl]
        in_maps.append(m)
    if _NC is None:
        _NC = build_program()
    res = run_bass_kernel_spmd(_NC, in_maps, core_ids=list(range(8)))
    R = res.results
    y_p = np.stack([np.concatenate([R[c]["yp"], R[c + 4]["yp"]]) for c in range(4)])
    y_s = np.concatenate([R[c]["ys"].reshape(16, 8, D) for c in range(8)])
    C_p = np.stack([R[c + 4]["Cp"] for c in range(4)])[None]
    n_p = np.stack([R[c + 4]["np"].reshape(NH, DQK) for c in range(4)])[None]
    m_p = np.stack([R[c + 4]["mp"].reshape(NH) for c in range(4)])[None]
    c_p = np.stack([R[c + 4]["cp"] for c in range(4)])[None]
    C_s = np.concatenate([R[c]["Cs"] for c in range(8)])[None]
    n_s = np.concatenate([R[c]["ns"].reshape(16, NH, DQK) for c in range(8)])[None]
    m_s = np.concatenate([R[c]["ms"] for c in range(8)])[None]
    c_s = np.concatenate([R[c]["cs"] for c in range(8)])[None]
    return tuple(np.ascontiguousarray(a, dtype=np.float32) for a in (y_p, y_s, C_p, n_p, m_p, c_p, C_s, n_s, m_s, c_s))
```

```python
import numpy as np
from contextlib import ExitStack
import concourse.bass as bass
import concourse.mybir as mybir
from concourse.bass_utils import run_bass_kernel_spmd

F32 = mybir.dt.float32
BF16 = mybir.dt.bfloat16
ALU = mybir.AluOpType
AF = mybir.ActivationFunctionType
AX = mybir.AxisListType

D = 2048
DFF = 5632
DIN = 14344
NH = 4
DQK = 256
DV = 512
CW = 31
HIST = CW - 1
EPS = 1e-5
ALPHA = 2.0 ** 0.25
NEG = -30000.0
O_Q, O_K, O_V, O_O, O_G, O_GA, O_GB, O_TA, O_TB = 0, 1024, 2048, 4096, 6144, 6152, 8200, 10248, 12296
C_ID, C_ONE, C_TRI, C_NM4, C_UB, C_TB, C_NM4S, C_NF4S, C_SM, C_FM, NCON = 0, 128, 256, 384, 512, 640, 768, 896, 1024, 1040, 1056
P_L1G, P_L1B, P_L2G, P_L2B, P_L3G, P_L3B, P_CLG, P_CLB, P_CB, P_CWT, NPF = 0, 16, 32, 48, 64, 80, 96, 112, 128, 144, 640


import os


class _Stop(Exception):
    pass


def ckpt(n):
    lim = float(os.environ.get("KSTOP", "0"))
    if lim and n >= lim:
        raise _Stop()


class Sched:
    def __init__(self, nc, stack):
        self.nc = nc
        self.eng = {"pe": nc.tensor, "act": nc.scalar, "dve": nc.vector, "pool": nc.gpsimd, "sp": nc.sync}
        self.sems = {}
        self.stack = stack
        self.cnt = {}
        self.seen = {e: {} for e in self.eng}
        self.ops = {e: [] for e in self.eng}
        self.buf = {}

    def sem(self, key):
        if key not in self.sems:
            self.sems[key] = self.stack.enter_context(self.nc.semaphore("s_" + key))
            self.cnt[key] = 0
        return self.sems[key]

    def _deps(self, reads, writes):
        toks = []
        for k in reads:
            st = self.buf.get(k)
            if st and st[0] is not None:
                toks.append(st[0])
        for k in writes:
            st = self.buf.get(k)
            if st:
                if st[0] is not None:
                    toks.append(st[0])
                toks.extend(st[1])
        return toks

    def _record(self, tok, reads, writes):
        for k in reads:
            st = self.buf.setdefault(k, [None, []])
            st[1].append(tok)
            if len(st[1]) > 48:
                best = {}
                for s, v in st[1]:
                    if best.get(s, -1) < v:
                        best[s] = v
                st[1] = list(best.items())
        for k in writes:
            self.buf[k] = [tok, []]

    def _waits(self, e, toks):
        need = {}
        for s, v in toks:
            if s not in self.eng:
                v = self.cnt[s]
            if self.seen[e].get(s, 0) >= v:
                continue
            if s == e and e == "pe":
                continue
            if need.get(s, 0) < v:
                need[s] = v
        out = []
        for s, v in need.items():
            self.seen[e][s] = v
            out.append((s, v))
        return out

    def op(self, e, fns, reads=(), writes=()):
        if not isinstance(fns, (list, tuple)):
            fns = [fns]
        waits = self._waits(e, self._deps(reads, writes))
        self.sem(e)
        self.cnt[e] += 1
        tok = (e, self.cnt[e])
        self.ops[e].append((waits, list(fns), (e, 1)))
        self._record(tok, reads, writes)
        return tok

    def dma(self, q, dsem, fn, reads=(), writes=()):
        waits = self._waits(q, self._deps(reads, writes))
        self.sem(dsem)
        self.cnt[dsem] += 16
        tok = (dsem, self.cnt[dsem])
        self.ops[q].append((waits, [fn], (dsem, 16)))
        self._record(tok, reads, writes)
        return tok

    def barrier(self):
        toks = [(s, v) for s, v in self.cnt.items() if v > 0]
        for e in self.eng:
            w = self._waits(e, toks)
            if w:
                self.ops[e].append((w, [], None))
        self.buf = {}

    def soft_barrier(self):
        toks = [(s, self.cnt[s]) for s in ("pe", "act", "dve", "pool") if self.cnt.get(s, 0) > 0]
        for e in ("pe", "act", "dve"):
            w = self._waits(e, toks)
            if w:
                self.ops[e].append((w, [], None))

    def emit(self, block):
        sems = self.sems
        for e, name in (("pe", "tensor"), ("act", "scalar"), ("dve", "vector"), ("pool", "gpsimd"), ("sp", "sync")):
            def body(engine, ops=self.ops[e]):
                for waits, fns, inc in ops:
                    for s, v in waits:
                        engine.wait_ge(sems[s], v)
                    for i, fn in enumerate(fns):
                        ins = fn(engine)
                        if i == len(fns) - 1 and inc is not None:
                            ins.then_inc(sems[inc[0]], inc[1])
            getattr(block, name)(body)


def build_program():
    nc = bass.Bass("TRN2", target_bir_lowering=False)

    def din(name, shape):
        return nc.dram_tensor(name, list(shape), F32, kind="ExternalInput").ap()

    def dout(name, shape):
        return nc.dram_tensor(name, list(shape), F32, kind="ExternalOutput").ap()

    xp = din("xp", [1024, D]); xpre = din("xpre", [1024, D]); msk_d = din("msk", [128, 1]); xs = din("xs", [128, D])
    sC = din("sC", [16, NH, DQK, DV]); sn = din("sn", [128, 128]); smt = din("smt", [128, NH]); cc = din("cc", [16, HIST, D])
    W = {n: din(n, s) for n, s in [("f1w1", [D, DFF]), ("f1w3", [D, DFF]), ("f1w2", [DFF, D]), ("win", [D, DIN]), ("wout", [D, D]),
                                   ("f2w1", [D, DFF]), ("f2w3", [D, DFF]), ("f2w2", [DFF, D])]}
    pf_d = din("pf", [128, NPF]); con_d = din("consts", [128, NCON]); mhg_d = din("mhg", [128, D]); gb_d = din("gb", [128, 8])
    yp = dout("yp", [1024, D]); ys = dout("ys", [128, D])
    Cp = dout("Cp", [NH, DQK, DV]); npo = dout("np", [8, 128]); mpo = dout("mp", [1, NH]); cpo = dout("cp", [HIST, D])
    Cs = dout("Cs", [16, NH, DQK, DV]); nso = dout("ns", [128, 128]); mso = dout("ms", [16, NH]); cso = dout("cs", [16, HIST, D])

    yscr = nc.dram_tensor("yscr", [16, 128, 512], F32).ap()
    wcache = nc.dram_tensor("wcache", [128, 803072], BF16).ap()
    with ExitStack() as st:
        S = Sched(nc, st)
        uid = [0]

        def sb(shape, dt=F32):
            uid[0] += 1
            return st.enter_context(nc.sbuf_tensor("t%d" % uid[0], list(shape), dt))

        def pst(shape, dt=F32):
            uid[0] += 1
            return st.enter_context(nc.psum_tensor("p%d" % uid[0], list(shape), dt))

        xT = sb([128, 16, 512], BF16)
        resT = sb([128, 16, 512])
        ybuf = sb([128, 512])
        big = sb([128, 44 * 512], BF16)
        hid = big[:].rearrange("p (f t) -> p f t", f=44)
        termA = big[:, 0:8192].rearrange("p (i c) -> p i c", i=4)
        mixT = big[:, 8192:16384].rearrange("p (k t) -> p k t", k=16)
        qkT = big[:, 16384:18432].rearrange("p (o t) -> p o t", o=4)
        ktok = big[:, 18432:19456].rearrange("p (i c) -> p i c", i=4)
        vtok = big[:, 19456:21504].rearrange("p (i c) -> p i c", i=4)
        wb = [sb([128, 8192], BF16) for _ in range(2)]
        Cst = sb([128, NH, 2, DV]); nst = sb([128, 8]); mst = sb([128, NH])
        Cb = sb([128, 2, DV], BF16); nb = sb([128, 16], BF16)
        wb.append(Cst[:].rearrange("p h a v -> p (h a v)").bitcast(BF16))
        con = sb([128, NCON]); pf = sb([128, NPF]); gb = sb([128, 8]); msk = sb([128, 1]); pfa = sb([128, 96])
        idb = sb([128, 128], BF16); oneb = sb([128, 16], BF16); epsc = sb([128, 2])
        xtok = [sb([128, D])]
        tmp = [sb([128, 512]) for _ in range(4)]
        ssum = sb([128, 512]); ssq = sb([128, 512]); mean = sb([128, 512]); rstd = sb([128, 512])
        gts = sb([128, 4, 8]); cdiag = sb([128, CW * 128], BF16)
        dg = cdiag[:, 0:1024].bitcast(F32).rearrange("p (h s) -> p h s", h=4)
        cdv = cdiag[:].rearrange("p (k q) -> p k q", k=CW)
        sv = sb([128, 4, 64])
        negmx = sb([128, 4, 4]); wint = sb([128, 4, 4]); wk_ = sb([128, 4, 4]); wc_ = sb([128, 4, 4]); emt = sb([128, 4, 4])
        dt_ = sb([128, 128]); smf = sb([128, 128]); smb = sb([128, 128], BF16); smT = sb([128, 128], BF16)
        kw = sb([128, 256], BF16); sc1 = sb([128, 16]); bnst = sb([128, 6]); wks = sb([128, 2])
        upb = [sb([128, HIST + 512], BF16) for _ in range(2)]
        hist = sb([128, 16, HIST], BF16); utail = sb([128, 32])
        stage = xtok[0]
        c0f = [sb([128, 2, DV]) for _ in range(2)]
        n0T = sb([128, 128]); nnew = sb([128, 128]); wm = sb([128, 16]); wcb = sb([128, 64]); wsel = sb([128, 64])
        ctk2 = [sb([128, 4, 128]) for _ in range(2)]; upS = sb([128, 16, HIST + 8], BF16); cnew = [sb([128, DV]), ybuf]
        accb = [pst([128, 512]) for _ in range(5)]
        auxb = [pst([128, 512]) for _ in range(2)]
        pTb = pst([128, 1024], BF16)
        ring = {"acc": 0, "aux": 0, "wb": 0, "tmp": 0, "c0f": 0, "cnew": 0, "ystg": 0, "xtok": 0}

        def nxt(kind, n):
            i = ring[kind]
            ring[kind] = (i + 1) % n
            return i

        def acc():
            i = nxt("acc", 5)
            return accb[i], ("acc", i)

        def aux():
            i = nxt("aux", 2)
            return auxb[i], ("aux", i)

        def gtmp():
            i = nxt("tmp", 4)
            return tmp[i], ("tmp", i)

        block = st.enter_context(nc.Block())
        ident = con[:, C_ID:C_ID + 128]
        ones = con[:, C_ONE:C_ONE + 128]

        S.dma("sp", "c_con", lambda e: e.dma_start(out=con[:], in_=con_d), writes=["con"])
        S.dma("sp", "c_pf", lambda e: e.dma_start(out=pf[:], in_=pf_d), writes=["pf"])
        S.dma("sp", "c_gb", lambda e: e.dma_start(out=gb[:], in_=gb_d), writes=["gb"])
        S.dma("sp", "c_msk", lambda e: e.dma_start(out=msk[:], in_=msk_d), writes=["msk"])
        S.op("dve", lambda e: e.tensor_copy(out=idb[:], in_=ident), reads=["con"], writes=["idb"])
        S.op("dve", lambda e: e.memset(oneb[:], 1.0), writes=["oneb"])
        S.op("dve", lambda e: e.memset(epsc[:], EPS), writes=["epsc"])
        S.op("dve", lambda e: e.tensor_scalar(out=pfa[:], in0=pf[:, 0:96], scalar1=ALPHA, scalar2=None, op0=ALU.mult), reads=["pf"], writes=["pfa"])

        wstate = {"idx": 0, "pass": 0, "off": 0, "coff": {}, "bgq": [], "bg_on": False}

        def wload(src, r0, nr, c0, ncol, slot, off, key, part=None):
            kch = nr // 128
            n = kch * ncol
            ck = (src.tensor.name, r0, nr, c0, ncol)
            wkeys = [(key, "a"), (key, "b")] if part is None else [(key, part)]
            if slot == 2:
                wkeys = wkeys + ["Cst"]
            fresh = ck not in wstate["coff"]
            if fresh:
                wstate["coff"][ck] = wstate["off"]
                wstate["off"] += n
            co = wstate["coff"][ck]
            idx = ck
            if (not fresh) and wstate.get("bg_on") and wstate["bgq"]:
                wstate["bgtick"] = wstate.get("bgtick", 0) + 1
                if wstate["bgtick"] % 2 == 0:
                    bg_cast(*wstate["bgq"].pop(0))
            if fresh:
                dst = wb[slot][:, off:off + n].rearrange("p (k c) -> p k c", k=kch)
                S.dma("pool", "d_w%d%s" % (slot, part or "a"),
                      lambda e: e.dma_start(out=dst, in_=src[r0:r0 + nr, c0:c0 + ncol].rearrange("(k p) c -> p k c", p=128)),
                      writes=wkeys)
                S.dma("sp", "d_wbk%d%s" % (slot, part or "a"), lambda e: e.dma_start(out=wcache[:, co:co + n], in_=wb[slot][:, off:off + n]),
                      reads=wkeys, writes=[("wc", idx)])
            else:
                S.dma("pool", "d_w%d%s" % (slot, part or "a"), lambda e: e.dma_start(out=wb[slot][:, off:off + n], in_=wcache[:, co:co + n]),
                      reads=[("wc", idx)], writes=wkeys)

        def bg_cast(src, r0, nr, c0, ncol):
            kch = nr // 128
            n = kch * ncol
            ck = (src.tensor.name, r0, nr, c0, ncol)
            if ck in wstate["coff"]:
                return
            wstate["coff"][ck] = wstate["off"]
            co = wstate["off"]
            wstate["off"] += n
            S.dma("pool", "d_bg", lambda e: e.dma_start(out=wcache[:, co:co + n].rearrange("p (k c) -> p k c", k=kch),
                                                       in_=src[r0:r0 + nr, c0:c0 + ncol].rearrange("(k p) c -> p k c", p=128)),
                  writes=[("wc", ck)])

        def wslot():
            i = nxt("wb", wstate.get("nslots", 2))
            return i, ("wb", i)

        def mm_group(psap, pairs, reads, pkey):
            fns = []
            n = len(pairs)
            for i, (l, r) in enumerate(pairs):
                fns.append(lambda e, l=l, r=r, i=i: e.matmul(psap, lhsT=l, rhs=r, start=(i == 0), stop=(i == n - 1)))
            rr = []
            for r in reads:
                if isinstance(r, tuple) and r[0] == "wb":
                    rr += [(r, "a"), (r, "b")]
                else:
                    rr.append(r)
            S.op("pe", fns, reads=rr, writes=[pkey])

        def run_pass(NT, sample, x_src, y_dst, first, last, prefix=False, lastprefix=False):
            ntile = NT // 128
            ckpt(0.5)

            for i in range(ntile):
                xi = 0
                S.dma("sp", "d_xtok", lambda e, xi=xi, i=i: e.dma_start(out=xtok[xi][:], in_=x_src[i * 128:(i + 1) * 128, :]), writes=[("xtok", xi)])
                for g in range(4):
                    ps, pk = acc()
                    S.op("pe", [lambda e, q=q, g=g, xi=xi, ps=ps: e.transpose(ps[:, q * 128:(q + 1) * 128], xtok[xi][:, (4 * g + q) * 128:(4 * g + q + 1) * 128], ident) for q in range(4)],
                         reads=[("xtok", xi), "con"], writes=[pk])
                    pv = ps[:].rearrange("p (q t) -> p q t", q=4)
                    tx, txk = gtmp()
                    tv = tx[:].rearrange("p (q t) -> p q t", q=4)
                    S.op("dve", lambda e, tx=tx, ps=ps: e.tensor_copy(out=tx[:], in_=ps[:]), reads=[pk], writes=[txk])
                    S.op("act", lambda e, g=g, i=i, tv=tv: e.copy(out=xT[:, 4 * g:4 * g + 4, i * 128:(i + 1) * 128], in_=tv), reads=[txk], writes=["xT"])
                    S.op("dve", lambda e, g=g, i=i, tv=tv: e.tensor_scalar(out=resT[:, 4 * g:4 * g + 4, i * 128:(i + 1) * 128], in0=tv, scalar1=ALPHA, scalar2=None, op0=ALU.mult), reads=[txk], writes=[("resT", 4 * g + q_) for q_ in range(4)])

            def stats_acc(src_ap, j, srckey):
                t, tk = gtmp()
                S.op("act", lambda e: e.activation(out=t[:, 0:NT], in_=src_ap, func=AF.Square), reads=[srckey], writes=[tk])
                if j == 0:
                    S.op("dve", lambda e: e.tensor_copy(out=ssum[:, 0:NT], in_=src_ap), reads=[srckey], writes=["ssum"])
                    S.op("dve", lambda e: e.tensor_copy(out=ssq[:, 0:NT], in_=t[:, 0:NT]), reads=[tk], writes=["ssq"])
                else:
                    S.op("dve", lambda e: e.tensor_tensor(out=ssum[:, 0:NT], in0=ssum[:, 0:NT], in1=src_ap, op=ALU.add), reads=[srckey, "ssum"], writes=["ssum"])
                    S.op("dve", lambda e: e.tensor_tensor(out=ssq[:, 0:NT], in0=ssq[:, 0:NT], in1=t[:, 0:NT], op=ALU.add), reads=[tk, "ssq"], writes=["ssq"])

            def stats_fin():
                ps, pk = aux()
                S.op("pe", lambda e: e.matmul(ps[:, 0:NT], lhsT=ones, rhs=ssum[:, 0:NT], start=True, stop=True), reads=["con", "ssum"], writes=[pk])
                S.op("act", lambda e: e.mul(out=mean[:, 0:NT], in_=ps[:, 0:NT], mul=1.0 / D), reads=[pk], writes=["mean"])
                ps2, pk2 = aux()
                S.op("pe", lambda e: e.matmul(ps2[:, 0:NT], lhsT=ones, rhs=ssq[:, 0:NT], start=True, stop=True), reads=["con", "ssq"], writes=[pk2])
                S.op("act", lambda e: e.mul(out=rstd[:, 0:NT], in_=ps2[:, 0:NT], mul=1.0 / D), reads=[pk2], writes=["rstd"])
                t, tk = gtmp()
                S.op("dve", lambda e: e.tensor_tensor(out=t[:, 0:NT], in0=mean[:, 0:NT], in1=mean[:, 0:NT], op=ALU.mult), reads=["mean"], writes=[tk])
                S.op("dve", lambda e: e.tensor_tensor(out=rstd[:, 0:NT], in0=rstd[:, 0:NT], in1=t[:, 0:NT], op=ALU.subtract), reads=[tk, "rstd"], writes=["rstd"])
                S.op("act", lambda e: e.activation(out=rstd[:, 0:NT], in_=rstd[:, 0:NT], func=AF.Ln, bias=epsc[:, 0:1]), reads=["rstd", "epsc"], writes=["rstd"])
                S.op("act", lambda e: e.activation(out=rstd[:, 0:NT], in_=rstd[:, 0:NT], func=AF.Exp, scale=-0.5), reads=["rstd"], writes=["rstd"])

            def ln_apply(pg, pb_, final, need_res=True):
                stats_fin()
                for j in range(16):
                    t, tk = gtmp()
                    S.op("dve", lambda e, j=j, t=t: e.tensor_tensor(out=t[:, 0:NT], in0=resT[:, j, 0:NT], in1=mean[:, 0:NT], op=ALU.subtract), reads=[("resT", j), "mean"], writes=[tk])
                    S.op("dve", lambda e, t=t: e.tensor_tensor(out=t[:, 0:NT], in0=t[:, 0:NT], in1=rstd[:, 0:NT], op=ALU.mult), reads=[tk, "rstd"], writes=[tk])
                    if not final:
                        S.op("act", lambda e, j=j, t=t: e.activation(out=xT[:, j, 0:NT], in_=t[:, 0:NT], func=AF.Identity, bias=pf[:, pb_ + j:pb_ + j + 1], scale=pf[:, pg + j:pg + j + 1]), reads=[tk, "pf"], writes=["xT"])
                        if need_res:
                            S.op("act", lambda e, j=j, t=t: e.activation(out=resT[:, j, 0:NT], in_=t[:, 0:NT], func=AF.Identity, bias=pfa[:, pb_ + j:pb_ + j + 1], scale=pfa[:, pg + j:pg + j + 1]), reads=[tk, "pfa"], writes=[("resT", j)])
                    else:
                        t2, tk2 = gtmp()
                        S.op("act", lambda e, j=j, t=t, t2=t2: e.activation(out=t2[:, 0:NT], in_=t[:, 0:NT], func=AF.Identity, bias=pf[:, pb_ + j:pb_ + j + 1], scale=pf[:, pg + j:pg + j + 1]), reads=[tk, "pf"], writes=[tk2])
                        ps, pk = acc()
                        S.op("pe", [lambda e, i=i, ps=ps, t2=t2: e.transpose(ps[:, i * 128:(i + 1) * 128], t2[:, i * 128:(i + 1) * 128], ident) for i in range(ntile)], reads=[tk2, "con"], writes=[pk])
                        ty, tyk = gtmp()
                        tyv = ty[:, 0:NT].rearrange("p (i c) -> p i c", i=ntile)
                        S.op("act", lambda e, tyv=tyv, ps=ps: e.copy(out=tyv, in_=ps[:, 0:NT].rearrange("p (i c) -> p i c", i=ntile)), reads=[pk], writes=[tyk])
                        S.dma("sp", "d_tmp%d" % tyk[1], lambda e, tyv=tyv, j=j: e.dma_start(out=y_dst[:, j * 128:(j + 1) * 128].rearrange("(i t) c -> t i c", t=128), in_=tyv), reads=[tyk], writes=["yout"])

            def ffn(w1, w3, w2):
                for s in range(22):
                    sl, wk = wslot()
                    wload(w1, 0, D, s * 256, 256, sl, 0, wk, "a")
                    wload(w3, 0, D, s * 256, 256, sl, 4096, wk, "b")
                    wv = wb[sl][:].rearrange("p (k c) -> p k c", k=32)
                    for fc in range(2):
                        f = 2 * s + fc
                        pa, pka = acc()
                        mm_group(pa[:, 0:NT], [(wv[:, kc, fc * 128:(fc + 1) * 128], xT[:, kc, 0:NT]) for kc in range(16)], [wk, "xT"], pka)
                        pb2, pkb = acc()
                        mm_group(pb2[:, 0:NT], [(wv[:, 16 + kc, fc * 128:(fc + 1) * 128], xT[:, kc, 0:NT]) for kc in range(16)], [wk, "xT"], pkb)
                        t, tk = gtmp()
                        S.op("act", lambda e, t=t, pa=pa: e.activation(out=t[:, 0:NT], in_=pa[:, 0:NT], func=AF.Silu), reads=[pka], writes=[tk])
                        S.op("dve", lambda e, t=t, pb2=pb2, f=f: e.tensor_tensor(out=hid[:, f, 0:NT], in0=t[:, 0:NT], in1=pb2[:, 0:NT], op=ALU.mult), reads=[tk, pkb], writes=["hid"])
                for j in range(16):
                    sl, wk = wslot()
                    wload(w2, 0, DFF, j * 128, 128, sl, 0, wk)
                    wv = wb[sl][:, 0:44 * 128].rearrange("p (k c) -> p k c", k=44)
                    pz, pkz = acc()
                    mm_group(pz[:, 0:NT], [(wv[:, f, :], hid[:, f, 0:NT]) for f in range(44)], [wk, "hid"], pkz)
                    S.op("dve", lambda e, j=j, pz=pz: e.scalar_tensor_tensor(out=resT[:, j, 0:NT], in0=pz[:, 0:NT], scalar=0.5, in1=resT[:, j, 0:NT], op0=ALU.mult, op1=ALU.add), reads=[pkz, ("resT", j)], writes=[("resT", j)])
                    stats_acc(resT[:, j, 0:NT], j, ("resT", j))

            base = 100 if sample else 0
            ckpt(base + 1)
            ffn(W["f1w1"], W["f1w3"], W["f1w2"])
            ckpt(base + 2)
            ln_apply(P_L1G, P_L1B, False, need_res=not prefix)
            ckpt(base + 3)
            S.soft_barrier()
            mixer(NT, sample, first, last, prefix, lastprefix)
            S.soft_barrier()
            if prefix:
                return
            ckpt(base + 20)
            ln_apply(P_L2G, P_L2B, False)
            ckpt(base + 21)
            ffn(W["f2w1"], W["f2w3"], W["f2w2"])
            ckpt(base + 22)
            ln_apply(P_L3G, P_L3B, True)
            ckpt(base + 23)

        def mixer(NT, sample, first, last, prefix=False, lastprefix=False):
            ntile = NT // 128
            win = W["win"]
            cU = con[:, C_UB:C_UB + 128] if sample else con[:, C_TRI:C_TRI + 128]
            cT = con[:, C_TB:C_TB + 128] if sample else ones
            cNM = con[:, C_NM4S:C_NM4S + 128] if sample else con[:, C_NM4:C_NM4 + 128]
            if first:
                S.op("dve", lambda e: e.memset(Cst[:], 0.0), writes=["Cst"])
                S.op("dve", lambda e: e.memset(nst[:], 0.0), writes=["nst"])
                S.op("dve", lambda e: e.memset(mst[:], 0.0), writes=["mst"])
                S.op("dve", lambda e: e.memset(hist[:], 0.0), writes=["hist"])
            if sample:
                S.dma("sp", "d_mst", lambda e: e.dma_start(out=mst[:], in_=smt), writes=["mst"])
                S.dma("sp", "d_nnew", lambda e: e.dma_start(out=nnew[:], in_=sn), writes=["nnew"])
                ps, pk = aux()
                S.op("pe", lambda e, ps=ps: e.transpose(ps[:, 0:128], nnew[:], ident), reads=["nnew", "con"], writes=[pk])
                S.op("dve", lambda e, ps=ps: e.tensor_copy(out=n0T[:], in_=ps[:, 0:128]), reads=[pk], writes=["n0T"])
                S.dma("sp", "d_dd", lambda e: e.dma_start(out=cso[:, 0:HIST - 8, :], in_=cc[:, 8:HIST, :]), writes=["cso_a"])

            sl, wk = wslot()
            wload(win, 0, D, O_G, 8, sl, 0, wk)
            wv = wb[sl][:, 0:128].rearrange("p (k c) -> p k c", k=16)
            for i in range(ntile):
                ps, pk = aux()
                mm_group(ps[:, 0:8], [(xT[:, kc, i * 128:(i + 1) * 128], wv[:, kc, :]) for kc in range(16)], [wk, "xT"], pk)
                S.op("dve", lambda e, i=i, ps=ps: e.tensor_tensor(out=gts[:, i, :], in0=ps[:, 0:8], in1=gb[:], op=ALU.add), reads=[pk, "gb"], writes=["gts"])
            for i in range(ntile):
                v = sv[:, i, :]
                S.op("act", lambda e, i=i, v=v: e.activation(out=v[:, 4:8], in_=gts[:, i, 4:8], func=AF.Exp, scale=-1.0), reads=["gts"], writes=["sv"])
                S.op("act", lambda e, v=v: e.activation(out=v[:, 0:4], in_=v[:, 4:8], func=AF.Ln, bias=1.0), reads=["sv"], writes=["sv"])
                ps, pk = aux()
                S.op("pe", [lambda e, v=v, ps=ps: e.matmul(ps[:, 0:4], lhsT=cU, rhs=v[:, 0:4], start=True, stop=True),
                            lambda e, v=v, ps=ps: e.matmul(ps[:, 4:8], lhsT=cT, rhs=v[:, 0:4], start=True, stop=True)], reads=["sv", "con"], writes=[pk])
                S.op("act", lambda e, v=v, ps=ps: e.mul(out=v[:, 8:16], in_=ps[:, 0:8], mul=-1.0), reads=[pk], writes=["sv"])
                S.op("dve", lambda e, i=i, v=v: e.tensor_tensor(out=v[:, 16:20], in0=gts[:, i, 0:4], in1=v[:, 8:12], op=ALU.subtract), reads=["sv", "gts"], writes=["sv"])
                for h in range(NH):
                    S.op("dve", lambda e, h=h, v=v: e.tensor_scalar(out=dg[:, h, :], in0=ident, scalar1=v[:, 16 + h:17 + h], scalar2=None, op0=ALU.mult), reads=["sv", "con"], writes=["dg"])
                pA, pkA = aux()
                S.op("pe", [lambda e, h=h, pA=pA: e.matmul(pA[:, h * 128:(h + 1) * 128], lhsT=ones, rhs=dg[:, h, :], start=True, stop=True) for h in range(NH)], reads=["dg", "con"], writes=[pkA])
                ta, tka = gtmp()
                for h in range(NH):
                    S.op("dve", lambda e, h=h, ta=ta, pA=pA: e.tensor_tensor(out=ta[:, h * 128:(h + 1) * 128], in0=pA[:, h * 128:(h + 1) * 128], in1=cNM, op=ALU.add), reads=[pkA, "con", tka], writes=[tka])
                S.op("dve", lambda e, v=v, ta=ta: e.tensor_reduce(out=v[:, 20:24], in_=ta[:].rearrange("p (h s) -> p h s", h=4), axis=AX.X, op=ALU.max), reads=[tka], writes=["sv"])
                if sample:
                    t, tk = gtmp()
                    for h in range(NH):
                        S.op("dve", lambda e, h=h, t=t, pA=pA: e.tensor_tensor(out=t[:, h * 128:(h + 1) * 128], in0=pA[:, h * 128:(h + 1) * 128], in1=con[:, C_NF4S:C_NF4S + 128], op=ALU.add), reads=[pkA, "con", tk], writes=[tk])
                    S.op("dve", lambda e, t=t, v=v: e.tensor_reduce(out=v[:, 24:28], in_=t[:].rearrange("p (h s) -> p h s", h=4), axis=AX.X, op=ALU.max), reads=[tk], writes=["sv"])
                else:
                    S.op("dve", lambda e, v=v, pA=pA: e.tensor_reduce(out=v[:, 24:28], in_=pA[:].rearrange("p (h s) -> p h s", h=4), axis=AX.X, op=ALU.max), reads=[pkA], writes=["sv"])
                S.op("dve", lambda e, v=v: e.tensor_tensor(out=v[:, 28:32], in0=v[:, 20:24], in1=mst[:], op=ALU.max), reads=["sv", "mst"], writes=["sv"])
                S.op("dve", lambda e, v=v: e.tensor_tensor(out=v[:, 32:36], in0=v[:, 24:28], in1=mst[:], op=ALU.max), reads=["sv", "mst"], writes=["sv"])
                S.op("dve", lambda e, v=v: e.tensor_tensor(out=v[:, 32:36], in0=v[:, 32:36], in1=v[:, 12:16], op=ALU.add), reads=["sv"], writes=["sv"])
                S.op("dve", lambda e, i=i, v=v: e.tensor_scalar(out=negmx[:, i, :], in0=v[:, 28:32], scalar1=-1.0, scalar2=None, op0=ALU.mult), reads=["sv"], writes=["gv"])
                S.op("dve", lambda e, v=v: e.tensor_tensor(out=v[:, 36:40], in0=mst[:], in1=v[:, 28:32], op=ALU.subtract), reads=["sv", "mst"], writes=["sv"])
                S.op("act", lambda e, i=i, v=v: e.activation(out=wint[:, i, :], in_=v[:, 36:40], func=AF.Exp), reads=["sv"], writes=["gv"])
                S.op("dve", lambda e, v=v: e.tensor_tensor(out=v[:, 40:44], in0=v[:, 12:16], in1=v[:, 32:36], op=ALU.subtract), reads=["sv"], writes=["sv"])
                S.op("dve", lambda e, v=v: e.tensor_tensor(out=v[:, 44:48], in0=v[:, 40:44], in1=v[:, 16:20], op=ALU.add), reads=["sv"], writes=["sv"])
                S.op("act", lambda e, i=i, v=v: e.activation(out=wk_[:, i, :], in_=v[:, 44:48], func=AF.Exp), reads=["sv"], writes=["gv"])
                S.op("dve", lambda e, v=v: e.tensor_tensor(out=v[:, 48:52], in0=v[:, 40:44], in1=mst[:], op=ALU.add), reads=["sv", "mst"], writes=["sv"])
                S.op("act", lambda e, i=i, v=v: e.activation(out=wc_[:, i, :], in_=v[:, 48:52], func=AF.Exp), reads=["sv"], writes=["gv"])
                S.op("dve", lambda e, v=v: e.tensor_tensor(out=v[:, 52:56], in0=v[:, 8:12], in1=v[:, 28:32], op=ALU.add), reads=["sv"], writes=["sv"])
                S.op("act", lambda e, i=i, v=v: e.activation(out=emt[:, i, :], in_=v[:, 52:56], func=AF.Exp, scale=-1.0), reads=["sv"], writes=["gv"])
                S.op("dve", lambda e, v=v: e.tensor_copy(out=mst[:], in_=v[:, 32:36]), reads=["sv"], writes=["mst"])

            ckpt(4)
            if not prefix:
                for blk in range(4):
                    sl, wk = wslot()
                    wload(win, 0, D, O_O + blk * 512, 512, sl, 0, wk)
                    wv = wb[sl][:].rearrange("p (k c) -> p k c", k=16)
                    mi = 0
                    S.dma("sp", "d_xtok", lambda e, mi=mi, blk=blk: e.dma_start(out=xtok[mi][:, 0:512], in_=mhg_d[:, blk * 512:(blk + 1) * 512]), writes=[("xtok", mi)])
                    for i in range(ntile):
                        ps, pk = acc()
                        mm_group(ps[:], [(xT[:, kc, i * 128:(i + 1) * 128], wv[:, kc, :]) for kc in range(16)], [wk, "xT"], pk)
                        t, tk = gtmp()
                        S.op("act", lambda e, t=t, ps=ps: e.activation(out=t[:], in_=ps[:], func=AF.Sigmoid), reads=[pk], writes=[tk])
                        S.op("dve", lambda e, t=t, i=i, blk=blk, mi=mi: e.tensor_tensor(out=termA[:, i, blk * 512:(blk + 1) * 512], in0=t[:], in1=xtok[mi][:, 0:512], op=ALU.mult), reads=[tk, ("xtok", mi)], writes=["termA"])
                for blk in range(4):
                    sl, wk = wslot()
                    wload(win, 0, D, O_TA + blk * 512, 512, sl, 0, wk)
                    wv = wb[sl][:].rearrange("p (k c) -> p k c", k=16)
                    for i in range(ntile):
                        ps, pk = acc()
                        mm_group(ps[:], [(xT[:, kc, i * 128:(i + 1) * 128], wv[:, kc, :]) for kc in range(16)], [wk, "xT"], pk)
                        t, tk = gtmp()
                        S.op("act", lambda e, t=t, ps=ps: e.activation(out=t[:], in_=ps[:], func=AF.Sigmoid), reads=[pk], writes=[tk])
                        S.op("dve", lambda e, t=t, i=i, blk=blk: e.tensor_tensor(out=termA[:, i, blk * 512:(blk + 1) * 512], in0=t[:], in1=termA[:, i, blk * 512:(blk + 1) * 512], op=ALU.mult), reads=[tk, "termA"], writes=["termA"])

            ckpt(5)
            for h in range(NH):
                sl, wk = wslot()
                if not prefix:
                    wload(win, 0, D, O_Q + h * 256, 256, sl, 0, wk, "a")
                wload(win, 0, D, O_K + h * 256, 256, sl, 4096, wk, "b")
                wv = wb[sl][:].rearrange("p (k c) -> p k c", k=32)
                for oc in (range(4) if not prefix else []):
                    ps, pk = acc()
                    base = 0 if oc < 2 else 16
                    mm_group(ps[:, 0:NT], [(wv[:, base + kc, (oc % 2) * 128:(oc % 2 + 1) * 128], xT[:, kc, 0:NT]) for kc in range(16)], [wk, "xT"], pk)
                    S.op("act", lambda e, oc=oc, ps=ps: e.mul(out=qkT[:, oc, 0:NT], in_=ps[:, 0:NT], mul=(1.0 if oc < 2 else 0.0625)), reads=[pk], writes=["qkT"])
                for i in range(ntile):
                    ps, pk = acc()
                    mm_group(ps[:, 0:256], [(xT[:, kc, i * 128:(i + 1) * 128], wv[:, 16 + kc, :]) for kc in range(16)], [wk, "xT"], pk)
                    S.op("act", lambda e, i=i, ps=ps: e.mul(out=ktok[:, i, :], in_=ps[:, 0:256], mul=0.0625), reads=[pk], writes=["ktok"])
                sl, wk = wslot()
                wload(win, 0, D, O_V + h * 512, 512, sl, 0, wk)
                wv = wb[sl][:].rearrange("p (k c) -> p k c", k=16)
                for i in range(ntile):
                    ps, pk = acc()
                    mm_group(ps[:], [(xT[:, kc, i * 128:(i + 1) * 128], wv[:, kc, :]) for kc in range(16)], [wk, "xT"], pk)
                    S.op("dve", lambda e, i=i, ps=ps: e.tensor_copy(out=vtok[:, i, :], in_=ps[:]), reads=[pk], writes=["vtok"])
                for i in range(ntile):
                    chunk(NT, sample, h, i, prefix)

            ckpt(6)
            if prefix:
                if lastprefix:
                    conv_branch(NT, sample, last, True)
                return
            conv_branch(NT, sample, last)
            ckpt(8)

            for s in range(4):
                sl, wk = wslot()
                wload(W["wout"], 0, D, s * 512, 512, sl, 0, wk)
                wv = wb[sl][:].rearrange("p (k c) -> p k c", k=16)
                for cc_ in range(4):
                    j = 4 * s + cc_
                    pz, pkz = acc()
                    mm_group(pz[:, 0:NT], [(wv[:, kc, cc_ * 128:(cc_ + 1) * 128], mixT[:, kc, 0:NT]) for kc in range(16)], [wk, "mixT"], pkz)
                    S.op("dve", lambda e, j=j, pz=pz: e.tensor_tensor(out=resT[:, j, 0:NT], in0=pz[:, 0:NT], in1=resT[:, j, 0:NT], op=ALU.add), reads=[pkz, ("resT", j)], writes=[("resT", j)])
                    stats_acc_g(NT, resT[:, j, 0:NT], j, ("resT", j))

            ckpt(9)
            if last and not sample:
                for h in range(NH):
                    S.dma("sp", "d_Cst", lambda e, h=h: e.dma_start(out=Cp[h].rearrange("(a p) v -> p a v", p=128), in_=Cst[:, h, :, :]), reads=["Cst"], writes=[("Cp", h)])
                ps, pk = aux()
                S.op("pe", lambda e, ps=ps: e.transpose(ps[0:8, 0:128], nst[:], ident), reads=["nst", "con"], writes=[pk])
                S.op("dve", lambda e, ps=ps: e.tensor_copy(out=stage[0:8, 0:128], in_=ps[0:8, 0:128]), reads=[pk], writes=[("xtok", 0)])
                S.dma("sp", "d_xtok", lambda e: e.dma_start(out=npo, in_=stage[0:8, 0:128]), reads=[("xtok", 0)], writes=["npo"])
                S.dma("sp", "d_mst", lambda e: e.dma_start(out=mpo, in_=mst[0:1, :]), reads=["mst"], writes=["mpo"])

        def stats_acc_g(NT, src_ap, j, srckey):
            t, tk = gtmp()
            S.op("act", lambda e: e.activation(out=t[:, 0:NT], in_=src_ap, func=AF.Square), reads=[srckey], writes=[tk])
            if j == 0:
                S.op("dve", lambda e: e.tensor_copy(out=ssum[:, 0:NT], in_=src_ap), reads=[srckey], writes=["ssum"])
                S.op("dve", lambda e: e.tensor_copy(out=ssq[:, 0:NT], in_=t[:, 0:NT]), reads=[tk], writes=["ssq"])
            else:
                S.op("dve", lambda e: e.tensor_tensor(out=ssum[:, 0:NT], in0=ssum[:, 0:NT], in1=src_ap, op=ALU.add), reads=[srckey, "ssum"], writes=["ssum"])
                S.op("dve", lambda e: e.tensor_tensor(out=ssq[:, 0:NT], in0=ssq[:, 0:NT], in1=t[:, 0:NT], op=ALU.add), reads=[tk, "ssq"], writes=["ssq"])

        def chunk(NT, sample, h, i, prefix=False):
            tsl = slice(i * 128, (i + 1) * 128)
            if prefix:
                state_update(h, i)
                return
            cNMc = con[:, C_NM4S:C_NM4S + 128] if sample else con[:, C_NM4:C_NM4 + 128]
            S.op("dve", lambda e: e.tensor_scalar(out=dg[:, 0, :], in0=ident, scalar1=sv[:, i, 16 + h:17 + h], scalar2=None, op0=ALU.mult), reads=["sv", "con"], writes=["dg"])
            pA, pkA = aux()
            S.op("pe", lambda e: e.matmul(pA[:, 0:128], lhsT=ones, rhs=dg[:, 0, :], start=True, stop=True), reads=["dg", "con"], writes=[pkA])
            S.op("dve", lambda e: e.tensor_tensor(out=dt_[:], in0=pA[:, 0:128], in1=cNMc, op=ALU.add), reads=[pkA, "con"], writes=["dt"])
            S.op("act", lambda e: e.activation(out=dt_[:], in_=dt_[:], func=AF.Exp, bias=negmx[:, i, h:h + 1]), reads=["dt", "gv"], writes=["dt"])
            pS, pkS = aux()
            mm_group(pS[:, 0:128], [(qkT[:, half, tsl], qkT[:, 2 + half, tsl]) for half in range(2)], ["qkT"], pkS)
            S.op("dve", lambda e: e.tensor_tensor(out=smf[:], in0=pS[:, 0:128], in1=dt_[:], op=ALU.mult), reads=[pkS, "dt"], writes=["smf"])
            S.op("dve", lambda e: e.tensor_reduce(out=sc1[:, 0:1], in_=smf[:], axis=AX.X, op=ALU.add), reads=["smf"], writes=["sc1"])
            S.op("act", lambda e: e.copy(out=smb[:], in_=smf[:]), reads=["smf"], writes=["smb"])
            S.op("pe", lambda e: e.transpose(pTb[:, 0:128], smb[:], idb[:]), reads=["smb", "idb"], writes=["pT"])
            S.op("act", lambda e: e.copy(out=smT[:], in_=pTb[:, 0:128]), reads=["pT"], writes=["smT"])
            gs, gk = gtmp()
            if not sample:
                S.op("dve", lambda e: e.tensor_copy(out=Cb[:], in_=Cst[:, h, :, :]), reads=["Cst"], writes=["Cb"])
                S.op("dve", lambda e: e.tensor_copy(out=nb[:, 0:2], in_=nst[:, 2 * h:2 * h + 2]), reads=["nst"], writes=["nb"])
                pG, pkG = acc()
                mm_group(pG[:], [(qkT[:, half, tsl], Cb[:, half, :]) for half in range(2)], ["qkT", "Cb"], pkG)
                pq, pkq = aux()
                mm_group(pq[:, 0:1], [(qkT[:, half, tsl], nb[:, half:half + 1]) for half in range(2)], ["qkT", "nb"], pkq)
                S.op("act", lambda e: e.activation(out=gs[:], in_=pG[:], func=AF.Identity, scale=wint[:, i, h:h + 1]), reads=[pkG, "gv"], writes=[gk])
                S.op("dve", lambda e: e.scalar_tensor_tensor(out=sc1[:, 1:2], in0=pq[:, 0:1], scalar=wint[:, i, h:h + 1], in1=sc1[:, 0:1], op0=ALU.mult, op1=ALU.add), reads=[pkq, "gv", "sc1"], writes=["sc1"])
            else:
                S.op("dve", lambda e: e.tensor_scalar(out=wm[:], in0=con[:, C_SM:C_SM + 16], scalar1=wint[:, i, h:h + 1], scalar2=None, op0=ALU.mult), reads=["con", "gv"], writes=["wm"])
                S.op("dve", lambda e: e.tensor_scalar(out=wsel[:, 0:16], in0=con[:, C_FM:C_FM + 16], scalar1=wc_[:, i, h:h + 1], scalar2=None, op0=ALU.mult), reads=["con", "gv"], writes=["wsel"])
                pW, pkW = aux()
                S.op("pe", lambda e: e.matmul(pW[:, 0:16], lhsT=ones, rhs=wsel[:, 0:16], start=True, stop=True), reads=["wsel", "con"], writes=[pkW])
                S.op("dve", lambda e: e.tensor_copy(out=wcb[:, 0:16], in_=pW[:, 0:16]), reads=[pkW], writes=["wcb"])
                S.op("dve", lambda e: e.tensor_copy(out=nb[:, 0:16], in_=n0T[:].rearrange("p (s r) -> p s r", r=8)[:, :, 2 * h]), reads=["n0T"], writes=["nb"])
                S.op("act", lambda e: e.copy(out=kw[:, 0:16], in_=n0T[:].rearrange("p (s r) -> p s r", r=8)[:, :, 2 * h + 1]), reads=["n0T"], writes=["kw"])
                pq, pkq = aux()
                mm_group(pq[:, 0:16], [(qkT[:, 0, tsl], nb[:, 0:16]), (qkT[:, 1, tsl], kw[:, 0:16])], ["qkT", "nb", "kw"], pkq)
                S.op("dve", lambda e: e.tensor_tensor(out=wsel[:, 16:32], in0=pq[:, 0:16], in1=wm[:], op=ALU.mult), reads=[pkq, "wm"], writes=["wsel2"])
                S.op("dve", lambda e: e.tensor_reduce(out=sc1[:, 2:3], in_=wsel[:, 16:32], axis=AX.X, op=ALU.add), reads=["wsel2"], writes=["sc1b"])
                S.op("dve", lambda e: e.tensor_tensor(out=sc1[:, 1:2], in0=sc1[:, 2:3], in1=sc1[:, 0:1], op=ALU.add), reads=["sc1b", "sc1"], writes=["sc1"])
                for s in range(16):
                    ci = nxt("c0f", 2)
                    S.dma("pool", "d_c0f%d" % ci, lambda e, s=s, ci=ci: e.dma_start(out=c0f[ci][:], in_=sC[s, h].rearrange("(a p) v -> p a v", p=128)), writes=[("c0f", ci)])
                    S.op("act", lambda e, ci=ci: e.copy(out=Cb[:], in_=c0f[ci][:]), reads=[("c0f", ci)], writes=["Cb"])
                    pG, pkG = acc()
                    mm_group(pG[:], [(qkT[:, half, tsl], Cb[:, half, :]) for half in range(2)], ["qkT", "Cb"], pkG)
                    if s == 0:
                        S.op("dve", lambda e, pG=pG, s=s: e.tensor_scalar(out=gs[:], in0=pG[:], scalar1=wm[:, s:s + 1], scalar2=None, op0=ALU.mult), reads=[pkG, "wm"], writes=[gk])
                    else:
                        S.op("dve", lambda e, pG=pG, s=s: e.scalar_tensor_tensor(out=gs[:], in0=pG[:], scalar=wm[:, s:s + 1], in1=gs[:], op0=ALU.mult, op1=ALU.add), reads=[pkG, "wm", gk], writes=[gk])
                    S.op("dve", lambda e, s=s: e.tensor_scalar(out=wks[:, 0:1], in0=wk_[:, i, h:h + 1], scalar1=con[:, C_SM + s:C_SM + s + 1], scalar2=None, op0=ALU.mult), reads=["gv", "con"], writes=["smf1"])
                    S.op("dve", lambda e: e.tensor_scalar(out=kw[:], in0=ktok[:, i, :], scalar1=wks[:, 0:1], scalar2=None, op0=ALU.mult), reads=["ktok", "smf1"], writes=["kw"])
                    for half in range(2):
                        pC, pkC = acc()
                        mm_group(pC[:], [(kw[:, half * 128:(half + 1) * 128], vtok[:, i, :])], ["kw", "vtok"], pkC)
                        cn = nxt("cnew", 2)
                        cnk = ("cnew", 0) if cn == 0 else "yT"
                        S.op("dve", lambda e, half=half, pC=pC, cn=cn, ci=ci, s=s: e.scalar_tensor_tensor(out=cnew[cn][:], in0=c0f[ci][:, half, :], scalar=wcb[:, s:s + 1], in1=pC[:], op0=ALU.mult, op1=ALU.add), reads=[("c0f", ci), "wcb", pkC], writes=[cnk])
                        S.dma("sp", "d_cnew%d" % cn, lambda e, half=half, cn=cn, s=s: e.dma_start(out=Cs[s, h, half * 128:(half + 1) * 128, :], in_=cnew[cn][:]), reads=[cnk], writes=[("Cs", s, h, half)])
                        pn, pkn = aux()
                        mm_group(pn[:, 0:1], [(kw[:, half * 128:(half + 1) * 128], oneb[:, 0:1])], ["kw", "oneb"], pkn)
                        col = s * 8 + 2 * h + half
                        S.op("dve", lambda e, pn=pn, col=col, s=s: e.scalar_tensor_tensor(out=nnew[:, col:col + 1], in0=n0T[:, col:col + 1], scalar=wcb[:, s:s + 1], in1=pn[:, 0:1], op0=ALU.mult, op1=ALU.add), reads=["n0T", "wcb", pkn], writes=["nnew"])
            pN, pkN = acc()
            mm_group(pN[:], [(smT[:], vtok[:, i, :])], ["smT", "vtok"], pkN)
            hu, hk = gtmp()
            S.op("dve", lambda e: e.tensor_tensor(out=hu[:], in0=pN[:], in1=gs[:], op=ALU.add), reads=[pkN, gk], writes=[hk])
            S.op("dve", lambda e: e.tensor_scalar(out=sc1[:, 9:10], in0=sc1[:, 1:2], scalar1=-1.0, scalar2=None, op0=ALU.mult), reads=["sc1"], writes=["sc1n"])
            S.op("dve", lambda e: e.tensor_tensor(out=sc1[:, 9:10], in0=sc1[:, 9:10], in1=sc1[:, 1:2], op=ALU.max), reads=["sc1", "sc1n"], writes=["sc1n"])
            S.op("dve", lambda e: e.tensor_tensor(out=sc1[:, 3:4], in0=sc1[:, 9:10], in1=emt[:, i, h:h + 1], op=ALU.max), reads=["sc1n", "gv"], writes=["sc1c"])
            S.op("dve", lambda e: e.bn_stats(out=bnst[:], in_=hu[:]), reads=[hk], writes=["bnst"])
            S.op("dve", lambda e: e.bn_aggr(out=sc1[:, 4:6], in_=bnst[:]), reads=["bnst"], writes=["sc1d"])
            S.op("dve", lambda e: e.tensor_scalar(out=sc1[:, 6:7], in0=sc1[:, 3:4], scalar1=sc1[:, 3:4], scalar2=EPS, op0=ALU.mult, op1=ALU.mult), reads=["sc1c"], writes=["sc1e"])
            S.op("act", lambda e: e.activation(out=sc1[:, 7:8], in_=sc1[:, 5:6], func=AF.Ln, bias=sc1[:, 6:7]), reads=["sc1d", "sc1e"], writes=["sc1f"])
            S.op("act", lambda e: e.activation(out=sc1[:, 7:8], in_=sc1[:, 7:8], func=AF.Exp, scale=-0.5), reads=["sc1f"], writes=["sc1f"])
            S.op("dve", lambda e: e.tensor_scalar(out=hu[:], in0=hu[:], scalar1=sc1[:, 4:5], scalar2=sc1[:, 7:8], op0=ALU.subtract, op1=ALU.mult), reads=[hk, "sc1d", "sc1f"], writes=[hk])
            S.op("dve", lambda e: e.tensor_tensor(out=termA[:, i, h * 512:(h + 1) * 512], in0=hu[:], in1=termA[:, i, h * 512:(h + 1) * 512], op=ALU.mult), reads=[hk, "termA"], writes=["termA"])
            if not sample:
                state_update(h, i)

        def state_update(h, i):
            S.op("dve", lambda e: e.tensor_scalar(out=kw[:], in0=ktok[:, i, :], scalar1=wk_[:, i, h:h + 1], scalar2=None, op0=ALU.mult), reads=["ktok", "gv"], writes=["kw"])
            for half in range(2):
                pC, pkC = acc()
                mm_group(pC[:], [(kw[:, half * 128:(half + 1) * 128], vtok[:, i, :])], ["kw", "vtok"], pkC)
                S.op("dve", lambda e, half=half, pC=pC: e.scalar_tensor_tensor(out=Cst[:, h, half, :], in0=Cst[:, h, half, :], scalar=wc_[:, i, h:h + 1], in1=pC[:], op0=ALU.mult, op1=ALU.add), reads=["Cst", "gv", pkC], writes=["Cst"])
                pn, pkn = aux()
                mm_group(pn[:, 0:1], [(kw[:, half * 128:(half + 1) * 128], oneb[:, 0:1])], ["kw", "oneb"], pkn)
                S.op("dve", lambda e, half=half, pn=pn: e.scalar_tensor_tensor(out=nst[:, 2 * h + half:2 * h + half + 1], in0=nst[:, 2 * h + half:2 * h + half + 1], scalar=wc_[:, i, h:h + 1], in1=pn[:, 0:1], op0=ALU.mult, op1=ALU.add), reads=["nst", "gv", pkn], writes=["nst"])


        def conv_branch(NT, sample, last, prefix=False):
            ntile = NT // 128
            win = W["win"]
            for s in range(8):
                sl, wk = wslot()
                wload(win, 0, D, O_GA + s * 256, 256, sl, 0, wk, "a")
                wload(win, 0, D, O_GB + s * 256, 256, sl, 4096, wk, "b")
                wv = wb[sl][:].rearrange("p (k c) -> p k c", k=32)
                for cc_ in range(2):
                    j = 2 * s + cc_
                    n0 = NT - 128 if prefix else 0
                    pa, pka = acc()
                    mm_group(pa[:, n0:NT], [(wv[:, kc, cc_ * 128:(cc_ + 1) * 128], xT[:, kc, n0:NT]) for kc in range(16)], [wk, "xT"], pka)
                    pb2, pkb = acc()
                    mm_group(pb2[:, n0:NT], [(wv[:, 16 + kc, cc_ * 128:(cc_ + 1) * 128], xT[:, kc, n0:NT]) for kc in range(16)], [wk, "xT"], pkb)
                    t, tk = gtmp()
                    S.op("act", lambda e, t=t, pb2=pb2, n0=n0: e.activation(out=t[:, n0:NT], in_=pb2[:, n0:NT], func=AF.Sigmoid), reads=[pkb], writes=[tk])
                    if prefix:
                        up = upb[j % 2]; uk = ("upb", j % 2)
                        S.op("dve", lambda e, up=up, t=t, pa=pa: e.tensor_tensor(out=up[:, HIST + 384:HIST + 512], in0=pa[:, 384:512], in1=t[:, 384:512], op=ALU.mult), reads=[pka, tk, uk], writes=[uk])
                        S.op("act", lambda e, up=up, j=j: e.copy(out=hist[:, j, :], in_=up[:, 512:512 + HIST]), reads=[uk], writes=["hist"])
                        continue
                    cwc = P_CWT + j * CW
                    S.op("dve", lambda e, cwc=cwc: e.tensor_tensor(out=cdv, in0=idb[:].unsqueeze(1).broadcast_to([128, CW, 128]), in1=pf[:, cwc:cwc + CW].unsqueeze(2).broadcast_to([128, CW, 128]), op=ALU.mult), reads=["idb", "pf"], writes=["dg"])
                    py, pky = acc()
                    if not sample:
                        up = upb[j % 2]; uk = ("upb", j % 2)
                        S.op("act", lambda e, up=up, j=j: e.copy(out=up[:, 0:HIST], in_=hist[:, j, :]), reads=["hist"], writes=[uk])
                        S.op("dve", lambda e, up=up, t=t, pa=pa: e.tensor_tensor(out=up[:, HIST:HIST + 512], in0=pa[:, 0:512], in1=t[:, 0:512], op=ALU.mult), reads=[pka, tk, uk], writes=[uk])
                        S.op("act", lambda e, up=up, j=j: e.copy(out=hist[:, j, :], in_=up[:, 512:512 + HIST]), reads=[uk], writes=["hist"])
                        mm_group(py[:, 0:512], [(cdv[:, k, :], up[:, k:k + 512]) for k in range(CW)], ["dg", uk], pky)
                        if last:
                            S.op("dve", lambda e, t=t, pa=pa: e.tensor_tensor(out=utail[:, 0:HIST], in0=pa[:, 512 - HIST:512], in1=t[:, 512 - HIST:512], op=ALU.mult), reads=[pka, tk], writes=["utail"])
                            ps, pk = aux()
                            S.op("pe", lambda e, ps=ps: e.transpose(ps[0:HIST, 0:128], utail[:, 0:HIST], ident), reads=["utail", "con"], writes=[pk])
                            S.op("act", lambda e, ps=ps, j=j: e.copy(out=stage[0:HIST, j * 128:(j + 1) * 128], in_=ps[0:HIST, 0:128]), reads=[pk], writes=[("xtok", 0)])
                    else:
                        ctk = ctk2[j % 2]; ckk = ("ctk", j % 2)
                        S.dma("sp", "d_ctk%d" % (j % 2), lambda e, j=j, ctk=ctk: e.dma_start(out=ctk[0:120, :, :], in_=cc.rearrange("(g q) r c -> (q r) g c", g=4)[:, :, j * 128:(j + 1) * 128]), writes=[ckk])
                        ps, pk = aux()
                        S.op("pe", [lambda e, g=g, ps=ps, ctk=ctk: e.transpose(ps[:, g * 120:(g + 1) * 120], ctk[0:120, g, :], ident[0:120, 0:120]) for g in range(4)], reads=[ckk, "con"], writes=[pk])
                        S.op("act", lambda e, ps=ps: e.copy(out=upS[:, :, 0:HIST], in_=ps[:, 0:480].rearrange("p (s r) -> p s r", r=HIST)), reads=[pk], writes=["upS"])
                        S.op("dve", lambda e, t=t, pa=pa: e.tensor_tensor(out=upS[:, :, HIST:HIST + 8], in0=pa[:, 0:128].rearrange("p (s r) -> p s r", r=8), in1=t[:, 0:128].rearrange("p (s r) -> p s r", r=8), op=ALU.mult), reads=[pka, tk, "upS"], writes=["upS"])
                        mm_group(py[:, 0:128], [(cdv[:, k, :], upS[:, :, k:k + 8]) for k in range(CW)], ["dg", "upS"], pky)
                        t2, tk2 = gtmp()
                        S.op("dve", lambda e, t2=t2, t=t, pa=pa: e.tensor_tensor(out=t2[:, 0:128], in0=pa[:, 0:128], in1=t[:, 0:128], op=ALU.mult), reads=[pka, tk], writes=[tk2])
                        ps2, pk2 = aux()
                        S.op("pe", lambda e, ps2=ps2, t2=t2: e.transpose(ps2[:, 0:128], t2[:, 0:128], ident), reads=[tk2, "con"], writes=[pk2])
                        S.op("act", lambda e, ps2=ps2, j=j: e.copy(out=stage[:, j * 128:(j + 1) * 128], in_=ps2[:, 0:128]), reads=[pk2], writes=[("xtok", 0)])
                    S.op("act", lambda e, py=py, j=j: e.activation(out=ybuf[:, 0:NT], in_=py[:, 0:NT], func=AF.Identity, bias=pf[:, P_CB + j:P_CB + j + 1]), reads=[pky, "pf"], writes=["yT"])
                    stats_acc_g(NT, ybuf[:, 0:NT], j, "yT")
                    S.dma("sp", "d_ybuf", lambda e, j=j: e.dma_start(out=yscr[j, :, 0:NT], in_=ybuf[:, 0:NT]), reads=["yT"], writes=[("yscr", j)])
            if prefix:
                return
            if sample:
                for s in range(16):
                    S.dma("sp", "d_xtok", lambda e, s=s: e.dma_start(out=cso[s, HIST - 8:HIST, :], in_=stage[s * 8:(s + 1) * 8, :]), reads=[("xtok", 0)], writes=[("cso_b", s)])
            elif last:
                S.dma("sp", "d_xtok", lambda e: e.dma_start(out=cpo, in_=stage[0:HIST, :]), reads=[("xtok", 0)], writes=["cpo"])
            ckpt(7)
            conv_ln_and_mix(NT, sample)

        def conv_ln_and_mix(NT, sample):
            ntile = NT // 128
            win = W["win"]
            ps, pk = aux()
            S.op("pe", lambda e, ps=ps: e.matmul(ps[:, 0:NT], lhsT=ones, rhs=ssum[:, 0:NT], start=True, stop=True), reads=["con", "ssum"], writes=[pk])
            S.op("act", lambda e, ps=ps: e.mul(out=mean[:, 0:NT], in_=ps[:, 0:NT], mul=1.0 / D), reads=[pk], writes=["mean"])
            ps2, pk2 = aux()
            S.op("pe", lambda e, ps2=ps2: e.matmul(ps2[:, 0:NT], lhsT=ones, rhs=ssq[:, 0:NT], start=True, stop=True), reads=["con", "ssq"], writes=[pk2])
            S.op("act", lambda e, ps2=ps2: e.mul(out=rstd[:, 0:NT], in_=ps2[:, 0:NT], mul=1.0 / D), reads=[pk2], writes=["rstd"])
            t, tk = gtmp()
            S.op("dve", lambda e, t=t: e.tensor_tensor(out=t[:, 0:NT], in0=mean[:, 0:NT], in1=mean[:, 0:NT], op=ALU.mult), reads=["mean"], writes=[tk])
            S.op("dve", lambda e, t=t: e.tensor_tensor(out=rstd[:, 0:NT], in0=rstd[:, 0:NT], in1=t[:, 0:NT], op=ALU.subtract), reads=[tk, "rstd"], writes=["rstd"])
            S.op("act", lambda e: e.activation(out=rstd[:, 0:NT], in_=rstd[:, 0:NT], func=AF.Ln, bias=epsc[:, 0:1]), reads=["rstd", "epsc"], writes=["rstd"])
            S.op("act", lambda e: e.activation(out=rstd[:, 0:NT], in_=rstd[:, 0:NT], func=AF.Exp, scale=-0.5), reads=["rstd"], writes=["rstd"])
            for s in range(4):
                sl, wk = wslot()
                wload(win, 0, D, O_TB + s * 512, 512, sl, 0, wk)
                wv = wb[sl][:].rearrange("p (k c) -> p k c", k=16)
                for cc_ in range(4):
                    j = 4 * s + cc_
                    pg, pkg = acc()
                    mm_group(pg[:, 0:NT], [(wv[:, kc, cc_ * 128:(cc_ + 1) * 128], xT[:, kc, 0:NT]) for kc in range(16)], [wk, "xT"], pkg)
                    t, tk = gtmp()
                    S.op("act", lambda e, t=t, pg=pg: e.activation(out=t[:, 0:NT], in_=pg[:, 0:NT], func=AF.Sigmoid), reads=[pkg], writes=[tk])
                    t2, tk2 = gtmp()
                    S.dma("sp", "d_tmp%d" % tk2[1], lambda e, j=j, t2=t2: e.dma_start(out=t2[:, 0:NT], in_=yscr[j, :, 0:NT]), reads=[("yscr", j)], writes=[tk2])
                    S.op("dve", lambda e, j=j, t2=t2: e.tensor_tensor(out=t2[:, 0:NT], in0=t2[:, 0:NT], in1=mean[:, 0:NT], op=ALU.subtract), reads=[tk2, "mean"], writes=[tk2])
                    S.op("dve", lambda e, t2=t2: e.tensor_tensor(out=t2[:, 0:NT], in0=t2[:, 0:NT], in1=rstd[:, 0:NT], op=ALU.mult), reads=[tk2, "rstd"], writes=[tk2])
                    S.op("dve", lambda e, j=j, t2=t2: e.tensor_scalar(out=t2[:, 0:NT], in0=t2[:, 0:NT], scalar1=pf[:, P_CLG + j:P_CLG + j + 1], scalar2=pf[:, P_CLB + j:P_CLB + j + 1], op0=ALU.mult, op1=ALU.add), reads=[tk2, "pf"], writes=[tk2])
                    s1, sk1 = gtmp()
                    S.op("act", lambda e, s1=s1, t2=t2: e.activation(out=s1[:, 0:NT], in_=t2[:, 0:NT], func=AF.Sigmoid), reads=[tk2], writes=[sk1])
                    S.op("dve", lambda e, s1=s1, t2=t2: e.tensor_tensor(out=t2[:, 0:NT], in0=t2[:, 0:NT], in1=s1[:, 0:NT], op=ALU.mult), reads=[sk1, tk2], writes=[tk2])
                    S.op("dve", lambda e, t=t, t2=t2: e.tensor_tensor(out=t2[:, 0:NT], in0=t2[:, 0:NT], in1=t[:, 0:NT], op=ALU.mult), reads=[tk, tk2], writes=[tk2])
                    S.op("pe", [lambda e, i=i, j=j: e.transpose(pTb[:, i * 128:(i + 1) * 128], termA[:, i, j * 128:(j + 1) * 128], idb[:]) for i in range(ntile)], reads=["termA", "idb"], writes=["pT"])
                    S.op("dve", lambda e, j=j, t2=t2: e.tensor_tensor(out=mixT[:, j, 0:NT], in0=pTb[:, 0:NT], in1=t2[:, 0:NT], op=ALU.add), reads=["pT", tk2], writes=["mixT"])

        def main_seq():
            run_pass(512, False, xpre[0:512, :], None, True, False, prefix=True)
            win_ = W["win"]
            bgq = []
            for blk in range(4):
                bgq.append((win_, 0, D, O_O + blk * 512, 512))
            for blk in range(4):
                bgq.append((win_, 0, D, O_TA + blk * 512, 512))
            for h in range(NH):
                bgq.append((win_, 0, D, O_Q + h * 256, 256))
            for s_ in range(4):
                bgq.append((win_, 0, D, O_TB + s_ * 512, 512))
            for s_ in range(4):
                bgq.append((W["wout"], 0, D, s_ * 512, 512))
            for s_ in range(22):
                bgq.append((W["f2w1"], 0, D, s_ * 256, 256))
                bgq.append((W["f2w3"], 0, D, s_ * 256, 256))
            for j in range(16):
                bgq.append((W["f2w2"], 0, DFF, j * 128, 128))
            wstate["bgq"] = bgq
            wstate["bg_on"] = True
            run_pass(512, False, xpre[512:1024, :], None, False, False, prefix=True, lastprefix=True)
            S.op("dve", lambda e: e.tensor_scalar(out=Cst[:].rearrange("p h a v -> p (h a v)"), in0=Cst[:].rearrange("p h a v -> p (h a v)"), scalar1=msk[:, 0:1], scalar2=None, op0=ALU.mult), reads=["Cst", "msk"], writes=["Cst"])
            S.op("dve", lambda e: e.tensor_scalar(out=nst[:], in0=nst[:], scalar1=msk[:, 0:1], scalar2=None, op0=ALU.mult), reads=["nst", "msk"], writes=["nst"])
            S.op("dve", lambda e: e.tensor_scalar(out=mst[:], in0=mst[:], scalar1=msk[:, 0:1], scalar2=None, op0=ALU.mult), reads=["mst", "msk"], writes=["mst"])
            S.op("dve", lambda e: e.tensor_scalar(out=hist[:].rearrange("p j r -> p (j r)"), in0=hist[:].rearrange("p j r -> p (j r)"), scalar1=msk[:, 0:1], scalar2=None, op0=ALU.mult), reads=["hist", "msk"], writes=["hist"])
            run_pass(512, False, xp[0:512, :], yp[0:512, :], False, False)
            run_pass(512, False, xp[512:1024, :], yp[512:1024, :], False, True)
            wstate["nslots"] = 3
            run_pass(128, True, xs, ys, False, True)
            ps, pk = aux()
            S.op("pe", lambda e: e.transpose(ps[:, 0:128], nnew[:], ident), reads=["nnew", "con"], writes=[pk])
            S.op("dve", lambda e: e.tensor_copy(out=stage[:, 0:128], in_=ps[:, 0:128]), reads=[pk], writes=[("xtok", 0)])
            S.dma("sp", "d_xtok", lambda e: e.dma_start(out=nso, in_=stage[:, 0:128]), reads=[("xtok", 0)], writes=["nso"])
            for s in range(16):
                S.dma("sp", "d_mst", lambda e, s=s: e.dma_start(out=mso[s:s + 1, :], in_=mst[8 * s:8 * s + 1, :]), reads=["mst"], writes=[("mso", s)])
        try:
            main_seq()
        except _Stop:
            pass
        S.barrier()
        S.emit(block)
    return nc


def _consts():
    c = np.zeros((128, NCON), np.float32)
    idx = np.arange(128)
    c[:, C_ID:C_ID + 128] = np.eye(128)
    c[:, C_ONE:C_ONE + 128] = 1.0
    tri = (idx[:, None] <= idx[None, :]).astype(np.float32)
    c[:, C_TRI:C_TRI + 128] = tri
    nm = np.where(idx[None, :] <= idx[:, None], 0.0, NEG).astype(np.float32)
    c[:, C_NM4:C_NM4 + 128] = nm
    seq = idx // 8
    same = (seq[:, None] == seq[None, :])
    c[:, C_UB:C_UB + 128] = tri * same
    c[:, C_TB:C_TB + 128] = same.astype(np.float32)
    nms = np.where(same & (idx[None, :] <= idx[:, None]), 0.0, NEG).astype(np.float32)
    c[:, C_NM4S:C_NM4S + 128] = nms
    nfs = np.where(same, 0.0, NEG).astype(np.float32)
    c[:, C_NF4S:C_NF4S + 128] = nfs
    c[:, C_SM:C_SM + 16] = (seq[:, None] == np.arange(16)[None, :])
    c[:, C_FM:C_FM + 16] = (idx[:, None] == 8 * np.arange(16)[None, :])
    return c


_NC = None


def kernel(x_prompt, x_sample, state_C, state_n, state_m, cache_conv,
           ffn1_w1, ffn1_w3, ffn1_w2, ln1_g, ln1_b, w_in, b_igate, b_fgate, mh_norm_g,
           conv_w, conv_b, conv_ln_g, conv_ln_b, w_out, ln2_g, ln2_b,
           ffn2_w1, ffn2_w3, ffn2_w2, ln3_g, ln3_b):
    global _NC
    f = lambda a: np.ascontiguousarray(np.asarray(a, dtype=np.float32))
    pfa = np.zeros((128, NPF), np.float32)
    for k, a in enumerate([ln1_g, ln1_b, ln2_g, ln2_b, ln3_g, ln3_b, conv_ln_g, conv_ln_b, conv_b]):
        pfa[:, k * 16:(k + 1) * 16] = f(a)[0].reshape(16, 128).T
    cw = f(conv_w)[0]
    pfa[:, P_CWT:P_CWT + 16 * CW] = cw.reshape(CW, 16, 128).transpose(2, 1, 0).reshape(128, 16 * CW)
    mhg = np.ascontiguousarray(np.broadcast_to(f(mh_norm_g)[0][None, :], (128, D)))
    gbb = np.ascontiguousarray(np.broadcast_to(np.concatenate([f(b_igate)[0], f(b_fgate)[0]])[None, :], (128, 8)))
    con = _consts()
    shared = {"f1w1": f(ffn1_w1)[0], "f1w3": f(ffn1_w3)[0], "f1w2": f(ffn1_w2)[0], "win": f(w_in)[0], "wout": f(w_out)[0],
              "f2w1": f(ffn2_w1)[0], "f2w3": f(ffn2_w3)[0], "f2w2": f(ffn2_w2)[0], "pf": pfa, "consts": con, "mhg": mhg, "gb": gbb}
    xpr = f(x_prompt); xsa = f(x_sample); sCf = f(state_C)[0]; snf = f(state_n)[0]; smf_ = f(state_m)[0]; ccf = f(cache_conv)[0]
    in_maps = []
    for c in range(8):
        sl = slice(16 * c, 16 * (c + 1))
        m = dict(shared)
        half = c // 4
        m["xp"] = np.ascontiguousarray(xpr[c % 4, half * 1024:(half + 1) * 1024])
        m["xpre"] = np.ascontiguousarray(xpr[c % 4, 0:1024])
        m["msk"] = np.full((128, 1), float(half), np.float32)
        m["xs"] = xsa[sl].reshape(128, D)
        m["sC"] = sCf[sl]
        m["sn"] = snf[sl].reshape(128, 128)
        m["smt"] = np.ascontiguousarray(np.repeat(smf_[sl], 8, axis=0))
        m["cc"] = ccf[sl]
        in_maps.append(m)
    if _NC is None:
        _NC = build_program()
    res = run_bass_kernel_spmd(_NC, in_maps, core_ids=list(range(8)))
    R = res.results
    y_p = np.stack([np.concatenate([R[c]["yp"], R[c + 4]["yp"]]) for c in range(4)])
    y_s = np.concatenate([R[c]["ys"].reshape(16, 8, D) for c in range(8)])
    C_p = np.stack([R[c + 4]["Cp"] for c in range(4)])[None]
    n_p = np.stack([R[c + 4]["np"].reshape(NH, DQK) for c in range(4)])[None]
    m_p = np.stack([R[c + 4]["mp"].reshape(NH) for c in range(4)])[None]
    c_p = np.stack([R[c + 4]["cp"] for c in range(4)])[None]
    C_s = np.concatenate([R[c]["Cs"] for c in range(8)])[None]
    n_s = np.concatenate([R[c]["ns"].reshape(16, NH, DQK) for c in range(8)])[None]
    m_s = np.concatenate([R[c]["ms"] for c in range(8)])[None]
    c_s = np.concatenate([R[c]["cs"] for c in range(8)])[None]
    return tuple(np.ascontiguousarray(a, dtype=np.float32) for a in (y_p, y_s, C_p, n_p, m_p, c_p, C_s, n_s, m_s, c_s))
```

```python
import numpy as np
from contextlib import ExitStack
import concourse.bass as bass
import concourse.mybir as mybir
from concourse.bass_utils import run_bass_kernel_spmd

F32 = mybir.dt.float32
BF16 = mybir.dt.bfloat16
ALU = mybir.AluOpType
AF = mybir.ActivationFunctionType
AX = mybir.AxisListType

D = 2048
DFF = 5632
DIN = 14344
NH = 4
DQK = 256
DV = 512
CW = 31
HIST = CW - 1
EPS = 1e-5
ALPHA = 2.0 ** 0.25
NEG = -30000.0
O_Q, O_K, O_V, O_O, O_G, O_GA, O_GB, O_TA, O_TB = 0, 1024, 2048, 4096, 6144, 6152, 8200, 10248, 12296
C_ID, C_ONE, C_TRI, C_NM4, C_UB, C_TB, C_NM4S, C_NF4S, C_SM, C_FM, NCON = 0, 128, 256, 384, 512, 640, 768, 896, 1024, 1040, 1056
P_L1G, P_L1B, P_L2G, P_L2B, P_L3G, P_L3B, P_CLG, P_CLB, P_CB, P_CWT, NPF = 0, 16, 32, 48, 64, 80, 96, 112, 128, 144, 640


import os


class _Stop(Exception):
    pass


def ckpt(n):
    lim = float(os.environ.get("KSTOP", "0"))
    if lim and n >= lim:
        raise _Stop()


class Sched:
    def __init__(self, nc, stack):
        self.nc = nc
        self.eng = {"pe": nc.tensor, "act": nc.scalar, "dve": nc.vector, "pool": nc.gpsimd, "sp": nc.sync}
        self.sems = {}
        self.stack = stack
        self.cnt = {}
        self.seen = {e: {} for e in self.eng}
        self.ops = {e: [] for e in self.eng}
        self.buf = {}

    def sem(self, key):
        if key not in self.sems:
            self.sems[key] = self.stack.enter_context(self.nc.semaphore("s_" + key))
            self.cnt[key] = 0
        return self.sems[key]

    def _deps(self, reads, writes):
        toks = []
        for k in reads:
            st = self.buf.get(k)
            if st and st[0] is not None:
                toks.append(st[0])
        for k in writes:
            st = self.buf.get(k)
            if st:
                if st[0] is not None:
                    toks.append(st[0])
                toks.extend(st[1])
        return toks

    def _record(self, tok, reads, writes):
        for k in reads:
            st = self.buf.setdefault(k, [None, []])
            st[1].append(tok)
            if len(st[1]) > 48:
                best = {}
                for s, v in st[1]:
                    if best.get(s, -1) < v:
                        best[s] = v
                st[1] = list(best.items())
        for k in writes:
            self.buf[k] = [tok, []]

    def _waits(self, e, toks):
        need = {}
        for s, v in toks:
            if s not in self.eng:
                v = self.cnt[s]
            if self.seen[e].get(s, 0) >= v:
                continue
            if s == e and e == "pe":
                continue
            if need.get(s, 0) < v:
                need[s] = v
        out = []
        for s, v in need.items():
            self.seen[e][s] = v
            out.append((s, v))
        return out

    def op(self, e, fns, reads=(), writes=()):
        if not isinstance(fns, (list, tuple)):
            fns = [fns]
        waits = self._waits(e, self._deps(reads, writes))
        self.sem(e)
        self.cnt[e] += 1
        tok = (e, self.cnt[e])
        self.ops[e].append((waits, list(fns), (e, 1)))
        self._record(tok, reads, writes)
        return tok

    def dma(self, q, dsem, fn, reads=(), writes=()):
        waits = self._waits(q, self._deps(reads, writes))
        self.sem(dsem)
        self.cnt[dsem] += 16
        tok = (dsem, self.cnt[dsem])
        self.ops[q].append((waits, [fn], (dsem, 16)))
        self._record(tok, reads, writes)
        return tok

    def barrier(self):
        toks = [(s, v) for s, v in self.cnt.items() if v > 0]
        for e in self.eng:
            w = self._waits(e, toks)
            if w:
                self.ops[e].append((w, [], None))
        self.buf = {}

    def soft_barrier(self):
        toks = [(s, self.cnt[s]) for s in ("pe", "act", "dve", "pool") if self.cnt.get(s, 0) > 0]
        for e in ("pe", "act", "dve"):
            w = self._waits(e, toks)
            if w:
                self.ops[e].append((w, [], None))

    def emit(self, block):
        sems = self.sems
        for e, name in (("pe", "tensor"), ("act", "scalar"), ("dve", "vector"), ("pool", "gpsimd"), ("sp", "sync")):
            def body(engine, ops=self.ops[e]):
                for waits, fns, inc in ops:
                    for s, v in waits:
                        engine.wait_ge(sems[s], v)
                    for i, fn in enumerate(fns):
                        ins = fn(engine)
                        if i == len(fns) - 1 and inc is not None:
                            ins.then_inc(sems[inc[0]], inc[1])
            getattr(block, name)(body)


def build_program():
    nc = bass.Bass("TRN2", target_bir_lowering=False)

    def din(name, shape):
        return nc.dram_tensor(name, list(shape), F32, kind="ExternalInput").ap()

    def dout(name, shape):
        return nc.dram_tensor(name, list(shape), F32, kind="ExternalOutput").ap()

    xp = din("xp", [1024, D]); xpre = din("xpre", [1024, D]); msk_d = din("msk", [128, 1]); xs = din("xs", [128, D])
    sC = din("sC", [16, NH, DQK, DV]); sn = din("sn", [128, 128]); smt = din("smt", [128, NH]); cc = din("cc", [16, HIST, D])
    W = {n: din(n, s) for n, s in [("f1w1", [D, DFF]), ("f1w3", [D, DFF]), ("f1w2", [DFF, D]), ("win", [D, DIN]), ("wout", [D, D]),
                                   ("f2w1", [D, DFF]), ("f2w3", [D, DFF]), ("f2w2", [DFF, D])]}
    pf_d = din("pf", [128, NPF]); con_d = din("consts", [128, NCON]); mhg_d = din("mhg", [128, D]); gb_d = din("gb", [128, 8])
    yp = dout("yp", [1024, D]); ys = dout("ys", [128, D])
    Cp = dout("Cp", [NH, DQK, DV]); npo = dout("np", [8, 128]); mpo = dout("mp", [1, NH]); cpo = dout("cp", [HIST, D])
    Cs = dout("Cs", [16, NH, DQK, DV]); nso = dout("ns", [128, 128]); mso = dout("ms", [16, NH]); cso = dout("cs", [16, HIST, D])

    yscr = nc.dram_tensor("yscr", [16, 128, 512], F32).ap()
    wcache = nc.dram_tensor("wcache", [128, 803072], BF16).ap()
    with ExitStack() as st:
        S = Sched(nc, st)
        uid = [0]

        def sb(shape, dt=F32):
            uid[0] += 1
            return st.enter_context(nc.sbuf_tensor("t%d" % uid[0], list(shape), dt))

        def pst(shape, dt=F32):
            uid[0] += 1
            return st.enter_context(nc.psum_tensor("p%d" % uid[0], list(shape), dt))

        xT = sb([128, 16, 512], BF16)
        resT = sb([128, 16, 512])
        ybuf = sb([128, 512])
        big = sb([128, 44 * 512], BF16)
        hid = big[:].rearrange("p (f t) -> p f t", f=44)
        termA = big[:, 0:8192].rearrange("p (i c) -> p i c", i=4)
        mixT = big[:, 8192:16384].rearrange("p (k t) -> p k t", k=16)
        qkT = big[:, 16384:18432].rearrange("p (o t) -> p o t", o=4)
        ktok = big[:, 18432:19456].rearrange("p (i c) -> p i c", i=4)
        vtok = big[:, 19456:21504].rearrange("p (i c) -> p i c", i=4)
        wb = [sb([128, 8192], BF16) for _ in range(2)]
        Cst = sb([128, NH, 2, DV]); nst = sb([128, 8]); mst = sb([128, NH])
        Cb = sb([128, 2, DV], BF16); nb = sb([128, 16], BF16)
        con = sb([128, NCON]); pf = sb([128, NPF]); gb = sb([128, 8]); msk = sb([128, 1]); pfa = sb([128, 96])
        idb = sb([128, 128], BF16); oneb = sb([128, 16], BF16); epsc = sb([128, 2])
        xtok = [sb([128, D])]
        tmp = [sb([128, 512]) for _ in range(4)]
        ssum = sb([128, 512]); ssq = sb([128, 512]); mean = sb([128, 512]); rstd = sb([128, 512])
        gts = sb([128, 4, 8]); cdiag = sb([128, CW * 128], BF16)
        dg = cdiag[:, 0:1024].bitcast(F32).rearrange("p (h s) -> p h s", h=4)
        cdv = cdiag[:].rearrange("p (k q) -> p k q", k=CW)
        sv = sb([128, 4, 64])
        negmx = sb([128, 4, 4]); wint = sb([128, 4, 4]); wk_ = sb([128, 4, 4]); wc_ = sb([128, 4, 4]); emt = sb([128, 4, 4])
        dt_ = sb([128, 128]); smf = sb([128, 128]); smb = sb([128, 128], BF16); smT = sb([128, 128], BF16)
        kw = sb([128, 256], BF16); sc1 = sb([128, 16]); bnst = sb([128, 6]); wks = sb([128, 2])
        upb = [sb([128, HIST + 512], BF16) for _ in range(2)]
        hist = sb([128, 16, HIST], BF16); utail = sb([128, 32])
        stage = xtok[0]
        c0f = [sb([128, 2, DV]) for _ in range(2)]
        n0T = sb([128, 128]); nnew = sb([128, 128]); wm = sb([128, 16]); wcb = sb([128, 64]); wsel = sb([128, 64])
        ctk = sb([128, 4, 128]); upS = sb([128, 16, HIST + 8], BF16); cnew = [sb([128, DV]), ybuf]
        accb = [pst([128, 512]) for _ in range(5)]
        auxb = [pst([128, 512]) for _ in range(2)]
        pTb = pst([128, 1024], BF16)
        ring = {"acc": 0, "aux": 0, "wb": 0, "tmp": 0, "c0f": 0, "cnew": 0, "ystg": 0, "xtok": 0}

        def nxt(kind, n):
            i = ring[kind]
            ring[kind] = (i + 1) % n
            return i

        def acc():
            i = nxt("acc", 5)
            return accb[i], ("acc", i)

        def aux():
            i = nxt("aux", 2)
            return auxb[i], ("aux", i)

        def gtmp():
            i = nxt("tmp", 4)
            return tmp[i], ("tmp", i)

        block = st.enter_context(nc.Block())
        ident = con[:, C_ID:C_ID + 128]
        ones = con[:, C_ONE:C_ONE + 128]

        S.dma("sp", "c_con", lambda e: e.dma_start(out=con[:], in_=con_d), writes=["con"])
        S.dma("sp", "c_pf", lambda e: e.dma_start(out=pf[:], in_=pf_d), writes=["pf"])
        S.dma("sp", "c_gb", lambda e: e.dma_start(out=gb[:], in_=gb_d), writes=["gb"])
        S.dma("sp", "c_msk", lambda e: e.dma_start(out=msk[:], in_=msk_d), writes=["msk"])
        S.op("dve", lambda e: e.tensor_copy(out=idb[:], in_=ident), reads=["con"], writes=["idb"])
        S.op("dve", lambda e: e.memset(oneb[:], 1.0), writes=["oneb"])
        S.op("dve", lambda e: e.memset(epsc[:], EPS), writes=["epsc"])
        S.op("dve", lambda e: e.tensor_scalar(out=pfa[:], in0=pf[:, 0:96], scalar1=ALPHA, scalar2=None, op0=ALU.mult), reads=["pf"], writes=["pfa"])

        wstate = {"idx": 0, "pass": 0, "off": 0, "coff": {}, "bgq": [], "bg_on": False}

        def wload(src, r0, nr, c0, ncol, slot, off, key, part=None):
            kch = nr // 128
            n = kch * ncol
            ck = (src.tensor.name, r0, nr, c0, ncol)
            wkeys = [(key, "a"), (key, "b")] if part is None else [(key, part)]
            fresh = ck not in wstate["coff"]
            if fresh:
                wstate["coff"][ck] = wstate["off"]
                wstate["off"] += n
            co = wstate["coff"][ck]
            idx = ck
            if (not fresh) and wstate.get("bg_on") and wstate["bgq"]:
                wstate["bgtick"] = wstate.get("bgtick", 0) + 1
                if wstate["bgtick"] % wstate.get("bgdiv", 2) == 0:
                    bg_cast(*wstate["bgq"].pop(0))
            if fresh:
                dst = wb[slot][:, off:off + n].rearrange("p (k c) -> p k c", k=kch)
                S.dma("pool", "d_w%d%s" % (slot, part or "a"),
                      lambda e: e.dma_start(out=dst, in_=src[r0:r0 + nr, c0:c0 + ncol].rearrange("(k p) c -> p k c", p=128)),
                      writes=wkeys)
                S.dma("sp", "d_wbk%d%s" % (slot, part or "a"), lambda e: e.dma_start(out=wcache[:, co:co + n], in_=wb[slot][:, off:off + n]),
                      reads=wkeys, writes=[("wc", idx)])
            else:
                S.dma("pool", "d_w%d%s" % (slot, part or "a"), lambda e: e.dma_start(out=wb[slot][:, off:off + n], in_=wcache[:, co:co + n]),
                      reads=[("wc", idx)], writes=wkeys)

        def bg_cast(src, r0, nr, c0, ncol):
            kch = nr // 128
            n = kch * ncol
            ck = (src.tensor.name, r0, nr, c0, ncol)
            if ck in wstate["coff"]:
                return
            wstate["coff"][ck] = wstate["off"]
            co = wstate["off"]
            wstate["off"] += n
            S.dma("pool", "d_bg", lambda e: e.dma_start(out=wcache[:, co:co + n].rearrange("p (k c) -> p k c", k=kch),
                                                       in_=src[r0:r0 + nr, c0:c0 + ncol].rearrange("(k p) c -> p k c", p=128)),
                  writes=[("wc", ck)])

        def wslot():
            i = nxt("wb", 2)
            return i, ("wb", i)

        def mm_group(psap, pairs, reads, pkey):
            fns = []
            n = len(pairs)
            for i, (l, r) in enumerate(pairs):
                fns.append(lambda e, l=l, r=r, i=i: e.matmul(psap, lhsT=l, rhs=r, start=(i == 0), stop=(i == n - 1)))
            rr = []
            for r in reads:
                if isinstance(r, tuple) and r[0] == "wb":
                    rr += [(r, "a"), (r, "b")]
                else:
                    rr.append(r)
            S.op("pe", fns, reads=rr, writes=[pkey])

        def run_pass(NT, sample, x_src, y_dst, first, last, prefix=False, lastprefix=False):
            ntile = NT // 128
            ckpt(0.5)

            for i in range(ntile):
                xi = 0
                S.dma("sp", "d_xtok", lambda e, xi=xi, i=i: e.dma_start(out=xtok[xi][:], in_=x_src[i * 128:(i + 1) * 128, :]), writes=[("xtok", xi)])
                for g in range(4):
                    ps, pk = acc()
                    S.op("pe", [lambda e, q=q, g=g, xi=xi, ps=ps: e.transpose(ps[:, q * 128:(q + 1) * 128], xtok[xi][:, (4 * g + q) * 128:(4 * g + q + 1) * 128], ident) for q in range(4)],
                         reads=[("xtok", xi), "con"], writes=[pk])
                    pv = ps[:].rearrange("p (q t) -> p q t", q=4)
                    tx, txk = gtmp()
                    tv = tx[:].rearrange("p (q t) -> p q t", q=4)
                    S.op("dve", lambda e, tx=tx, ps=ps: e.tensor_copy(out=tx[:], in_=ps[:]), reads=[pk], writes=[txk])
                    S.op("act", lambda e, g=g, i=i, tv=tv: e.copy(out=xT[:, 4 * g:4 * g + 4, i * 128:(i + 1) * 128], in_=tv), reads=[txk], writes=["xT"])
                    S.op("dve", lambda e, g=g, i=i, tv=tv: e.tensor_scalar(out=resT[:, 4 * g:4 * g + 4, i * 128:(i + 1) * 128], in0=tv, scalar1=ALPHA, scalar2=None, op0=ALU.mult), reads=[txk], writes=[("resT", 4 * g + q_) for q_ in range(4)])

            def stats_acc(src_ap, j, srckey):
                t, tk = gtmp()
                S.op("act", lambda e: e.activation(out=t[:, 0:NT], in_=src_ap, func=AF.Square), reads=[srckey], writes=[tk])
                if j == 0:
                    S.op("dve", lambda e: e.tensor_copy(out=ssum[:, 0:NT], in_=src_ap), reads=[srckey], writes=["ssum"])
                    S.op("dve", lambda e: e.tensor_copy(out=ssq[:, 0:NT], in_=t[:, 0:NT]), reads=[tk], writes=["ssq"])
                else:
                    S.op("dve", lambda e: e.tensor_tensor(out=ssum[:, 0:NT], in0=ssum[:, 0:NT], in1=src_ap, op=ALU.add), reads=[srckey, "ssum"], writes=["ssum"])
                    S.op("dve", lambda e: e.tensor_tensor(out=ssq[:, 0:NT], in0=ssq[:, 0:NT], in1=t[:, 0:NT], op=ALU.add), reads=[tk, "ssq"], writes=["ssq"])

            def stats_fin():
                ps, pk = aux()
                S.op("pe", lambda e: e.matmul(ps[:, 0:NT], lhsT=ones, rhs=ssum[:, 0:NT], start=True, stop=True), reads=["con", "ssum"], writes=[pk])
                S.op("act", lambda e: e.mul(out=mean[:, 0:NT], in_=ps[:, 0:NT], mul=1.0 / D), reads=[pk], writes=["mean"])
                ps2, pk2 = aux()
                S.op("pe", lambda e: e.matmul(ps2[:, 0:NT], lhsT=ones, rhs=ssq[:, 0:NT], start=True, stop=True), reads=["con", "ssq"], writes=[pk2])
                S.op("act", lambda e: e.mul(out=rstd[:, 0:NT], in_=ps2[:, 0:NT], mul=1.0 / D), reads=[pk2], writes=["rstd"])
                t, tk = gtmp()
                S.op("dve", lambda e: e.tensor_tensor(out=t[:, 0:NT], in0=mean[:, 0:NT], in1=mean[:, 0:NT], op=ALU.mult), reads=["mean"], writes=[tk])
                S.op("dve", lambda e: e.tensor_tensor(out=rstd[:, 0:NT], in0=rstd[:, 0:NT], in1=t[:, 0:NT], op=ALU.subtract), reads=[tk, "rstd"], writes=["rstd"])
                S.op("act", lambda e: e.activation(out=rstd[:, 0:NT], in_=rstd[:, 0:NT], func=AF.Ln, bias=epsc[:, 0:1]), reads=["rstd", "epsc"], writes=["rstd"])
                S.op("act", lambda e: e.activation(out=rstd[:, 0:NT], in_=rstd[:, 0:NT], func=AF.Exp, scale=-0.5), reads=["rstd"], writes=["rstd"])

            def ln_apply(pg, pb_, final, need_res=True):
                stats_fin()
                for j in range(16):
                    t, tk = gtmp()
                    S.op("dve", lambda e, j=j, t=t: e.tensor_tensor(out=t[:, 0:NT], in0=resT[:, j, 0:NT], in1=mean[:, 0:NT], op=ALU.subtract), reads=[("resT", j), "mean"], writes=[tk])
                    S.op("dve", lambda e, t=t: e.tensor_tensor(out=t[:, 0:NT], in0=t[:, 0:NT], in1=rstd[:, 0:NT], op=ALU.mult), reads=[tk, "rstd"], writes=[tk])
                    if not final:
                        S.op("act", lambda e, j=j, t=t: e.activation(out=xT[:, j, 0:NT], in_=t[:, 0:NT], func=AF.Identity, bias=pf[:, pb_ + j:pb_ + j + 1], scale=pf[:, pg + j:pg + j + 1]), reads=[tk, "pf"], writes=["xT"])
                        if need_res:
                            S.op("act", lambda e, j=j, t=t: e.activation(out=resT[:, j, 0:NT], in_=t[:, 0:NT], func=AF.Identity, bias=pfa[:, pb_ + j:pb_ + j + 1], scale=pfa[:, pg + j:pg + j + 1]), reads=[tk, "pfa"], writes=[("resT", j)])
                    else:
                        t2, tk2 = gtmp()
                        S.op("act", lambda e, j=j, t=t, t2=t2: e.activation(out=t2[:, 0:NT], in_=t[:, 0:NT], func=AF.Identity, bias=pf[:, pb_ + j:pb_ + j + 1], scale=pf[:, pg + j:pg + j + 1]), reads=[tk, "pf"], writes=[tk2])
                        ps, pk = acc()
                        S.op("pe", [lambda e, i=i, ps=ps, t2=t2: e.transpose(ps[:, i * 128:(i + 1) * 128], t2[:, i * 128:(i + 1) * 128], ident) for i in range(ntile)], reads=[tk2, "con"], writes=[pk])
                        ty, tyk = gtmp()
                        tyv = ty[:, 0:NT].rearrange("p (i c) -> p i c", i=ntile)
                        S.op("act", lambda e, tyv=tyv, ps=ps: e.copy(out=tyv, in_=ps[:, 0:NT].rearrange("p (i c) -> p i c", i=ntile)), reads=[pk], writes=[tyk])
                        S.dma("sp", "d_tmp%d" % tyk[1], lambda e, tyv=tyv, j=j: e.dma_start(out=y_dst[:, j * 128:(j + 1) * 128].rearrange("(i t) c -> t i c", t=128), in_=tyv), reads=[tyk], writes=["yout"])

            def ffn(w1, w3, w2):
                for s in range(22):
                    sl, wk = wslot()
                    wload(w1, 0, D, s * 256, 256, sl, 0, wk, "a")
                    wload(w3, 0, D, s * 256, 256, sl, 4096, wk, "b")
                    wv = wb[sl][:].rearrange("p (k c) -> p k c", k=32)
                    for fc in range(2):
                        f = 2 * s + fc
                        pa, pka = acc()
                        mm_group(pa[:, 0:NT], [(wv[:, kc, fc * 128:(fc + 1) * 128], xT[:, kc, 0:NT]) for kc in range(16)], [wk, "xT"], pka)
                        pb2, pkb = acc()
                        mm_group(pb2[:, 0:NT], [(wv[:, 16 + kc, fc * 128:(fc + 1) * 128], xT[:, kc, 0:NT]) for kc in range(16)], [wk, "xT"], pkb)
                        t, tk = gtmp()
                        S.op("act", lambda e, t=t, pa=pa: e.activation(out=t[:, 0:NT], in_=pa[:, 0:NT], func=AF.Silu), reads=[pka], writes=[tk])
                        S.op("dve", lambda e, t=t, pb2=pb2, f=f: e.tensor_tensor(out=hid[:, f, 0:NT], in0=t[:, 0:NT], in1=pb2[:, 0:NT], op=ALU.mult), reads=[tk, pkb], writes=["hid"])
                for j in range(16):
                    sl, wk = wslot()
                    wload(w2, 0, DFF, j * 128, 128, sl, 0, wk)
                    wv = wb[sl][:, 0:44 * 128].rearrange("p (k c) -> p k c", k=44)
                    pz, pkz = acc()
                    mm_group(pz[:, 0:NT], [(wv[:, f, :], hid[:, f, 0:NT]) for f in range(44)], [wk, "hid"], pkz)
                    S.op("dve", lambda e, j=j, pz=pz: e.scalar_tensor_tensor(out=resT[:, j, 0:NT], in0=pz[:, 0:NT], scalar=0.5, in1=resT[:, j, 0:NT], op0=ALU.mult, op1=ALU.add), reads=[pkz, ("resT", j)], writes=[("resT", j)])
                    stats_acc(resT[:, j, 0:NT], j, ("resT", j))

            base = 100 if sample else 0
            ckpt(base + 1)
            ffn(W["f1w1"], W["f1w3"], W["f1w2"])
            ckpt(base + 2)
            ln_apply(P_L1G, P_L1B, False, need_res=not prefix)
            ckpt(base + 3)
            S.soft_barrier()
            mixer(NT, sample, first, last, prefix, lastprefix)
            S.soft_barrier()
            if prefix:
                return
            ckpt(base + 20)
            ln_apply(P_L2G, P_L2B, False)
            ckpt(base + 21)
            ffn(W["f2w1"], W["f2w3"], W["f2w2"])
            ckpt(base + 22)
            ln_apply(P_L3G, P_L3B, True)
            ckpt(base + 23)

        def mixer(NT, sample, first, last, prefix=False, lastprefix=False):
            ntile = NT // 128
            win = W["win"]
            cU = con[:, C_UB:C_UB + 128] if sample else con[:, C_TRI:C_TRI + 128]
            cT = con[:, C_TB:C_TB + 128] if sample else ones
            cNM = con[:, C_NM4S:C_NM4S + 128] if sample else con[:, C_NM4:C_NM4 + 128]
            if first:
                S.op("dve", lambda e: e.memset(Cst[:], 0.0), writes=["Cst"])
                S.op("dve", lambda e: e.memset(nst[:], 0.0), writes=["nst"])
                S.op("dve", lambda e: e.memset(mst[:], 0.0), writes=["mst"])
                S.op("dve", lambda e: e.memset(hist[:], 0.0), writes=["hist"])
            if sample:
                S.dma("sp", "d_mst", lambda e: e.dma_start(out=mst[:], in_=smt), writes=["mst"])
                S.dma("sp", "d_nnew", lambda e: e.dma_start(out=nnew[:], in_=sn), writes=["nnew"])
                ps, pk = aux()
                S.op("pe", lambda e, ps=ps: e.transpose(ps[:, 0:128], nnew[:], ident), reads=["nnew", "con"], writes=[pk])
                S.op("dve", lambda e, ps=ps: e.tensor_copy(out=n0T[:], in_=ps[:, 0:128]), reads=[pk], writes=["n0T"])
                S.dma("sp", "d_dd", lambda e: e.dma_start(out=cso[:, 0:HIST - 8, :], in_=cc[:, 8:HIST, :]), writes=["cso_a"])

            sl, wk = wslot()
            wload(win, 0, D, O_G, 8, sl, 0, wk)
            wv = wb[sl][:, 0:128].rearrange("p (k c) -> p k c", k=16)
            for i in range(ntile):
                ps, pk = aux()
                mm_group(ps[:, 0:8], [(xT[:, kc, i * 128:(i + 1) * 128], wv[:, kc, :]) for kc in range(16)], [wk, "xT"], pk)
                S.op("dve", lambda e, i=i, ps=ps: e.tensor_tensor(out=gts[:, i, :], in0=ps[:, 0:8], in1=gb[:], op=ALU.add), reads=[pk, "gb"], writes=["gts"])
            for i in range(ntile):
                v = sv[:, i, :]
                S.op("act", lambda e, i=i, v=v: e.activation(out=v[:, 4:8], in_=gts[:, i, 4:8], func=AF.Exp, scale=-1.0), reads=["gts"], writes=["sv"])
                S.op("act", lambda e, v=v: e.activation(out=v[:, 0:4], in_=v[:, 4:8], func=AF.Ln, bias=1.0), reads=["sv"], writes=["sv"])
                ps, pk = aux()
                S.op("pe", [lambda e, v=v, ps=ps: e.matmul(ps[:, 0:4], lhsT=cU, rhs=v[:, 0:4], start=True, stop=True),
                            lambda e, v=v, ps=ps: e.matmul(ps[:, 4:8], lhsT=cT, rhs=v[:, 0:4], start=True, stop=True)], reads=["sv", "con"], writes=[pk])
                S.op("act", lambda e, v=v, ps=ps: e.mul(out=v[:, 8:16], in_=ps[:, 0:8], mul=-1.0), reads=[pk], writes=["sv"])
                S.op("dve", lambda e, i=i, v=v: e.tensor_tensor(out=v[:, 16:20], in0=gts[:, i, 0:4], in1=v[:, 8:12], op=ALU.subtract), reads=["sv", "gts"], writes=["sv"])
                for h in range(NH):
                    S.op("dve", lambda e, h=h, v=v: e.tensor_scalar(out=dg[:, h, :], in0=ident, scalar1=v[:, 16 + h:17 + h], scalar2=None, op0=ALU.mult), reads=["sv", "con"], writes=["dg"])
                pA, pkA = aux()
                S.op("pe", [lambda e, h=h, pA=pA: e.matmul(pA[:, h * 128:(h + 1) * 128], lhsT=ones, rhs=dg[:, h, :], start=True, stop=True) for h in range(NH)], reads=["dg", "con"], writes=[pkA])
                ta, tka = gtmp()
                for h in range(NH):
                    S.op("dve", lambda e, h=h, ta=ta, pA=pA: e.tensor_tensor(out=ta[:, h * 128:(h + 1) * 128], in0=pA[:, h * 128:(h + 1) * 128], in1=cNM, op=ALU.add), reads=[pkA, "con", tka], writes=[tka])
                S.op("dve", lambda e, v=v, ta=ta: e.tensor_reduce(out=v[:, 20:24], in_=ta[:].rearrange("p (h s) -> p h s", h=4), axis=AX.X, op=ALU.max), reads=[tka], writes=["sv"])
                if sample:
                    t, tk = gtmp()
                    for h in range(NH):
                        S.op("dve", lambda e, h=h, t=t, pA=pA: e.tensor_tensor(out=t[:, h * 128:(h + 1) * 128], in0=pA[:, h * 128:(h + 1) * 128], in1=con[:, C_NF4S:C_NF4S + 128], op=ALU.add), reads=[pkA, "con", tk], writes=[tk])
                    S.op("dve", lambda e, t=t, v=v: e.tensor_reduce(out=v[:, 24:28], in_=t[:].rearrange("p (h s) -> p h s", h=4), axis=AX.X, op=ALU.max), reads=[tk], writes=["sv"])
                else:
                    S.op("dve", lambda e, v=v, pA=pA: e.tensor_reduce(out=v[:, 24:28], in_=pA[:].rearrange("p (h s) -> p h s", h=4), axis=AX.X, op=ALU.max), reads=[pkA], writes=["sv"])
                S.op("dve", lambda e, v=v: e.tensor_tensor(out=v[:, 28:32], in0=v[:, 20:24], in1=mst[:], op=ALU.max), reads=["sv", "mst"], writes=["sv"])
                S.op("dve", lambda e, v=v: e.tensor_tensor(out=v[:, 32:36], in0=v[:, 24:28], in1=mst[:], op=ALU.max), reads=["sv", "mst"], writes=["sv"])
                S.op("dve", lambda e, v=v: e.tensor_tensor(out=v[:, 32:36], in0=v[:, 32:36], in1=v[:, 12:16], op=ALU.add), reads=["sv"], writes=["sv"])
                S.op("dve", lambda e, i=i, v=v: e.tensor_scalar(out=negmx[:, i, :], in0=v[:, 28:32], scalar1=-1.0, scalar2=None, op0=ALU.mult), reads=["sv"], writes=["gv"])
                S.op("dve", lambda e, v=v: e.tensor_tensor(out=v[:, 36:40], in0=mst[:], in1=v[:, 28:32], op=ALU.subtract), reads=["sv", "mst"], writes=["sv"])
                S.op("act", lambda e, i=i, v=v: e.activation(out=wint[:, i, :], in_=v[:, 36:40], func=AF.Exp), reads=["sv"], writes=["gv"])
                S.op("dve", lambda e, v=v: e.tensor_tensor(out=v[:, 40:44], in0=v[:, 12:16], in1=v[:, 32:36], op=ALU.subtract), reads=["sv"], writes=["sv"])
                S.op("dve", lambda e, v=v: e.tensor_tensor(out=v[:, 44:48], in0=v[:, 40:44], in1=v[:, 16:20], op=ALU.add), reads=["sv"], writes=["sv"])
                S.op("act", lambda e, i=i, v=v: e.activation(out=wk_[:, i, :], in_=v[:, 44:48], func=AF.Exp), reads=["sv"], writes=["gv"])
                S.op("dve", lambda e, v=v: e.tensor_tensor(out=v[:, 48:52], in0=v[:, 40:44], in1=mst[:], op=ALU.add), reads=["sv", "mst"], writes=["sv"])
                S.op("act", lambda e, i=i, v=v: e.activation(out=wc_[:, i, :], in_=v[:, 48:52], func=AF.Exp), reads=["sv"], writes=["gv"])
                S.op("dve", lambda e, v=v: e.tensor_tensor(out=v[:, 52:56], in0=v[:, 8:12], in1=v[:, 28:32], op=ALU.add), reads=["sv"], writes=["sv"])
                S.op("act", lambda e, i=i, v=v: e.activation(out=emt[:, i, :], in_=v[:, 52:56], func=AF.Exp, scale=-1.0), reads=["sv"], writes=["gv"])
                S.op("dve", lambda e, v=v: e.tensor_copy(out=mst[:], in_=v[:, 32:36]), reads=["sv"], writes=["mst"])

            ckpt(4)
            if not prefix:
                for blk in range(4):
                    sl, wk = wslot()
                    wload(win, 0, D, O_O + blk * 512, 512, sl, 0, wk)
                    wv = wb[sl][:].rearrange("p (k c) -> p k c", k=16)
                    mi = 0
                    S.dma("sp", "d_xtok", lambda e, mi=mi, blk=blk: e.dma_start(out=xtok[mi][:, 0:512], in_=mhg_d[:, blk * 512:(blk + 1) * 512]), writes=[("xtok", mi)])
                    for i in range(ntile):
                        ps, pk = acc()
                        mm_group(ps[:], [(xT[:, kc, i * 128:(i + 1) * 128], wv[:, kc, :]) for kc in range(16)], [wk, "xT"], pk)
                        t, tk = gtmp()
                        S.op("act", lambda e, t=t, ps=ps: e.activation(out=t[:], in_=ps[:], func=AF.Sigmoid), reads=[pk], writes=[tk])
                        S.op("dve", lambda e, t=t, i=i, blk=blk, mi=mi: e.tensor_tensor(out=termA[:, i, blk * 512:(blk + 1) * 512], in0=t[:], in1=xtok[mi][:, 0:512], op=ALU.mult), reads=[tk, ("xtok", mi)], writes=["termA"])
                for blk in range(4):
                    sl, wk = wslot()
                    wload(win, 0, D, O_TA + blk * 512, 512, sl, 0, wk)
                    wv = wb[sl][:].rearrange("p (k c) -> p k c", k=16)
                    for i in range(ntile):
                        ps, pk = acc()
                        mm_group(ps[:], [(xT[:, kc, i * 128:(i + 1) * 128], wv[:, kc, :]) for kc in range(16)], [wk, "xT"], pk)
                        t, tk = gtmp()
                        S.op("act", lambda e, t=t, ps=ps: e.activation(out=t[:], in_=ps[:], func=AF.Sigmoid), reads=[pk], writes=[tk])
                        S.op("dve", lambda e, t=t, i=i, blk=blk: e.tensor_tensor(out=termA[:, i, blk * 512:(blk + 1) * 512], in0=t[:], in1=termA[:, i, blk * 512:(blk + 1) * 512], op=ALU.mult), reads=[tk, "termA"], writes=["termA"])

            ckpt(5)
            for h in range(NH):
                sl, wk = wslot()
                if not prefix:
                    wload(win, 0, D, O_Q + h * 256, 256, sl, 0, wk, "a")
                wload(win, 0, D, O_K + h * 256, 256, sl, 4096, wk, "b")
                wv = wb[sl][:].rearrange("p (k c) -> p k c", k=32)
                for oc in (range(4) if not prefix else []):
                    ps, pk = acc()
                    base = 0 if oc < 2 else 16
                    mm_group(ps[:, 0:NT], [(wv[:, base + kc, (oc % 2) * 128:(oc % 2 + 1) * 128], xT[:, kc, 0:NT]) for kc in range(16)], [wk, "xT"], pk)
                    S.op("act", lambda e, oc=oc, ps=ps: e.mul(out=qkT[:, oc, 0:NT], in_=ps[:, 0:NT], mul=(1.0 if oc < 2 else 0.0625)), reads=[pk], writes=["qkT"])
                for i in range(ntile):
                    ps, pk = acc()
                    mm_group(ps[:, 0:256], [(xT[:, kc, i * 128:(i + 1) * 128], wv[:, 16 + kc, :]) for kc in range(16)], [wk, "xT"], pk)
                    S.op("act", lambda e, i=i, ps=ps: e.mul(out=ktok[:, i, :], in_=ps[:, 0:256], mul=0.0625), reads=[pk], writes=["ktok"])
                sl, wk = wslot()
                wload(win, 0, D, O_V + h * 512, 512, sl, 0, wk)
                wv = wb[sl][:].rearrange("p (k c) -> p k c", k=16)
                for i in range(ntile):
                    ps, pk = acc()
                    mm_group(ps[:], [(xT[:, kc, i * 128:(i + 1) * 128], wv[:, kc, :]) for kc in range(16)], [wk, "xT"], pk)
                    S.op("dve", lambda e, i=i, ps=ps: e.tensor_copy(out=vtok[:, i, :], in_=ps[:]), reads=[pk], writes=["vtok"])
                for i in range(ntile):
                    chunk(NT, sample, h, i, prefix)

            ckpt(6)
            if prefix:
                if lastprefix:
                    conv_branch(NT, sample, last, True)
                return
            conv_branch(NT, sample, last)
            ckpt(8)

            for s in range(4):
                sl, wk = wslot()
                wload(W["wout"], 0, D, s * 512, 512, sl, 0, wk)
                wv = wb[sl][:].rearrange("p (k c) -> p k c", k=16)
                for cc_ in range(4):
                    j = 4 * s + cc_
                    pz, pkz = acc()
                    mm_group(pz[:, 0:NT], [(wv[:, kc, cc_ * 128:(cc_ + 1) * 128], mixT[:, kc, 0:NT]) for kc in range(16)], [wk, "mixT"], pkz)
                    S.op("dve", lambda e, j=j, pz=pz: e.tensor_tensor(out=resT[:, j, 0:NT], in0=pz[:, 0:NT], in1=resT[:, j, 0:NT], op=ALU.add), reads=[pkz, ("resT", j)], writes=[("resT", j)])
                    stats_acc_g(NT, resT[:, j, 0:NT], j, ("resT", j))

            ckpt(9)
            if last and not sample:
                for h in range(NH):
                    S.dma("sp", "d_Cst", lambda e, h=h: e.dma_start(out=Cp[h].rearrange("(a p) v -> p a v", p=128), in_=Cst[:, h, :, :]), reads=["Cst"], writes=[("Cp", h)])
                ps, pk = aux()
                S.op("pe", lambda e, ps=ps: e.transpose(ps[0:8, 0:128], nst[:], ident), reads=["nst", "con"], writes=[pk])
                S.op("dve", lambda e, ps=ps: e.tensor_copy(out=stage[0:8, 0:128], in_=ps[0:8, 0:128]), reads=[pk], writes=[("xtok", 0)])
                S.dma("sp", "d_xtok", lambda e: e.dma_start(out=npo, in_=stage[0:8, 0:128]), reads=[("xtok", 0)], writes=["npo"])
                S.dma("sp", "d_mst", lambda e: e.dma_start(out=mpo, in_=mst[0:1, :]), reads=["mst"], writes=["mpo"])

        def stats_acc_g(NT, src_ap, j, srckey):
            t, tk = gtmp()
            S.op("act", lambda e: e.activation(out=t[:, 0:NT], in_=src_ap, func=AF.Square), reads=[srckey], writes=[tk])
            if j == 0:
                S.op("dve", lambda e: e.tensor_copy(out=ssum[:, 0:NT], in_=src_ap), reads=[srckey], writes=["ssum"])
                S.op("dve", lambda e: e.tensor_copy(out=ssq[:, 0:NT], in_=t[:, 0:NT]), reads=[tk], writes=["ssq"])
            else:
                S.op("dve", lambda e: e.tensor_tensor(out=ssum[:, 0:NT], in0=ssum[:, 0:NT], in1=src_ap, op=ALU.add), reads=[srckey, "ssum"], writes=["ssum"])
                S.op("dve", lambda e: e.tensor_tensor(out=ssq[:, 0:NT], in0=ssq[:, 0:NT], in1=t[:, 0:NT], op=ALU.add), reads=[tk, "ssq"], writes=["ssq"])

        def chunk(NT, sample, h, i, prefix=False):
            tsl = slice(i * 128, (i + 1) * 128)
            if prefix:
                state_update(h, i)
                return
            cNMc = con[:, C_NM4S:C_NM4S + 128] if sample else con[:, C_NM4:C_NM4 + 128]
            S.op("dve", lambda e: e.tensor_scalar(out=dg[:, 0, :], in0=ident, scalar1=sv[:, i, 16 + h:17 + h], scalar2=None, op0=ALU.mult), reads=["sv", "con"], writes=["dg"])
            pA, pkA = aux()
            S.op("pe", lambda e: e.matmul(pA[:, 0:128], lhsT=ones, rhs=dg[:, 0, :], start=True, stop=True), reads=["dg", "con"], writes=[pkA])
            S.op("dve", lambda e: e.tensor_tensor(out=dt_[:], in0=pA[:, 0:128], in1=cNMc, op=ALU.add), reads=[pkA, "con"], writes=["dt"])
            S.op("act", lambda e: e.activation(out=dt_[:], in_=dt_[:], func=AF.Exp, bias=negmx[:, i, h:h + 1]), reads=["dt", "gv"], writes=["dt"])
            pS, pkS = aux()
            mm_group(pS[:, 0:128], [(qkT[:, half, tsl], qkT[:, 2 + half, tsl]) for half in range(2)], ["qkT"], pkS)
            S.op("dve", lambda e: e.tensor_tensor(out=smf[:], in0=pS[:, 0:128], in1=dt_[:], op=ALU.mult), reads=[pkS, "dt"], writes=["smf"])
            S.op("dve", lambda e: e.tensor_reduce(out=sc1[:, 0:1], in_=smf[:], axis=AX.X, op=ALU.add), reads=["smf"], writes=["sc1"])
            S.op("act", lambda e: e.copy(out=smb[:], in_=smf[:]), reads=["smf"], writes=["smb"])
            S.op("pe", lambda e: e.transpose(pTb[:, 0:128], smb[:], idb[:]), reads=["smb", "idb"], writes=["pT"])
            S.op("act", lambda e: e.copy(out=smT[:], in_=pTb[:, 0:128]), reads=["pT"], writes=["smT"])
            gs, gk = gtmp()
            if not sample:
                S.op("dve", lambda e: e.tensor_copy(out=Cb[:], in_=Cst[:, h, :, :]), reads=["Cst"], writes=["Cb"])
                S.op("dve", lambda e: e.tensor_copy(out=nb[:, 0:2], in_=nst[:, 2 * h:2 * h + 2]), reads=["nst"], writes=["nb"])
                pG, pkG = acc()
                mm_group(pG[:], [(qkT[:, half, tsl], Cb[:, half, :]) for half in range(2)], ["qkT", "Cb"], pkG)
                pq, pkq = aux()
                mm_group(pq[:, 0:1], [(qkT[:, half, tsl], nb[:, half:half + 1]) for half in range(2)], ["qkT", "nb"], pkq)
                S.op("act", lambda e: e.activation(out=gs[:], in_=pG[:], func=AF.Identity, scale=wint[:, i, h:h + 1]), reads=[pkG, "gv"], writes=[gk])
                S.op("dve", lambda e: e.scalar_tensor_tensor(out=sc1[:, 1:2], in0=pq[:, 0:1], scalar=wint[:, i, h:h + 1], in1=sc1[:, 0:1], op0=ALU.mult, op1=ALU.add), reads=[pkq, "gv", "sc1"], writes=["sc1"])
            else:
                S.op("dve", lambda e: e.tensor_scalar(out=wm[:], in0=con[:, C_SM:C_SM + 16], scalar1=wint[:, i, h:h + 1], scalar2=None, op0=ALU.mult), reads=["con", "gv"], writes=["wm"])
                S.op("dve", lambda e: e.tensor_scalar(out=wsel[:, 0:16], in0=con[:, C_FM:C_FM + 16], scalar1=wc_[:, i, h:h + 1], scalar2=None, op0=ALU.mult), reads=["con", "gv"], writes=["wsel"])
                pW, pkW = aux()
                S.op("pe", lambda e: e.matmul(pW[:, 0:16], lhsT=ones, rhs=wsel[:, 0:16], start=True, stop=True), reads=["wsel", "con"], writes=[pkW])
                S.op("dve", lambda e: e.tensor_copy(out=wcb[:, 0:16], in_=pW[:, 0:16]), reads=[pkW], writes=["wcb"])
                S.op("dve", lambda e: e.tensor_copy(out=nb[:, 0:16], in_=n0T[:].rearrange("p (s r) -> p s r", r=8)[:, :, 2 * h]), reads=["n0T"], writes=["nb"])
                S.op("act", lambda e: e.copy(out=kw[:, 0:16], in_=n0T[:].rearrange("p (s r) -> p s r", r=8)[:, :, 2 * h + 1]), reads=["n0T"], writes=["kw"])
                pq, pkq = aux()
                mm_group(pq[:, 0:16], [(qkT[:, 0, tsl], nb[:, 0:16]), (qkT[:, 1, tsl], kw[:, 0:16])], ["qkT", "nb", "kw"], pkq)
                S.op("dve", lambda e: e.tensor_tensor(out=wsel[:, 16:32], in0=pq[:, 0:16], in1=wm[:], op=ALU.mult), reads=[pkq, "wm"], writes=["wsel2"])
                S.op("dve", lambda e: e.tensor_reduce(out=sc1[:, 2:3], in_=wsel[:, 16:32], axis=AX.X, op=ALU.add), reads=["wsel2"], writes=["sc1b"])
                S.op("dve", lambda e: e.tensor_tensor(out=sc1[:, 1:2], in0=sc1[:, 2:3], in1=sc1[:, 0:1], op=ALU.add), reads=["sc1b", "sc1"], writes=["sc1"])
                for s in range(16):
                    ci = nxt("c0f", 2)
                    S.dma("pool", "d_c0f%d" % ci, lambda e, s=s, ci=ci: e.dma_start(out=c0f[ci][:], in_=sC[s, h].rearrange("(a p) v -> p a v", p=128)), writes=[("c0f", ci)])
                    S.op("act", lambda e, ci=ci: e.copy(out=Cb[:], in_=c0f[ci][:]), reads=[("c0f", ci)], writes=["Cb"])
                    pG, pkG = acc()
                    mm_group(pG[:], [(qkT[:, half, tsl], Cb[:, half, :]) for half in range(2)], ["qkT", "Cb"], pkG)
                    if s == 0:
                        S.op("dve", lambda e, pG=pG, s=s: e.tensor_scalar(out=gs[:], in0=pG[:], scalar1=wm[:, s:s + 1], scalar2=None, op0=ALU.mult), reads=[pkG, "wm"], writes=[gk])
                    else:
                        S.op("dve", lambda e, pG=pG, s=s: e.scalar_tensor_tensor(out=gs[:], in0=pG[:], scalar=wm[:, s:s + 1], in1=gs[:], op0=ALU.mult, op1=ALU.add), reads=[pkG, "wm", gk], writes=[gk])
                    S.op("dve", lambda e, s=s: e.tensor_scalar(out=wks[:, 0:1], in0=wk_[:, i, h:h + 1], scalar1=con[:, C_SM + s:C_SM + s + 1], scalar2=None, op0=ALU.mult), reads=["gv", "con"], writes=["smf1"])
                    S.op("dve", lambda e: e.tensor_scalar(out=kw[:], in0=ktok[:, i, :], scalar1=wks[:, 0:1], scalar2=None, op0=ALU.mult), reads=["ktok", "smf1"], writes=["kw"])
                    for half in range(2):
                        pC, pkC = acc()
                        mm_group(pC[:], [(kw[:, half * 128:(half + 1) * 128], vtok[:, i, :])], ["kw", "vtok"], pkC)
                        cn = nxt("cnew", 2)
                        cnk = ("cnew", 0) if cn == 0 else "yT"
                        S.op("dve", lambda e, half=half, pC=pC, cn=cn, ci=ci, s=s: e.scalar_tensor_tensor(out=cnew[cn][:], in0=c0f[ci][:, half, :], scalar=wcb[:, s:s + 1], in1=pC[:], op0=ALU.mult, op1=ALU.add), reads=[("c0f", ci), "wcb", pkC], writes=[cnk])
                        S.dma("sp", "d_cnew%d" % cn, lambda e, half=half, cn=cn, s=s: e.dma_start(out=Cs[s, h, half * 128:(half + 1) * 128, :], in_=cnew[cn][:]), reads=[cnk], writes=[("Cs", s, h, half)])
                        pn, pkn = aux()
                        mm_group(pn[:, 0:1], [(kw[:, half * 128:(half + 1) * 128], oneb[:, 0:1])], ["kw", "oneb"], pkn)
                        col = s * 8 + 2 * h + half
                        S.op("dve", lambda e, pn=pn, col=col, s=s: e.scalar_tensor_tensor(out=nnew[:, col:col + 1], in0=n0T[:, col:col + 1], scalar=wcb[:, s:s + 1], in1=pn[:, 0:1], op0=ALU.mult, op1=ALU.add), reads=["n0T", "wcb", pkn], writes=["nnew"])
            pN, pkN = acc()
            mm_group(pN[:], [(smT[:], vtok[:, i, :])], ["smT", "vtok"], pkN)
            hu, hk = gtmp()
            S.op("dve", lambda e: e.tensor_tensor(out=hu[:], in0=pN[:], in1=gs[:], op=ALU.add), reads=[pkN, gk], writes=[hk])
            S.op("dve", lambda e: e.tensor_scalar(out=sc1[:, 9:10], in0=sc1[:, 1:2], scalar1=-1.0, scalar2=None, op0=ALU.mult), reads=["sc1"], writes=["sc1n"])
            S.op("dve", lambda e: e.tensor_tensor(out=sc1[:, 9:10], in0=sc1[:, 9:10], in1=sc1[:, 1:2], op=ALU.max), reads=["sc1", "sc1n"], writes=["sc1n"])
            S.op("dve", lambda e: e.tensor_tensor(out=sc1[:, 3:4], in0=sc1[:, 9:10], in1=emt[:, i, h:h + 1], op=ALU.max), reads=["sc1n", "gv"], writes=["sc1c"])
            S.op("dve", lambda e: e.bn_stats(out=bnst[:], in_=hu[:]), reads=[hk], writes=["bnst"])
            S.op("dve", lambda e: e.bn_aggr(out=sc1[:, 4:6], in_=bnst[:]), reads=["bnst"], writes=["sc1d"])
            S.op("dve", lambda e: e.tensor_scalar(out=sc1[:, 6:7], in0=sc1[:, 3:4], scalar1=sc1[:, 3:4], scalar2=EPS, op0=ALU.mult, op1=ALU.mult), reads=["sc1c"], writes=["sc1e"])
            S.op("act", lambda e: e.activation(out=sc1[:, 7:8], in_=sc1[:, 5:6], func=AF.Ln, bias=sc1[:, 6:7]), reads=["sc1d", "sc1e"], writes=["sc1f"])
            S.op("act", lambda e: e.activation(out=sc1[:, 7:8], in_=sc1[:, 7:8], func=AF.Exp, scale=-0.5), reads=["sc1f"], writes=["sc1f"])
            S.op("dve", lambda e: e.tensor_scalar(out=hu[:], in0=hu[:], scalar1=sc1[:, 4:5], scalar2=sc1[:, 7:8], op0=ALU.subtract, op1=ALU.mult), reads=[hk, "sc1d", "sc1f"], writes=[hk])
            S.op("dve", lambda e: e.tensor_tensor(out=termA[:, i, h * 512:(h + 1) * 512], in0=hu[:], in1=termA[:, i, h * 512:(h + 1) * 512], op=ALU.mult), reads=[hk, "termA"], writes=["termA"])
            if not sample:
                state_update(h, i)

        def state_update(h, i):
            S.op("dve", lambda e: e.tensor_scalar(out=kw[:], in0=ktok[:, i, :], scalar1=wk_[:, i, h:h + 1], scalar2=None, op0=ALU.mult), reads=["ktok", "gv"], writes=["kw"])
            for half in range(2):
                pC, pkC = acc()
                mm_group(pC[:], [(kw[:, half * 128:(half + 1) * 128], vtok[:, i, :])], ["kw", "vtok"], pkC)
                S.op("dve", lambda e, half=half, pC=pC: e.scalar_tensor_tensor(out=Cst[:, h, half, :], in0=Cst[:, h, half, :], scalar=wc_[:, i, h:h + 1], in1=pC[:], op0=ALU.mult, op1=ALU.add), reads=["Cst", "gv", pkC], writes=["Cst"])
                pn, pkn = aux()
                mm_group(pn[:, 0:1], [(kw[:, half * 128:(half + 1) * 128], oneb[:, 0:1])], ["kw", "oneb"], pkn)
                S.op("dve", lambda e, half=half, pn=pn: e.scalar_tensor_tensor(out=nst[:, 2 * h + half:2 * h + half + 1], in0=nst[:, 2 * h + half:2 * h + half + 1], scalar=wc_[:, i, h:h + 1], in1=pn[:, 0:1], op0=ALU.mult, op1=ALU.add), reads=["nst", "gv", pkn], writes=["nst"])


        def conv_branch(NT, sample, last, prefix=False):
            ntile = NT // 128
            win = W["win"]
            for s in range(8):
                sl, wk = wslot()
                wload(win, 0, D, O_GA + s * 256, 256, sl, 0, wk, "a")
                wload(win, 0, D, O_GB + s * 256, 256, sl, 4096, wk, "b")
                wv = wb[sl][:].rearrange("p (k c) -> p k c", k=32)
                for cc_ in range(2):
                    j = 2 * s + cc_
                    n0 = NT - 128 if prefix else 0
                    pa, pka = acc()
                    mm_group(pa[:, n0:NT], [(wv[:, kc, cc_ * 128:(cc_ + 1) * 128], xT[:, kc, n0:NT]) for kc in range(16)], [wk, "xT"], pka)
                    pb2, pkb = acc()
                    mm_group(pb2[:, n0:NT], [(wv[:, 16 + kc, cc_ * 128:(cc_ + 1) * 128], xT[:, kc, n0:NT]) for kc in range(16)], [wk, "xT"], pkb)
                    t, tk = gtmp()
                    S.op("act", lambda e, t=t, pb2=pb2, n0=n0: e.activation(out=t[:, n0:NT], in_=pb2[:, n0:NT], func=AF.Sigmoid), reads=[pkb], writes=[tk])
                    if prefix:
                        up = upb[j % 2]; uk = ("upb", j % 2)
                        S.op("dve", lambda e, up=up, t=t, pa=pa: e.tensor_tensor(out=up[:, HIST + 384:HIST + 512], in0=pa[:, 384:512], in1=t[:, 384:512], op=ALU.mult), reads=[pka, tk, uk], writes=[uk])
                        S.op("act", lambda e, up=up, j=j: e.copy(out=hist[:, j, :], in_=up[:, 512:512 + HIST]), reads=[uk], writes=["hist"])
                        continue
                    cwc = P_CWT + j * CW
                    S.op("dve", lambda e, cwc=cwc: e.tensor_tensor(out=cdv, in0=idb[:].unsqueeze(1).broadcast_to([128, CW, 128]), in1=pf[:, cwc:cwc + CW].unsqueeze(2).broadcast_to([128, CW, 128]), op=ALU.mult), reads=["idb", "pf"], writes=["dg"])
                    py, pky = acc()
                    if not sample:
                        up = upb[j % 2]; uk = ("upb", j % 2)
                        S.op("act", lambda e, up=up, j=j: e.copy(out=up[:, 0:HIST], in_=hist[:, j, :]), reads=["hist"], writes=[uk])
                        S.op("dve", lambda e, up=up, t=t, pa=pa: e.tensor_tensor(out=up[:, HIST:HIST + 512], in0=pa[:, 0:512], in1=t[:, 0:512], op=ALU.mult), reads=[pka, tk, uk], writes=[uk])
                        S.op("act", lambda e, up=up, j=j: e.copy(out=hist[:, j, :], in_=up[:, 512:512 + HIST]), reads=[uk], writes=["hist"])
                        mm_group(py[:, 0:512], [(cdv[:, k, :], up[:, k:k + 512]) for k in range(CW)], ["dg", uk], pky)
                        if last:
                            S.op("dve", lambda e, t=t, pa=pa: e.tensor_tensor(out=utail[:, 0:HIST], in0=pa[:, 512 - HIST:512], in1=t[:, 512 - HIST:512], op=ALU.mult), reads=[pka, tk], writes=["utail"])
                            ps, pk = aux()
                            S.op("pe", lambda e, ps=ps: e.transpose(ps[0:HIST, 0:128], utail[:, 0:HIST], ident), reads=["utail", "con"], writes=[pk])
                            S.op("act", lambda e, ps=ps, j=j: e.copy(out=stage[0:HIST, j * 128:(j + 1) * 128], in_=ps[0:HIST, 0:128]), reads=[pk], writes=[("xtok", 0)])
                    else:
                        S.dma("sp", "d_ctk", lambda e, j=j: e.dma_start(out=ctk[0:120, :, :], in_=cc.rearrange("(g q) r c -> (q r) g c", g=4)[:, :, j * 128:(j + 1) * 128]), writes=["ctk"])
                        ps, pk = aux()
                        S.op("pe", [lambda e, g=g, ps=ps: e.transpose(ps[:, g * 120:(g + 1) * 120], ctk[0:120, g, :], ident[0:120, 0:120]) for g in range(4)], reads=["ctk", "con"], writes=[pk])
                        S.op("act", lambda e, ps=ps: e.copy(out=upS[:, :, 0:HIST], in_=ps[:, 0:480].rearrange("p (s r) -> p s r", r=HIST)), reads=[pk], writes=["upS"])
                        S.op("dve", lambda e, t=t, pa=pa: e.tensor_tensor(out=upS[:, :, HIST:HIST + 8], in0=pa[:, 0:128].rearrange("p (s r) -> p s r", r=8), in1=t[:, 0:128].rearrange("p (s r) -> p s r", r=8), op=ALU.mult), reads=[pka, tk, "upS"], writes=["upS"])
                        mm_group(py[:, 0:128], [(cdv[:, k, :], upS[:, :, k:k + 8]) for k in range(CW)], ["dg", "upS"], pky)
                        t2, tk2 = gtmp()
                        S.op("dve", lambda e, t2=t2, t=t, pa=pa: e.tensor_tensor(out=t2[:, 0:128], in0=pa[:, 0:128], in1=t[:, 0:128], op=ALU.mult), reads=[pka, tk], writes=[tk2])
                        ps2, pk2 = aux()
                        S.op("pe", lambda e, ps2=ps2, t2=t2: e.transpose(ps2[:, 0:128], t2[:, 0:128], ident), reads=[tk2, "con"], writes=[pk2])
                        S.op("act", lambda e, ps2=ps2, j=j: e.copy(out=stage[:, j * 128:(j + 1) * 128], in_=ps2[:, 0:128]), reads=[pk2], writes=[("xtok", 0)])
                    S.op("act", lambda e, py=py, j=j: e.activation(out=ybuf[:, 0:NT], in_=py[:, 0:NT], func=AF.Identity, bias=pf[:, P_CB + j:P_CB + j + 1]), reads=[pky, "pf"], writes=["yT"])
                    stats_acc_g(NT, ybuf[:, 0:NT], j, "yT")
                    S.dma("sp", "d_ybuf", lambda e, j=j: e.dma_start(out=yscr[j, :, 0:NT], in_=ybuf[:, 0:NT]), reads=["yT"], writes=[("yscr", j)])
            if prefix:
                return
            if sample:
                for s in range(16):
                    S.dma("sp", "d_xtok", lambda e, s=s: e.dma_start(out=cso[s, HIST - 8:HIST, :], in_=stage[s * 8:(s + 1) * 8, :]), reads=[("xtok", 0)], writes=[("cso_b", s)])
            elif last:
                S.dma("sp", "d_xtok", lambda e: e.dma_start(out=cpo, in_=stage[0:HIST, :]), reads=[("xtok", 0)], writes=["cpo"])
            ckpt(7)
            conv_ln_and_mix(NT, sample)

        def conv_ln_and_mix(NT, sample):
            ntile = NT // 128
            win = W["win"]
            ps, pk = aux()
            S.op("pe", lambda e, ps=ps: e.matmul(ps[:, 0:NT], lhsT=ones, rhs=ssum[:, 0:NT], start=True, stop=True), reads=["con", "ssum"], writes=[pk])
            S.op("act", lambda e, ps=ps: e.mul(out=mean[:, 0:NT], in_=ps[:, 0:NT], mul=1.0 / D), reads=[pk], writes=["mean"])
            ps2, pk2 = aux()
            S.op("pe", lambda e, ps2=ps2: e.matmul(ps2[:, 0:NT], lhsT=ones, rhs=ssq[:, 0:NT], start=True, stop=True), reads=["con", "ssq"], writes=[pk2])
            S.op("act", lambda e, ps2=ps2: e.mul(out=rstd[:, 0:NT], in_=ps2[:, 0:NT], mul=1.0 / D), reads=[pk2], writes=["rstd"])
            t, tk = gtmp()
            S.op("dve", lambda e, t=t: e.tensor_tensor(out=t[:, 0:NT], in0=mean[:, 0:NT], in1=mean[:, 0:NT], op=ALU.mult), reads=["mean"], writes=[tk])
            S.op("dve", lambda e, t=t: e.tensor_tensor(out=rstd[:, 0:NT], in0=rstd[:, 0:NT], in1=t[:, 0:NT], op=ALU.subtract), reads=[tk, "rstd"], writes=["rstd"])
            S.op("act", lambda e: e.activation(out=rstd[:, 0:NT], in_=rstd[:, 0:NT], func=AF.Ln, bias=epsc[:, 0:1]), reads=["rstd", "epsc"], writes=["rstd"])
            S.op("act", lambda e: e.activation(out=rstd[:, 0:NT], in_=rstd[:, 0:NT], func=AF.Exp, scale=-0.5), reads=["rstd"], writes=["rstd"])
            for s in range(4):
                sl, wk = wslot()
                wload(win, 0, D, O_TB + s * 512, 512, sl, 0, wk)
                wv = wb[sl][:].rearrange("p (k c) -> p k c", k=16)
                for cc_ in range(4):
                    j = 4 * s + cc_
                    pg, pkg = acc()
                    mm_group(pg[:, 0:NT], [(wv[:, kc, cc_ * 128:(cc_ + 1) * 128], xT[:, kc, 0:NT]) for kc in range(16)], [wk, "xT"], pkg)
                    t, tk = gtmp()
                    S.op("act", lambda e, t=t, pg=pg: e.activation(out=t[:, 0:NT], in_=pg[:, 0:NT], func=AF.Sigmoid), reads=[pkg], writes=[tk])
                    t2, tk2 = gtmp()
                    S.dma("sp", "d_tmp%d" % tk2[1], lambda e, j=j, t2=t2: e.dma_start(out=t2[:, 0:NT], in_=yscr[j, :, 0:NT]), reads=[("yscr", j)], writes=[tk2])
                    S.op("dve", lambda e, j=j, t2=t2: e.tensor_tensor(out=t2[:, 0:NT], in0=t2[:, 0:NT], in1=mean[:, 0:NT], op=ALU.subtract), reads=[tk2, "mean"], writes=[tk2])
                    S.op("dve", lambda e, t2=t2: e.tensor_tensor(out=t2[:, 0:NT], in0=t2[:, 0:NT], in1=rstd[:, 0:NT], op=ALU.mult), reads=[tk2, "rstd"], writes=[tk2])
                    S.op("dve", lambda e, j=j, t2=t2: e.tensor_scalar(out=t2[:, 0:NT], in0=t2[:, 0:NT], scalar1=pf[:, P_CLG + j:P_CLG + j + 1], scalar2=pf[:, P_CLB + j:P_CLB + j + 1], op0=ALU.mult, op1=ALU.add), reads=[tk2, "pf"], writes=[tk2])
                    s1, sk1 = gtmp()
                    S.op("act", lambda e, s1=s1, t2=t2: e.activation(out=s1[:, 0:NT], in_=t2[:, 0:NT], func=AF.Sigmoid), reads=[tk2], writes=[sk1])
                    S.op("dve", lambda e, s1=s1, t2=t2: e.tensor_tensor(out=t2[:, 0:NT], in0=t2[:, 0:NT], in1=s1[:, 0:NT], op=ALU.mult), reads=[sk1, tk2], writes=[tk2])
                    S.op("dve", lambda e, t=t, t2=t2: e.tensor_tensor(out=t2[:, 0:NT], in0=t2[:, 0:NT], in1=t[:, 0:NT], op=ALU.mult), reads=[tk, tk2], writes=[tk2])
                    S.op("pe", [lambda e, i=i, j=j: e.transpose(pTb[:, i * 128:(i + 1) * 128], termA[:, i, j * 128:(j + 1) * 128], idb[:]) for i in range(ntile)], reads=["termA", "idb"], writes=["pT"])
                    S.op("dve", lambda e, j=j, t2=t2: e.tensor_tensor(out=mixT[:, j, 0:NT], in0=pTb[:, 0:NT], in1=t2[:, 0:NT], op=ALU.add), reads=["pT", tk2], writes=["mixT"])

        def main_seq():
            run_pass(512, False, xpre[0:512, :], None, True, False, prefix=True)
            win_ = W["win"]
            bgq = []
            for blk in range(4):
                bgq.append((win_, 0, D, O_O + blk * 512, 512))
            for blk in range(4):
                bgq.append((win_, 0, D, O_TA + blk * 512, 512))
            for h in range(NH):
                bgq.append((win_, 0, D, O_Q + h * 256, 256))
            for s_ in range(4):
                bgq.append((win_, 0, D, O_TB + s_ * 512, 512))
            for s_ in range(4):
                bgq.append((W["wout"], 0, D, s_ * 512, 512))
            for s_ in range(22):
                bgq.append((W["f2w1"], 0, D, s_ * 256, 256))
                bgq.append((W["f2w3"], 0, D, s_ * 256, 256))
            for j in range(16):
                bgq.append((W["f2w2"], 0, DFF, j * 128, 128))
            wstate["bgq"] = bgq
            wstate["bg_on"] = True
            wstate["bgdiv"] = 3
            run_pass(512, False, xpre[512:1024, :], None, False, False, prefix=True, lastprefix=True)
            wstate["bgdiv"] = 2
            S.op("dve", lambda e: e.tensor_scalar(out=Cst[:].rearrange("p h a v -> p (h a v)"), in0=Cst[:].rearrange("p h a v -> p (h a v)"), scalar1=msk[:, 0:1], scalar2=None, op0=ALU.mult), reads=["Cst", "msk"], writes=["Cst"])
            S.op("dve", lambda e: e.tensor_scalar(out=nst[:], in0=nst[:], scalar1=msk[:, 0:1], scalar2=None, op0=ALU.mult), reads=["nst", "msk"], writes=["nst"])
            S.op("dve", lambda e: e.tensor_scalar(out=mst[:], in0=mst[:], scalar1=msk[:, 0:1], scalar2=None, op0=ALU.mult), reads=["mst", "msk"], writes=["mst"])
            S.op("dve", lambda e: e.tensor_scalar(out=hist[:].rearrange("p j r -> p (j r)"), in0=hist[:].rearrange("p j r -> p (j r)"), scalar1=msk[:, 0:1], scalar2=None, op0=ALU.mult), reads=["hist", "msk"], writes=["hist"])
            run_pass(512, False, xp[0:512, :], yp[0:512, :], False, False)
            run_pass(512, False, xp[512:1024, :], yp[512:1024, :], False, True)
            run_pass(128, True, xs, ys, False, True)
            ps, pk = aux()
            S.op("pe", lambda e: e.transpose(ps[:, 0:128], nnew[:], ident), reads=["nnew", "con"], writes=[pk])
            S.op("dve", lambda e: e.tensor_copy(out=stage[:, 0:128], in_=ps[:, 0:128]), reads=[pk], writes=[("xtok", 0)])
            S.dma("sp", "d_xtok", lambda e: e.dma_start(out=nso, in_=stage[:, 0:128]), reads=[("xtok", 0)], writes=["nso"])
            for s in range(16):
                S.dma("sp", "d_mst", lambda e, s=s: e.dma_start(out=mso[s:s + 1, :], in_=mst[8 * s:8 * s + 1, :]), reads=["mst"], writes=[("mso", s)])
        try:
            main_seq()
        except _Stop:
            pass
        S.barrier()
        S.emit(block)
    return nc


def _consts():
    c = np.zeros((128, NCON), np.float32)
    idx = np.arange(128)
    c[:, C_ID:C_ID + 128] = np.eye(128)
    c[:, C_ONE:C_ONE + 128] = 1.0
    tri = (idx[:, None] <= idx[None, :]).astype(np.float32)
    c[:, C_TRI:C_TRI + 128] = tri
    nm = np.where(idx[None, :] <= idx[:, None], 0.0, NEG).astype(np.float32)
    c[:, C_NM4:C_NM4 + 128] = nm
    seq = idx // 8
    same = (seq[:, None] == seq[None, :])
    c[:, C_UB:C_UB + 128] = tri * same
    c[:, C_TB:C_TB + 128] = same.astype(np.float32)
    nms = np.where(same & (idx[None, :] <= idx[:, None]), 0.0, NEG).astype(np.float32)
    c[:, C_NM4S:C_NM4S + 128] = nms
    nfs = np.where(same, 0.0, NEG).astype(np.float32)
    c[:, C_NF4S:C_NF4S + 128] = nfs
    c[:, C_SM:C_SM + 16] = (seq[:, None] == np.arange(16)[None, :])
    c[:, C_FM:C_FM + 16] = (idx[:, None] == 8 * np.arange(16)[None, :])
    return c


_NC = None


def kernel(x_prompt, x_sample, state_C, state_n, state_m, cache_conv,
           ffn1_w1, ffn1_w3, ffn1_w2, ln1_g, ln1_b, w_in, b_igate, b_fgate, mh_norm_g,
           conv_w, conv_b, conv_ln_g, conv_ln_b, w_out, ln2_g, ln2_b,
           ffn2_w1, ffn2_w3, ffn2_w2, ln3_g, ln3_b):
    global _NC
    f = lambda a: np.ascontiguousarray(np.asarray(a, dtype=np.float32))
    pfa = np.zeros((128, NPF), np.float32)
    for k, a in enumerate([ln1_g, ln1_b, ln2_g, ln2_b, ln3_g, ln3_b, conv_ln_g, conv_ln_b, conv_b]):
        pfa[:, k * 16:(k + 1) * 16] = f(a)[0].reshape(16, 128).T
    cw = f(conv_w)[0]
    pfa[:, P_CWT:P_CWT + 16 * CW] = cw.reshape(CW, 16, 128).transpose(2, 1, 0).reshape(128, 16 * CW)
    mhg = np.ascontiguousarray(np.broadcast_to(f(mh_norm_g)[0][None, :], (128, D)))
    gbb = np.ascontiguousarray(np.broadcast_to(np.concatenate([f(b_igate)[0], f(b_fgate)[0]])[None, :], (128, 8)))
    con = _consts()
    shared = {"f1w1": f(ffn1_w1)[0], "f1w3": f(ffn1_w3)[0], "f1w2": f(ffn1_w2)[0], "win": f(w_in)[0], "wout": f(w_out)[0],
              "f2w1": f(ffn2_w1)[0], "f2w3": f(ffn2_w3)[0], "f2w2": f(ffn2_w2)[0], "pf": pfa, "consts": con, "mhg": mhg, "gb": gbb}
    xpr = f(x_prompt); xsa = f(x_sample); sCf = f(state_C)[0]; snf = f(state_n)[0]; smf_ = f(state_m)[0]; ccf = f(cache_conv)[0]
    in_maps = []
    for c in range(8):
        sl = slice(16 * c, 16 * (c + 1))
        m = dict(shared)
        half = c // 4
        m["xp"] = np.ascontiguousarray(xpr[c % 4, half * 1024:(half + 1) * 1024])
        m["xpre"] = np.ascontiguousarray(xpr[c % 4, 0:1024])
        m["msk"] = np.full((128, 1), float(half), np.float32)
        m["xs"] = xsa[sl].reshape(128, D)
        m["sC"] = sCf[sl]
        m["sn"] = snf[sl].reshape(128, 128)
        m["smt"] = np.ascontiguousarray(np.repeat(smf_[sl], 8, axis=0))
        m["cc"] = ccf[sl]
        in_maps.append(m)
    if _NC is None:
        _NC = build_program()
    res = run_bass_kernel_spmd(_NC, in_maps, core_ids=list(range(8)))
    R = res.results
    y_p = np.stack([np.concatenate([R[c]["yp"], R[c + 4]["yp"]]) for c in range(4)])
    y_s = np.concatenate([R[c]["ys"].reshape(16, 8, D) for c in range(8)])
    C_p = np.stack([R[c + 4]["Cp"] for c in range(4)])[None]
    n_p = np.stack([R[c + 4]["np"].reshape(NH, DQK) for c in range(4)])[None]
    m_p = np.stack([R[c + 4]["mp"].reshape(NH) for c in range(4)])[None]
    c_p = np.stack([R[c + 4]["cp"] for c in range(4)])[None]
    C_s = np.concatenate([R[c]["Cs"] for c in range(8)])[None]
    n_s = np.concatenate([R[c]["ns"].reshape(16, NH, DQK) for c in range(8)])[None]
    m_s = np.concatenate([R[c]["ms"] for c in range(8)])[None]
    c_s = np.concatenate([R[c]["cs"] for c in range(8)])[None]
    return tuple(np.ascontiguousarray(a, dtype=np.float32) for a in (y_p, y_s, C_p, n_p, m_p, c_p, C_s, n_s, m_s, c_s))
```
